# Optimizing a Trainium2 kernel written in Bass

```python
import math
import jax
import jax.numpy as jnp
from jax import lax
import numpy as np

D_MODEL = 1024
BATCH = 4
SEQ = 8192
DEPTH = 4

BRANCH_WIDTH = D_MODEL // 2
HEAD_DIM = 128
N_HEADS = BRANCH_WIDTH // HEAD_DIM
N_BRANCHES = 3
CONV_WIDTH = 4
CHUNK = 64
SB_BLOCK = 128
N_META = 16
FRONT = SB_BLOCK
PAD_FRONT = FRONT - N_META
NORM_EPS = 1e-6
SPLIT_SIZES = (BRANCH_WIDTH, BRANCH_WIDTH, BRANCH_WIDTH, BRANCH_WIDTH,
               3 * BRANCH_WIDTH, BRANCH_WIDTH, N_HEADS, N_HEADS,
               BRANCH_WIDTH, BRANCH_WIDTH, BRANCH_WIDTH, BRANCH_WIDTH,
               N_BRANCHES * D_MODEL)
N_IN = 12 * BRANCH_WIDTH + 2 * N_HEADS + N_BRANCHES * D_MODEL

kernel_name = 'meta_sb_gdn_hgrn2_gated_hybrid'


def rms_norm(x, w):
    xf = x.astype(jnp.float32)
    y = xf * lax.rsqrt(jnp.mean(xf * xf, axis=-1, keepdims=True) + NORM_EPS)
    return (y * w.astype(jnp.float32)).astype(x.dtype)


def l2_norm(x):
    return x * lax.rsqrt(jnp.sum(x * x, axis=-1, keepdims=True) + NORM_EPS)


def causal_conv(x, w):
    t_len = x.shape[1]
    xp = jnp.pad(x, ((0, 0), (CONV_WIDTH - 1, 0), (0, 0)))
    y = xp[:, 0:t_len] * w[0]
    for i in range(1, CONV_WIDTH):
        y = y + xp[:, i:i + t_len] * w[i]
    return y


def stick_breaking_attention(q, k, v, key_valid):
    t_len, d = q.shape[2], q.shape[3]
    scale = d ** -0.5
    outs = []
    for blk in range(t_len // SB_BLOCK):
        q0 = blk * SB_BLOCK
        q1 = q0 + SB_BLOCK
        z = jnp.einsum('bhqd,bhkd->bhqk', q[:, :, q0:q1], k[:, :, :q1]) * scale
        t_idx = jnp.arange(q0, q1)[:, None]
        s_idx = jnp.arange(q1)[None, :]
        mask = (s_idx < t_idx) & key_valid[None, :q1]
        log_keep = jnp.where(mask, jax.nn.log_sigmoid(-z), 0.0)
        log_passed = lax.cumsum(log_keep, axis=3, reverse=True) - log_keep
        attn = jnp.where(mask, jnp.exp(jax.nn.log_sigmoid(z) + log_passed), 0.0)
        outs.append(jnp.einsum('bhqk,bhkd->bhqd', attn, v[:, :, :q1]))
    return jnp.concatenate(outs, axis=2)


def to_chunks(a):
    bsz, t_len, h = a.shape[0], a.shape[1], a.shape[2]
    a = a.reshape((bsz, t_len // CHUNK, CHUNK, h) + a.shape[3:])
    return jnp.moveaxis(a, 3, 1)


def from_chunks(o):
    bsz, h, n, c, dv = o.shape
    return jnp.moveaxis(o, 1, 3).reshape(bsz, n * c, h, dv)


def gated_delta_rule_chunked(q, k, v, beta, g):
    bsz, _, h, dk = q.shape
    dv = v.shape[-1]
    q = to_chunks(q) * (dk ** -0.5)
    k = to_chunks(k)
    v = to_chunks(v)
    beta = to_chunks(beta)
    G = jnp.cumsum(to_chunks(g), axis=-1)
    causal = np.tril(np.ones((CHUNK, CHUNK), dtype=bool))
    strict = np.tril(np.ones((CHUNK, CHUNK), dtype=bool), k=-1)
    decay = jnp.exp(jnp.where(causal, G[..., :, None] - G[..., None, :], -jnp.inf))
    k_beta = k * beta[..., None]
    m = jnp.where(strict, jnp.einsum('bhncd,bhnsd->bhncs', k_beta, k) * decay, 0.0)
    t_mat = m + jnp.eye(CHUNK, dtype=jnp.float32)
    u = lax.linalg.triangular_solve(t_mat, v * beta[..., None], left_side=True, lower=True, unit_diagonal=True)
    w = lax.linalg.triangular_solve(t_mat, k_beta * jnp.exp(G)[..., None], left_side=True, lower=True, unit_diagonal=True)
    a_qk = jnp.einsum('bhncd,bhnsd->bhncs', q, k) * decay
    q_dec = q * jnp.exp(G)[..., None]
    k_dec = k * jnp.exp(G[..., -1:] - G)[..., None]
    g_last = jnp.exp(G[..., -1])
    xs = tuple(jnp.moveaxis(a, 2, 0) for a in (u, w, a_qk, q_dec, k_dec, g_last))

    def step(state, inp):
        u_c, w_c, aqk_c, qd_c, kd_c, gl_c = inp
        v_new = u_c - jnp.einsum('bhcd,bhde->bhce', w_c, state)
        o = jnp.einsum('bhcd,bhde->bhce', qd_c, state) + jnp.einsum('bhcs,bhse->bhce', aqk_c, v_new)
        state = state * gl_c[..., None, None] + jnp.einsum('bhcd,bhce->bhde', kd_c, v_new)
        return state, o

    s0 = jnp.zeros((bsz, h, dk, dv), jnp.float32)
    _, o = lax.scan(step, s0, xs)
    return from_chunks(jnp.moveaxis(o, 0, 2))


def hgrn2_chunked(q, k, v, g):
    bsz, _, h, dk = q.shape
    dv = v.shape[-1]
    G = jnp.cumsum(to_chunks(g), axis=3)
    xs = tuple(jnp.moveaxis(a, 2, 0) for a in (to_chunks(q), to_chunks(k), to_chunks(v), G))
    causal = np.tril(np.ones((CHUNK, CHUNK), dtype=bool))[:, :, None]

    def step(state, inp):
        q_c, k_c, v_c, g_c = inp
        g_end = g_c[:, :, -1:, :]
        o_inter = jnp.einsum('bhcd,bhde->bhce', q_c * jnp.exp(g_c), state)
        diff = g_c[:, :, :, None, :] - g_c[:, :, None, :, :]
        dec = jnp.exp(jnp.where(causal, diff, -jnp.inf))
        a = jnp.einsum('bhid,bhjd,bhijd->bhij', q_c, k_c, dec)
        o = o_inter + jnp.einsum('bhij,bhje->bhie', a, v_c)
        state = state * jnp.exp(g_end)[:, :, 0, :, None] + jnp.einsum('bhcd,bhce->bhde', k_c * jnp.exp(g_end - g_c), v_c)
        return state, o

    s0 = jnp.zeros((bsz, h, dk, dv), jnp.float32)
    _, o = lax.scan(step, s0, xs)
    return from_chunks(jnp.moveaxis(o, 0, 2))


def hybrid_layer(h, valid, norm_w, w_in, sb_qn, sb_kn, conv_w, a_log, dt_bias, gdn_on, lb, hg_on, w_branch, w_out):
    bsz, t_len, _ = h.shape
    f32 = jnp.float32
    xn = rms_norm(h, norm_w)
    proj = jnp.einsum('btd,dn->btn', xn, w_in)
    split_points = np.cumsum(SPLIT_SIZES)[:-1].tolist()
    (sb_q, sb_k, sb_v, sb_z, gd_qkv, gd_z, gd_b, gd_a,
     hg_q, hg_f, hg_i, hg_z, mix) = jnp.split(proj, split_points, axis=-1)
    vmask = valid[None, :, None].astype(f32)

    def heads(a):
        return a.reshape(bsz, t_len, N_HEADS, HEAD_DIM)

    q = jnp.transpose(rms_norm(heads(sb_q), sb_qn).astype(f32), (0, 2, 1, 3))
    k = jnp.transpose(rms_norm(heads(sb_k), sb_kn).astype(f32), (0, 2, 1, 3))
    v = jnp.transpose(heads(sb_v).astype(f32), (0, 2, 1, 3))
    o_sb = stick_breaking_attention(q, k, v, valid)
    o_sb = jnp.transpose(o_sb, (0, 2, 1, 3)).reshape(bsz, t_len, BRANCH_WIDTH) * jax.nn.silu(sb_z.astype(f32))

    qkv = jax.nn.silu(causal_conv(gd_qkv.astype(f32), conv_w.astype(f32)))
    gq, gk, gv = jnp.split(qkv, 3, axis=-1)
    beta = jax.nn.sigmoid(gd_b.astype(f32)) * vmask
    g = -jnp.exp(a_log.astype(f32)) * jax.nn.softplus(gd_a.astype(f32) + dt_bias.astype(f32))
    o_gd = gated_delta_rule_chunked(l2_norm(heads(gq)), l2_norm(heads(gk)), heads(gv), beta, g)
    o_gd = rms_norm(o_gd, gdn_on).reshape(bsz, t_len, BRANCH_WIDTH) * jax.nn.silu(gd_z.astype(f32))

    lbf = lb.astype(f32)
    f_pre = hg_f.astype(f32)
    forget = lbf + (1.0 - lbf) * jax.nn.sigmoid(f_pre)
    hk = (1.0 - lbf) * jax.nn.sigmoid(-f_pre)
    o_hg = hgrn2_chunked(heads(jax.nn.silu(hg_q.astype(f32))), heads(hk),
                         heads(hg_i.astype(f32) * vmask), heads(jnp.log(forget)))
    o_hg = rms_norm(o_hg, hg_on).reshape(bsz, t_len, BRANCH_WIDTH) * jax.nn.silu(hg_z.astype(f32))

    gates = jax.nn.sigmoid(mix.astype(f32)).reshape(bsz, t_len, N_BRANCHES, D_MODEL)
    y = (gates[:, :, 0] * jnp.einsum('btw,wd->btd', o_sb, w_branch[0])
         + gates[:, :, 1] * jnp.einsum('btw,wd->btd', o_gd, w_branch[1])
         + gates[:, :, 2] * jnp.einsum('btw,wd->btd', o_hg, w_branch[2]))
    out = jnp.einsum('btd,de->bte', y.astype(h.dtype), w_out)
    return h + out.astype(h.dtype)


def setup_inputs(seed: int = 0) -> dict:
    key = jax.random.key(seed)
    ks = jax.random.split(key, 16)
    f32 = jnp.float32
    x = jax.random.normal(ks[0], (BATCH, SEQ, D_MODEL), f32)
    meta_tokens = jax.random.normal(ks[1], (N_META, D_MODEL), f32)
    norm_w = 1.0 + 0.02 * jax.random.normal(ks[2], (DEPTH, D_MODEL), f32)
    w_in = jax.random.normal(ks[3], (DEPTH, D_MODEL, N_IN), f32) * D_MODEL ** -0.5
    sb_q_norm = 1.0 + 0.02 * jax.random.normal(ks[4], (DEPTH, HEAD_DIM), f32)
    sb_k_norm = 1.0 + 0.02 * jax.random.normal(ks[5], (DEPTH, HEAD_DIM), f32)
    gdn_conv_w = jax.random.normal(ks[6], (DEPTH, CONV_WIDTH, 3 * BRANCH_WIDTH), f32) * CONV_WIDTH ** -0.5
    gdn_a_log = jnp.log(jax.random.uniform(ks[7], (DEPTH, N_HEADS), f32, 1.0, 16.0))
    dt = jnp.exp(jax.random.uniform(ks[8], (DEPTH, N_HEADS), f32) * (math.log(0.1) - math.log(0.001)) + math.log(0.001))
    gdn_dt_bias = dt + jnp.log(-jnp.expm1(-dt))
    gdn_out_norm = 1.0 + 0.02 * jax.random.normal(ks[9], (DEPTH, HEAD_DIM), f32)
    hgrn_lb_logits = jax.random.normal(ks[10], (DEPTH, BRANCH_WIDTH), f32)
    hgrn_out_norm = 1.0 + 0.02 * jax.random.normal(ks[11], (DEPTH, HEAD_DIM), f32)
    w_branch = jax.random.normal(ks[12], (DEPTH, N_BRANCHES, BRANCH_WIDTH, D_MODEL), f32) * BRANCH_WIDTH ** -0.5
    w_out = jax.random.normal(ks[13], (DEPTH, D_MODEL, D_MODEL), f32) * D_MODEL ** -0.5
    return {'x': x, 'meta_tokens': meta_tokens, 'norm_w': norm_w, 'w_in': w_in,
            'sb_q_norm': sb_q_norm, 'sb_k_norm': sb_k_norm, 'gdn_conv_w': gdn_conv_w,
            'gdn_a_log': gdn_a_log, 'gdn_dt_bias': gdn_dt_bias, 'gdn_out_norm': gdn_out_norm,
            'hgrn_lb_logits': hgrn_lb_logits, 'hgrn_out_norm': hgrn_out_norm,
            'w_branch': w_branch, 'w_out': w_out}


def reference(x, meta_tokens, norm_w, w_in, sb_q_norm, sb_k_norm, gdn_conv_w, gdn_a_log,
              gdn_dt_bias, gdn_out_norm, hgrn_lb_logits, hgrn_out_norm, w_branch, w_out):
    bsz = x.shape[0]
    h = jnp.concatenate([
        jnp.zeros((bsz, PAD_FRONT, D_MODEL), x.dtype),
        jnp.broadcast_to(meta_tokens.astype(x.dtype)[None], (bsz, N_META, D_MODEL)),
        x], axis=1)
    t_len = h.shape[1]
    valid = jnp.arange(t_len) >= PAD_FRONT
    p = jax.nn.softmax(hgrn_lb_logits.astype(jnp.float32), axis=0)
    lower_bounds = jnp.cumsum(p, axis=0) - p[0:1]
    for layer in range(DEPTH):
        h = hybrid_layer(h, valid, norm_w[layer], w_in[layer], sb_q_norm[layer], sb_k_norm[layer],
                         gdn_conv_w[layer], gdn_a_log[layer], gdn_dt_bias[layer], gdn_out_norm[layer],
                         lower_bounds[layer], hgrn_out_norm[layer], w_branch[layer], w_out[layer])
    return h[:, FRONT:]
```

```python
from contextlib import ExitStack
import numpy as np
import ml_dtypes
import concourse.bass as bass
import concourse.mybir as mybir
from concourse.bass_utils import run_bass_kernel_spmd

F32 = mybir.dt.float32
BF16 = mybir.dt.bfloat16
AF = mybir.ActivationFunctionType
ALU = mybir.AluOpType
AX = mybir.AxisListType

D = 1024
NIN = 9224
NEG = -30000.0
EPS = 1e-6

WT_SPECS = [
    (0, 512), (512, 512), (1024, 512), (1536, 512),
    (2048, 512), (2560, 512), (3072, 512), (3584, 512),
    (4096, 8),
    (4104, 512), (4616, 512), (5128, 512), (5640, 512),
    (6152, 512), (6664, 512), (7176, 512), (7688, 512), (8200, 512), (8712, 512),
]
T_WB = 19
T_WO = 22
NTILES = 24


class _RecCall:
    def __init__(self, name, a, k):
        self.name, self.a, self.k = name, a, k

    def __call__(self, e):
        return getattr(e, self.name)(*self.a, **self.k)


class _Rec:
    def __getattr__(self, name):
        return lambda *a, **k: _RecCall(name, a, k)


_REC = _Rec()


class Prog:
    ENGS = ("pe", "act", "dve", "pool", "sp")

    def __init__(self, nc, es, n_dma_sems=20):
        self.nc = nc
        self.streams = {e: [] for e in self.ENGS}
        self.cnt = {e: 0 for e in self.ENGS}
        self.sem = {e: es.enter_context(nc.semaphore("s_" + e)) for e in ("pe", "act", "dve", "pool")}
        self.dsem = {q: [es.enter_context(nc.semaphore(f"d_{q}{i}")) for i in range(n_dma_sems)]
                     for q in ("sp", "pool")}
        self.dval = {q: [0] * n_dma_sems for q in ("sp", "pool")}
        self.dnext = {q: 0 for q in ("sp", "pool")}
        self.semobj = {}
        for e, s in self.sem.items():
            self.semobj[("c", e)] = s
        for q in self.dsem:
            for i, s in enumerate(self.dsem[q]):
                self.semobj[("d", q, i)] = s
        self.seen = {e: {} for e in self.ENGS}
        self.lastw = {}
        self.readers = {}

    def _deps(self, eng, reads, writes, extra=()):
        need = {}

        def add(tok):
            if tok is None:
                return
            k, v = tok
            if need.get(k, 0) < v:
                need[k] = v
        for r in reads:
            add(self.lastw.get(r))
        for w in writes:
            add(self.lastw.get(w))
            for t in self.readers.get(w, ()):
                add(t)
        for t in extra:
            add(t)
        waits = []
        sn = self.seen[eng]
        for k, v in need.items():
            if sn.get(k, 0) >= v:
                continue
            sn[k] = v
            waits.append((k, v))
        return waits

    def _commit(self, tok, reads, writes):
        for w in writes:
            self.lastw[w] = tok
            self.readers[w] = []
        for r in reads:
            if r in writes:
                continue
            self.readers.setdefault(r, []).append(tok)

    def op(self, eng, fn, reads=(), writes=()):
        reads = [r for r in reads if r is not None]
        writes = [w for w in writes if w is not None]
        ex = [r for r in reads if isinstance(r, str) and r.startswith("ps")]
        if ex:
            writes = writes + [r for r in ex if r not in writes]
            reads = [r for r in reads if r not in ex]
        waits = self._deps(eng, reads, writes)
        self.cnt[eng] += 1
        tok = (("c", eng), self.cnt[eng])
        self.streams[eng].append((waits, fn(_REC), ("c", eng), 1))
        self._commit(tok, reads, writes)

    def dma(self, q, fn, reads=(), writes=()):
        i = self.dnext[q]
        self.dnext[q] = (i + 1) % len(self.dsem[q])
        k = ("d", q, i)
        prev = (k, self.dval[q][i]) if self.dval[q][i] > 0 else None
        waits = self._deps(q, list(reads), list(writes), extra=(prev,))
        self.dval[q][i] += 16
        tok = (k, self.dval[q][i])
        self.streams[q].append((waits, fn(_REC), k, 16))
        self._commit(tok, reads, writes)

    def barrier(self):
        toks = [(("c", e), self.cnt[e]) for e in ("pe", "act", "dve", "pool") if self.cnt[e] > 0]
        for q in self.dsem:
            for i, v in enumerate(self.dval[q]):
                if v > 0:
                    toks.append((("d", q, i), v))
        for e in self.ENGS:
            waits = []
            for k, v in toks:
                if self.seen[e].get(k, 0) < v:
                    self.seen[e][k] = v
                    waits.append((k, v))
            if waits:
                self.streams[e].append((waits, None, None, 0))
        self.lastw.clear()
        self.readers.clear()

    def emit(self):
        nc = self.nc
        P = self

        def replay(name, e):
            for waits, fn, k, inc in P.streams[name]:
                for (wk, wv) in waits:
                    e.wait_ge(P.semobj[wk], wv)
                if fn is not None:
                    fn(e).then_inc(P.semobj[k], inc)

        with nc.Block() as block:
            @block.tensor
            def _(e):
                replay("pe", e)

            @block.scalar
            def _(e):
                replay("act", e)

            @block.vector
            def _(e):
                replay("dve", e)

            @block.gpsimd
            def _(e):
                replay("pool", e)

            @block.sync
            def _(e):
                replay("sp", e)


def _const_packs():
    i = np.arange(128)
    c = {}
    c["ident"] = np.eye(128, dtype=np.float32)
    c["ones"] = np.ones((128, 128), np.float32)
    c["uincl"] = (i[:, None] <= i[None, :]).astype(np.float32)
    c["strict"] = (i[None, :] < i[:, None]).astype(np.float32)
    c["causal"] = (i[None, :] <= i[:, None]).astype(np.float32)
    ch = i // 64
    same = (ch[:, None] == ch[None, :])
    mid = ch * 64 + 31
    c["umid"] = (same * ((i[:, None] <= i[None, :]).astype(np.float32)
                         - (i[:, None] <= mid[None, :]).astype(np.float32))).astype(np.float32)
    c["ucs"] = (same & (i[:, None] <= i[None, :])).astype(np.float32)
    c["uend"] = (same & (i[:, None] > i[None, :])).astype(np.float32)
    c["maskt"] = (same & (i[:, None] <= i[None, :])).astype(np.float32)
    valid = (i >= 112).astype(np.float32)
    c["validcol"] = np.repeat(valid[:, None], 4, axis=1)
    c["biasvalid"] = np.repeat(((1.0 - valid) * NEG)[:, None], 4, axis=1).astype(np.float32)
    c["ch0col"] = np.repeat((i < 64).astype(np.float32)[:, None], 4, axis=1)
    c["ch1col"] = np.repeat((i >= 64).astype(np.float32)[:, None], 4, axis=1)
    t512 = np.arange(512)
    c["chm"] = np.broadcast_to((t512 % 64 != 0).astype(np.float32)[None, :], (128, 512)).copy()
    names32 = ["ident", "ones", "uincl", "strict", "causal", "maskt",
               "validcol", "biasvalid", "ch0col", "ch1col", "chm"]
    off32 = {}
    cols = 0
    for n in names32:
        off32[n] = (cols, c[n].shape[1])
        cols += c[n].shape[1]
    p32 = np.concatenate([c[n] for n in names32], axis=1).astype(np.float32)
    b = {}
    b["ident"] = c["ident"]
    b["ones"] = c["ones"]
    b["ones0"] = c["ones"] * valid[:, None]
    negl = -(i[:, None] >= i[None, :]).astype(np.float32)
    b["negl"] = negl
    b["negl0"] = negl * valid[:, None]
    t = np.arange(512)
    for k in range(4):
        m01 = ((128 * k + i[:, None]) < t[None, :]).astype(np.float32)
        b[f"m01_{k}"] = m01
        b[f"mneg_{k}"] = (1.0 - m01) * NEG
    namesb = ["ident", "ones", "ones0", "negl", "negl0"] + [f"m01_{k}" for k in range(4)] + \
             [f"mneg_{k}" for k in range(4)]
    offb = {}
    cols = 0
    for n in namesb:
        offb[n] = (cols, b[n].shape[1])
        cols += b[n].shape[1]
    pb = np.concatenate([b[n] for n in namesb], axis=1).astype(ml_dtypes.bfloat16)
    return p32, off32, pb, offb


def _param_pack(depth, norm_w, sb_q_norm, sb_k_norm, gdn_conv_w, gdn_a_log, gdn_dt_bias,
                gdn_out_norm, hgrn_lb_logits, hgrn_out_norm):
    segs = {}
    segs["normw"] = norm_w.reshape(depth, 8, 128).transpose(2, 0, 1).reshape(128, depth * 8)
    segs["sbqn"] = sb_q_norm.T
    segs["sbkn"] = sb_k_norm.T
    segs["gdon"] = gdn_out_norm.T
    segs["hgon"] = hgrn_out_norm.T
    segs["convw"] = gdn_conv_w.reshape(depth, 4, 12, 128).transpose(3, 0, 2, 1).reshape(128, depth * 48)
    segs["lbl"] = hgrn_lb_logits.reshape(depth, 4, 128).transpose(2, 1, 0).reshape(128, 4 * depth)
    segs["alog"] = np.broadcast_to(gdn_a_log.reshape(1, depth * 4), (128, depth * 4))
    segs["dtb"] = np.broadcast_to(gdn_dt_bias.reshape(1, depth * 4), (128, depth * 4))
    off = {}
    cols = 0
    parts = []
    for n, a in segs.items():
        a = np.ascontiguousarray(a, dtype=np.float32)
        off[n] = (cols, a.shape[1])
        cols += a.shape[1]
        parts.append(a)
    return np.concatenate(parts, axis=1), off


class _Stop(Exception):
    pass


def build(nblk, depth, debug=False, stop=None):
    T = nblk * 128
    TR = T - 128
    groups = [(0, 1)]
    b0 = 1
    while b0 < nblk:
        nb = min(4, nblk - b0)
        groups.append((b0, nb))
        b0 += nb
    p32, off32, pb, offb = _const_packs()
    npar = depth * 8 + 4 * depth + depth * 48 + 4 * depth + 8 * depth
    nc = bass.Bass("TRN2", target_bir_lowering=False)
    es = ExitStack()
    dr = {}
    dr["x"] = nc.dram_tensor("x", [TR, D], F32, kind="ExternalInput")
    dr["meta"] = nc.dram_tensor("meta", [16, D], F32, kind="ExternalInput")
    dr["w_in"] = nc.dram_tensor("w_in", [depth, D, NIN], F32, kind="ExternalInput")
    dr["w_branch"] = nc.dram_tensor("w_branch", [depth, 3, 512, D], F32, kind="ExternalInput")
    dr["w_out"] = nc.dram_tensor("w_out", [depth, D, D], F32, kind="ExternalInput")
    dr["c32"] = nc.dram_tensor("c32", list(p32.shape), F32, kind="ExternalInput")
    dr["cbf"] = nc.dram_tensor("cbf", list(pb.shape), BF16, kind="ExternalInput")
    dr["par"] = nc.dram_tensor("par", [128, npar], F32, kind="ExternalInput")
    dr["nwrow"] = nc.dram_tensor("nwrow", [depth, 128, D], F32, kind="ExternalInput")
    dr["y"] = nc.dram_tensor("y", [TR, D], F32, kind="ExternalOutput")
    hbuf = nc.dram_tensor("hbuf", [T, D], F32, kind="Internal")
    kts = nc.dram_tensor("kts", [4, 128, T], BF16, kind="Internal")
    vs = nc.dram_tensor("vs", [4, 128, nblk, 128], BF16, kind="Internal")
    wsc = nc.dram_tensor("wsc", [depth, NTILES, 128, 4096], BF16, kind="Internal")
    dbg = {}

    P = Prog(nc, es)
    sb_count = [0]

    def SB(shape, dt, name=None):
        sb_count[0] += 1
        return nc.alloc_sbuf_tensor(name or f"t{sb_count[0]}", shape, dt)

    C32 = SB(list(p32.shape), F32, "C32")
    CBF = SB(list(pb.shape), BF16, "CBF")
    PAR = SB([128, npar], F32, "PAR")
    P.dma("sp", lambda e: e.dma_start(out=C32[:, :], in_=dr["c32"].ap()), writes=["C32"])
    P.dma("sp", lambda e: e.dma_start(out=CBF[:, :], in_=dr["cbf"].ap()), writes=["CBF"])
    P.dma("sp", lambda e: e.dma_start(out=PAR[:, :], in_=dr["par"].ap()), writes=["PAR"])

    def c32(n, lo=0, hi=None):
        o, w = off32[n]
        hi = w if hi is None else hi
        return C32[:, o + lo:o + hi]

    def cbf(n, lo=0, hi=None):
        o, w = offb[n]
        hi = w if hi is None else hi
        return CBF[:, o + lo:o + hi]

    poff = {}
    cols = 0
    for n, w in (("normw", depth * 8), ("sbqn", depth), ("sbkn", depth), ("gdon", depth), ("hgon", depth),
                 ("convw", depth * 48), ("lbl", 4 * depth), ("alog", 4 * depth), ("dtb", 4 * depth)):
        poff[n] = cols
        cols += w
    assert cols == npar

    def par(n, idx, w=1):
        return PAR[:, poff[n] + idx:poff[n] + idx + w]

    PS = [nc.alloc_psum_tensor(f"ps{i}", [128, 512], F32) for i in range(6)]
    ps_rr = [0]

    def ps():
        i = ps_rr[0]
        ps_rr[0] = (i + 1) % 4
        return PS[i], f"ps{i}"

    with nc.sbuf_tensor("stg0", [128, 4096], F32) as stg0, nc.sbuf_tensor("stg1", [128, 4096], F32) as stg1, \
            nc.sbuf_tensor("cv0", [128, 4096], BF16) as cv0, nc.sbuf_tensor("cv1", [128, 4096], BF16) as cv1:
        stg = [stg0, stg1]
        cv = [cv0, cv1]
        n = 0
        for l in range(depth):
            for ti in range(NTILES):
                s = stg[n % 2]
                cvt = cv[n % 2]
                sk, ck = f"stg{n % 2}", f"cv{n % 2}"
                if ti < 19:
                    c0, w = WT_SPECS[ti]
                    src = dr["w_in"].ap()[l].rearrange("(kc p) n -> p kc n", p=128)[:, :, c0:c0 + w]
                    dst = s[:, :].rearrange("p (kc n) -> p kc n", kc=8)[:, :, 0:w]
                    wid = 4096 if w == 512 else None
                elif ti < 22:
                    src = dr["w_branch"].ap()[l, ti - 19].rearrange("(h p) c -> p h c", p=128)
                    dst = s[:, :].rearrange("p (h c) -> p h c", h=4)
                    wid = 4096
                else:
                    e0 = (ti - 22) * 512
                    src = dr["w_out"].ap()[l].rearrange("(c p) e -> p c e", p=128)[:, :, e0:e0 + 512]
                    dst = s[:, :].rearrange("p (c e) -> p c e", c=8)
                    wid = 4096
                P.dma("sp", (lambda e, dst=dst, src=src: e.dma_start(out=dst, in_=src)), writes=[sk])
                if ti == 8:
                    sv = s[:, :].rearrange("p (kc n) -> p kc n", kc=8)[:, :, 0:8]
                    dv = cvt[:, :].rearrange("p (kc n) -> p kc n", kc=8)[:, :, 0:8]
                    P.op("dve", (lambda e, dv=dv, sv=sv: e.tensor_copy(out=dv, in_=sv)), reads=[sk], writes=[ck])
                elif n % 2 == 0:
                    P.op("dve", (lambda e, cvt=cvt, s=s: e.tensor_copy(out=cvt[:, :], in_=s[:, :])),
                         reads=[sk], writes=[ck])
                else:
                    P.op("act", (lambda e, cvt=cvt, s=s: e.activation(out=cvt[:, :], in_=s[:, :], func=AF.Copy)),
                         reads=[sk], writes=[ck])
                P.dma("sp", (lambda e, cvt=cvt, l=l, ti=ti: e.dma_start(out=wsc.ap()[l, ti], in_=cvt[:, :])),
                      reads=[ck], writes=[("wsc", l, ti)])
                n += 1
        P.barrier()

    WR = [SB([128, 4096], BF16, f"wr{i}") for i in range(3)]
    wr_rr = [0]

    def load_w(l, ti):
        i = wr_rr[0]
        wr_rr[0] = (i + 1) % len(WR)
        t = WR[i]
        P.dma("sp", (lambda e: e.dma_start(out=t[:, :], in_=wsc.ap()[l, ti])), reads=[("wsc", l, ti)],
              writes=[f"wr{i}"])
        return t, f"wr{i}"

    HB = [SB([128, D], F32, f"hb{i}") for i in range(2)]
    XS = SB([128, D], BF16, "xs")
    NWROW = SB([128, D], F32, "nwrow_sb")
    JUNK = SB([128, D], BF16, "junk")
    SM = SB([128, 64], F32, "sm")
    XNT = SB([128, 8, 512], BF16, "xnt")
    A32 = [SB([128, 512], F32, f"a32_{i}") for i in range(8)]
    A16 = [SB([128, 512], BF16, f"a16_{i}") for i in range(8)]
    B32 = [SB([128, 128], F32, f"b32_{i}") for i in range(20)]
    C32R = [SB([128, 128], F32, f"c32r_{i}") for i in range(14)]
    B16 = [SB([128, 128], BF16, f"b16_{i}") for i in range(16)]
    C16 = [SB([128, 128], BF16, f"c16_{i}") for i in range(10)]
    rr = {"a32": 0, "a16": 0, "b32": 0, "b16": 0, "c16": 0, "c32r": 0}

    def ring(kind):
        lst = {"a32": A32, "a16": A16, "b32": B32, "b16": B16, "c16": C16, "c32r": C32R}[kind]
        i = rr[kind]
        rr[kind] = (i + 1) % len(lst)
        return lst[i], f"{kind}_{i}"

    GA = [SB([128, 512], BF16, f"ga{h}") for h in range(4)]
    GBt = [SB([128, 512], BF16, f"gb{h}") for h in range(4)]
    GC = [SB([128, 512], BF16, f"gc{h}") for h in range(4)]
    TOK = SB([128, 4, 512], BF16, "tok")
    SZ4 = [SB([128, 512], BF16, f"sz{h}") for h in range(4)]
    OB = [SB([128, 512], BF16, f"ob{i}") for i in range(12)]
    KSEG = [SB([128, 1024], BF16, f"kseg{i}") for i in range(2)]
    VSEG = [SB([128, 8, 128], BF16, f"vseg{i}") for i in range(2)]
    RSB = SB([128, 512], F32, "rsb")
    CONVB = [SB([128, 515], F32, f"convb{i}") for i in range(2)]
    TAIL = SB([128, 12, 4], F32, "tail")
    BD = SB([128, 4, 8], F32, "bd")
    SG32 = [SB([128, 128], F32, f"sg32_{h}") for h in range(4)]
    SG16 = [SB([128, 128], BF16, f"sg16_{h}") for h in range(4)]
    SH32 = [SB([128, 128], F32, f"sh32_{h}") for h in range(4)]
    SH16 = [SB([128, 128], BF16, f"sh16_{h}") for h in range(4)]
    HK = [SB([128, 512], F32, f"hk{h}") for h in range(4)]
    HG = [SB([128, 512], F32, f"hgl{h}") for h in range(4)]
    GST = SB([128, 32], F32, "gst")
    QD = [[SB([128, 128], BF16, f"qd{i}_{c}") for c in range(2)] for i in range(2)]
    ZERO16 = SB([128, 128], BF16, "zero16")
    YT = [SB([128, 512], BF16, f"yt{c}") for c in range(8)]
    LB = SB([128, 4, 2 * depth], F32, "lb")
    NEGA = SB([128, 4 * depth], F32, "nega")
    PST = [nc.alloc_psum_tensor(f"pst{i}", [128, 1024], BF16) for i in range(2)]
    pst_rr = [0]

    def pst():
        i = pst_rr[0]
        pst_rr[0] = (i + 1) % 8
        return PST[i % 2][:, (i // 2) * 128:(i // 2 + 1) * 128], f"pst{i % 2}"

    def dve(fn, reads, writes):
        P.op("dve", fn, reads, writes)

    def act(fn, reads, writes):
        P.op("act", fn, reads, writes)

    def pool(fn, reads, writes):
        P.op("pool", fn, reads, writes)

    def pe(fn, reads, writes):
        P.op("pe", fn, reads, writes)

    def mm(out, lhsT, rhs, start, stop, reads, writes):
        pe(lambda e: e.matmul(out, lhsT, rhs, start=start, stop=stop), reads, writes)

    def rstd_from(out_ap, in_ap, scale, n_free, reads, writes, tmpk):
        act(lambda e: e.activation(out=out_ap, in_=in_ap, func=AF.Ln, scale=scale, bias=EPSB[:, 0:1]),
            reads + ["epsb"], writes)
        act(lambda e: e.activation(out=out_ap, in_=out_ap, func=AF.Exp, scale=-0.5), writes, writes)

    EPSB = SB([128, 2], F32, "epsb")
    dve(lambda e: e.memset(EPSB[:, 0:1], EPS), [], ["epsb"])
    dve(lambda e: e.memset(EPSB[:, 1:2], 1.0), [], ["epsb"])
    dve(lambda e: e.memset(ZERO16[:, :], 0.0), [], ["zero16"])
    for i_ in range(2):
        for c_ in range(2):
            dve(lambda e, i_=i_, c_=c_: e.memset(QD[i_][c_][:, :], 0.0), [], [f"qd{i_}_{c_}"])
    dve(lambda e: e.memset(TAIL[:, :, :], 0.0), [], [("tail", c_) for c_ in range(12)])

    for ch in range(4):
        lg = par("lbl", ch * depth, depth)
        mx = SM[:, 0:1]
        dve(lambda e, lg=lg, mx=mx: e.tensor_reduce(out=mx, in_=lg, axis=AX.X, op=ALU.max), ["PAR"], ["sm"])
        dve(lambda e, mx=mx: e.tensor_scalar(out=SM[:, 1:2], in0=mx, scalar1=-1.0, scalar2=None, op0=ALU.mult),
            ["sm"], ["sm"])
        ex = SM[:, 8:8 + depth]
        act(lambda e, lg=lg, ex=ex: e.activation(out=ex, in_=lg, func=AF.Exp, bias=SM[:, 1:2], scale=1.0),
            ["PAR", "sm"], ["sm"])
        dve(lambda e, ex=ex: e.tensor_reduce(out=SM[:, 2:3], in_=ex, axis=AX.X, op=ALU.add), ["sm"], ["sm"])
        dve(lambda e: e.reciprocal(out=SM[:, 3:4], in_=SM[:, 2:3]), ["sm"], ["sm"])
        pr = SM[:, 24:24 + depth]
        dve(lambda e, ex=ex, pr=pr: e.tensor_scalar(out=pr, in0=ex, scalar1=SM[:, 3:4], scalar2=None, op0=ALU.mult),
            ["sm"], ["sm"])
        dve(lambda e, ch=ch: e.memset(LB[:, ch, 0:1], 0.0), [], ["lb"])
        for l in range(1, depth):
            dve(lambda e, ch=ch, l=l: e.tensor_tensor(out=LB[:, ch, l:l + 1], in0=LB[:, ch, l - 1:l],
                                                      in1=SM[:, 24 + l:25 + l], op=ALU.add), ["lb", "sm"], ["lb"])
        dve(lambda e, ch=ch: e.tensor_scalar(out=LB[:, ch, depth:2 * depth], in0=LB[:, ch, 0:depth],
                                             scalar1=-1.0, scalar2=1.0, op0=ALU.mult, op1=ALU.add), ["lb"], ["lb"])
    act(lambda e: e.activation(out=NEGA[:, :], in_=par("alog", 0, 4 * depth), func=AF.Exp), ["PAR"], ["nega"])
    dve(lambda e: e.tensor_scalar(out=NEGA[:, :], in0=NEGA[:, :], scalar1=-1.0, scalar2=None, op0=ALU.mult),
        ["nega"], ["nega"])

    try:
        if stop == "pre":
            raise _Stop()
        for l in range(depth):
            first = (l == 0)
            last = (l == depth - 1)
            for h in range(4):
                dve(lambda e, h=h: e.memset(SG32[h][:, :], 0.0), [], [f"sg32_{h}"])
                dve(lambda e, h=h: e.memset(SG16[h][:, :], 0.0), [], [f"sg16_{h}"])
                dve(lambda e, h=h: e.memset(SH32[h][:, :], 0.0), [], [f"sh32_{h}"])
                dve(lambda e, h=h: e.memset(SH16[h][:, :], 0.0), [], [f"sh16_{h}"])
            dve(lambda e: e.memset(TAIL[:, :, :], 0.0), [], [("tail", c_) for c_ in range(12)])
            P.dma("sp", lambda e: e.dma_start(out=NWROW[:, :], in_=dr["nwrow"].ap()[l]), writes=["nwrow"])

            for gi, (gb0, nb) in enumerate(groups):
                N = nb * 128
                t0 = gb0 * 128

                def h_src(blk):
                    tg = (gb0 + blk) * 128
                    if first:
                        return None if gb0 + blk == 0 else dr["x"].ap()[tg - 128:tg, :]
                    return hbuf.ap()[tg:tg + 128, :]

                def load_hblk(blk, hb, hbk):
                    tg = (gb0 + blk) * 128
                    if first and gb0 + blk == 0:
                        dve(lambda e: e.memset(hb[:, :], 0.0), [], [hbk])
                        P.dma("sp", lambda e: e.dma_start(out=hb[112:128, :], in_=dr["meta"].ap()), writes=[hbk])
                    else:
                        src = h_src(blk)
                        rk = [("h", gb0 + blk)] if not first else []
                        P.dma("sp", lambda e: e.dma_start(out=hb[:, :], in_=src), reads=rk, writes=[hbk])

                for blk in range(nb):
                    hb = HB[blk % 2]
                    hbk = f"hb{blk % 2}"
                    load_hblk(blk, hb, hbk)
                    act(lambda e, hb=hb: e.activation(out=JUNK[:, :], in_=hb[:, :], func=AF.Square,
                                                       accum_out=SM[:, 40:41]), [hbk], ["junk", "sm40"])
                    rstd_from(SM[:, 41:42], SM[:, 40:41], 1.0 / D, 1, ["sm40"], ["sm41"], None)
                    dve(lambda e, hb=hb: e.scalar_tensor_tensor(out=XS[:, :], in0=hb[:, :], scalar=SM[:, 41:42],
                                                                 in1=NWROW[:, :], op0=ALU.mult, op1=ALU.mult),
                        [hbk, "sm41", "nwrow"], ["xs"])
                    for half in range(2):
                        pbank = PST[half]
                        pk = f"pst{half}"
                        for q in range(4):
                            kc = half * 4 + q
                            pe(lambda e, pbank=pbank, q=q, kc=kc: e.transpose(
                                pbank[:, q * 128:(q + 1) * 128], XS[:, kc * 128:(kc + 1) * 128], cbf("ident")),
                               ["xs", "CBF"], [pk])
                        o = XNT[:, half * 4:half * 4 + 4, blk * 128:(blk + 1) * 128]
                        src = pbank[:, 0:512].rearrange("p (q t) -> p q t", q=4)
                        if half == 0:
                            act(lambda e, o=o, src=src: e.activation(out=o, in_=src, func=AF.Copy), [pk], [("xnt", blk)])
                        else:
                            dve(lambda e, o=o, src=src: e.tensor_copy(out=o, in_=src), [pk], [("xnt", blk)])
                xk = [("xnt", b) for b in range(nb)]

                def proj_fm(wt, wk, f):
                    bank, bk = ps()
                    for kc in range(8):
                        mm(bank[:, :N], wt[:, kc * 512 + f * 128:kc * 512 + (f + 1) * 128], XNT[:, kc, :N],
                           kc == 0, kc == 7, [wk] + xk, [bk])
                    return bank, bk

                def proj_tm(wt, wk, blk, w):
                    bank, bk = ps()
                    for kc in range(8):
                        mm(bank[:, :w], XNT[:, kc, blk * 128:(blk + 1) * 128], wt[:, kc * 512:kc * 512 + w],
                           kc == 0, kc == 7, [wk, ("xnt", blk)], [bk])
                    return bank, bk

                def silu_gate(wt, wk, f, dst, dk):
                    bank, bk = proj_fm(wt, wk, f)
                    tmp, tk = ring("a32")
                    act(lambda e: e.activation(out=tmp[:, :N], in_=bank[:, :N], func=AF.Sigmoid), [bk], [tk])
                    dve(lambda e: e.tensor_tensor(out=dst[:, :N], in0=bank[:, :N], in1=tmp[:, :N], op=ALU.mult),
                        [bk, tk], [dk])

                def headnorm_fm(bank, bk, wcol, dst, dk, extra_scale):
                    sq, sqk = ring("a32")
                    act(lambda e: e.activation(out=sq[:, :N], in_=bank[:, :N], func=AF.Square), [bk], [sqk])
                    b2, b2k = ps()
                    mm(b2[:, :N], c32("ones"), sq[:, :N], True, True, [sqk, "C32"], [b2k])
                    rs, rsk = ring("a32")
                    rstd_from(rs[:, :N], b2[:, :N], 1.0 / 128.0, N, [b2k], [rsk], None)
                    if extra_scale != 1.0:
                        dve(lambda e: e.tensor_scalar(out=rs[:, :N], in0=rs[:, :N], scalar1=wcol, scalar2=extra_scale,
                                                      op0=ALU.mult, op1=ALU.mult), [rsk, "PAR"], [rsk])
                    else:
                        dve(lambda e: e.tensor_scalar(out=rs[:, :N], in0=rs[:, :N], scalar1=wcol, scalar2=None,
                                                      op0=ALU.mult), [rsk, "PAR"], [rsk])
                    dve(lambda e: e.tensor_tensor(out=dst[:, :N], in0=bank[:, :N], in1=rs[:, :N], op=ALU.mult),
                        [bk, rsk], [dk])

                if stop == "rms" or stop == f"rms@{gi}":
                    raise _Stop()
                wt, wk = load_w(l, 0)
                for h in range(4):
                    bank, bk = proj_fm(wt, wk, h)
                    headnorm_fm(bank, bk, par("sbqn", l), GA[h], f"ga{h}", 128.0 ** -0.5)
                wt, wk = load_w(l, 1)
                for h in range(4):
                    bank, bk = proj_fm(wt, wk, h)
                    headnorm_fm(bank, bk, par("sbkn", l), GBt[h], f"gb{h}", 1.0)
                    P.dma("sp", (lambda e, h=h: e.dma_start(out=kts.ap()[h, :, t0:t0 + N], in_=GBt[h][:, :N])),
                          reads=[f"gb{h}"], writes=[("kts", h, gi)])
                wt, wk = load_w(l, 2)
                for blk in range(nb):
                    bank, bk = proj_tm(wt, wk, blk, 512)
                    act(lambda e, bank=bank, blk=blk: e.activation(out=TOK[:, blk, :], in_=bank[:, :], func=AF.Copy),
                        [bk], [("tok", blk)])
                for h in range(4):
                    P.dma("sp", (lambda e, h=h: e.dma_start(out=vs.ap()[h, :, gb0:gb0 + nb, :],
                                                           in_=TOK[:, 0:nb, h * 128:(h + 1) * 128])),
                          reads=[("tok", b) for b in range(nb)], writes=[("vs", h, gi)])
                wt, wk = load_w(l, 3)
                for h in range(4):
                    silu_gate(wt, wk, h, SZ4[h], f"sz{h}")

                if stop == "sbproj" or stop == f"sbproj@{gi}":
                    raise _Stop()
                nkb = gb0 + nb
                for h in range(4):
                    OT, otk = PS[4 + h % 2], f"ps{4 + h % 2}"
                    nseg = (nkb + 7) // 8
                    firststep = True
                    for sgi in range(nseg - 1, -1, -1):
                        kb_lo = sgi * 8
                        kb_hi = min(nkb, kb_lo + 8)
                        si = (h * nseg + sgi) % 2
                        ks, vsg = KSEG[si], VSEG[si]
                        gread = [gg for gg, (g0, gn) in enumerate(groups) if g0 < kb_hi and g0 + gn > kb_lo and gg <= gi]
                        P.dma("sp", (lambda e, ks=ks, kb_lo=kb_lo, kb_hi=kb_hi, h=h: e.dma_start(
                            out=ks[:, 0:(kb_hi - kb_lo) * 128], in_=kts.ap()[h, :, kb_lo * 128:kb_hi * 128])),
                            reads=[("kts", h, gg) for gg in gread], writes=[f"kseg{si}"])
                        P.dma("sp", (lambda e, vsg=vsg, kb_lo=kb_lo, kb_hi=kb_hi, h=h: e.dma_start(
                            out=vsg[:, 0:kb_hi - kb_lo, :], in_=vs.ap()[h, :, kb_lo:kb_hi, :])),
                            reads=[("vs", h, gg) for gg in gread], writes=[f"vseg{si}"])
                        for kb in range(kb_hi - 1, kb_lo - 1, -1):
                            j = kb - kb_lo
                            di = kb - gb0
                            isdiag = di >= 0
                            isb0 = (kb == 0)
                            laststep = (kb == 0)
                            Z1, z1k = ps()
                            mm(Z1[:, :N], ks[:, j * 128:(j + 1) * 128], GA[h][:, :N], True, True,
                               [f"kseg{si}", f"ga{h}"], [z1k])
                            E, ek = ring("a32")
                            act(lambda e, E=E, Z1=Z1: e.activation(out=E[:, :N], in_=Z1[:, :N], func=AF.Exp), [z1k], [ek])
                            Zb, zk = ps()
                            mm(Zb[:, :N], ks[:, j * 128:(j + 1) * 128], GA[h][:, :N], True, False,
                               [f"kseg{si}", f"ga{h}"], [zk])
                            SPb, spk = ring("a16")
                            act(lambda e, E=E, SPb=SPb: e.activation(out=SPb[:, :N], in_=E[:, :N], func=AF.Ln,
                                                                     bias=EPSB[:, 1:2], scale=1.0), [ek, "epsb"], [spk])
                            if isdiag:
                                dve(lambda e, SPb=SPb, di=di: e.tensor_tensor(out=SPb[:, :N], in0=SPb[:, :N],
                                                                              in1=cbf(f"m01_{di}", 0, N), op=ALU.mult),
                                     [spk, "CBF"], [spk])
                            mm(Zb[:, :N], cbf("negl0") if isb0 else cbf("negl"), SPb[:, :N], False, not isdiag,
                               [spk, "CBF"], [zk])
                            if isdiag:
                                mm(Zb[:, :N], cbf("ident"), cbf(f"mneg_{di}", 0, N), False, True, ["CBF"], [zk])
                            if not laststep:
                                Cb, ck = ps()
                                mm(Cb[:, :N], cbf("ones0") if isb0 else cbf("ones"), SPb[:, :N], True, True,
                                   [spk, "CBF"], [ck])
                            AT, atk = ring("a16")
                            bias_ap = c32("biasvalid", 0, 1) if isb0 else None
                            if firststep:
                                src, srck = Zb, zk
                            else:
                                T1, t1k = ring("a32")
                                dve(lambda e, T1=T1, Zb=Zb: e.tensor_tensor(out=T1[:, :N], in0=Zb[:, :N], in1=RSB[:, :N],
                                                                          op=ALU.subtract), [zk, "rsb"], [t1k])
                                src, srck = T1, t1k
                            if bias_ap is not None:
                                act(lambda e, AT=AT, src=src, bias_ap=bias_ap: e.activation(
                                    out=AT[:, :N], in_=src[:, :N], func=AF.Exp, bias=bias_ap, scale=1.0),
                                    [srck, "C32"], [atk])
                            else:
                                act(lambda e, AT=AT, src=src: e.activation(out=AT[:, :N], in_=src[:, :N], func=AF.Exp),
                                    [srck], [atk])
                            if not laststep:
                                if firststep:
                                    dve(lambda e, Cb=Cb: e.tensor_copy(out=RSB[:, :N], in_=Cb[:, :N]), [ck], ["rsb"])
                                else:
                                    dve(lambda e, Cb=Cb: e.tensor_tensor(out=RSB[:, :N], in0=RSB[:, :N], in1=Cb[:, :N],
                                                                       op=ALU.add), [ck, "rsb"], ["rsb"])
                            mm(OT[:, :N], vsg[:, j, :], AT[:, :N], firststep, laststep, [f"vseg{si}", atk], [otk])
                            firststep = False
                    dve(lambda e, OT=OT, h=h: e.tensor_tensor(out=OB[h][:, :N], in0=OT[:, :N], in1=SZ4[h][:, :N],
                                                            op=ALU.mult), [otk, f"sz{h}"], [f"ob{h}"])

                if stop == "sb" or stop == f"sb@{gi}":
                    raise _Stop()
                for ti in (4, 5, 6):
                    wt, wk = load_w(l, ti)
                    for f in range(4):
                        cidx = (ti - 4) * 4 + f
                        bank, bk = proj_fm(wt, wk, f)
                        cb = CONVB[cidx % 2]
                        cbk = f"convb{cidx % 2}"
                        act(lambda e, cb=cb, bank=bank: e.activation(out=cb[:, 3:3 + N], in_=bank[:, :N], func=AF.Copy),
                            [bk], [cbk])
                        dve(lambda e, cb=cb, cidx=cidx: e.tensor_copy(out=cb[:, 0:3], in_=TAIL[:, cidx, 0:3]),
                            [("tail", cidx)], [cbk])
                        dve(lambda e, cb=cb, cidx=cidx: e.tensor_copy(out=TAIL[:, cidx, 0:3], in_=cb[:, N:N + 3]),
                             [cbk], [("tail", cidx)])
                        acc, acck = ring("a32")
                        cw = poff["convw"] + l * 48 + cidx * 4
                        dve(lambda e, acc=acc, cb=cb, cw=cw: e.tensor_scalar(out=acc[:, :N], in0=cb[:, 0:N],
                                                                            scalar1=PAR[:, cw:cw + 1], scalar2=None,
                                                                            op0=ALU.mult), [cbk, "PAR"], [acck])
                        for i in (1, 2, 3):
                            dve(lambda e, acc=acc, cb=cb, cw=cw, i=i: e.scalar_tensor_tensor(
                                out=acc[:, :N], in0=cb[:, i:i + N], scalar=PAR[:, cw + i:cw + i + 1], in1=acc[:, :N],
                                op0=ALU.mult, op1=ALU.add), [cbk, "PAR", acck], [acck])
                        sg, sgk = ring("a32")
                        act(lambda e, sg=sg, acc=acc: e.activation(out=sg[:, :N], in_=acc[:, :N], func=AF.Sigmoid),
                            [acck], [sgk])
                        hh = cidx % 4
                        if cidx >= 8:
                            dve(lambda e, acc=acc, sg=sg, hh=hh: e.tensor_tensor(out=GC[hh][:, :N], in0=acc[:, :N],
                                                                               in1=sg[:, :N], op=ALU.mult),
                                [acck, sgk], [f"gc{hh}"])
                        else:
                            dve(lambda e, acc=acc, sg=sg: e.tensor_tensor(out=acc[:, :N], in0=acc[:, :N], in1=sg[:, :N],
                                                                        op=ALU.mult), [acck, sgk], [acck])
                            sq, sqk = ring("a32")
                            dve(lambda e, sq=sq, acc=acc: e.tensor_tensor(out=sq[:, :N], in0=acc[:, :N], in1=acc[:, :N],
                                                                         op=ALU.mult), [acck], [sqk])
                            b2, b2k = ps()
                            mm(b2[:, :N], c32("ones"), sq[:, :N], True, True, [sqk, "C32"], [b2k])
                            rs, rsk = ring("a32")
                            rstd_from(rs[:, :N], b2[:, :N], 1.0, N, [b2k], [rsk], None)
                            dst, dk = (GA[hh], f"ga{hh}") if cidx < 4 else (GBt[hh], f"gb{hh}")
                            sc = 128.0 ** -0.5 if cidx < 4 else 1.0
                            dve(lambda e, dst=dst, acc=acc, rs=rs, sc=sc: e.scalar_tensor_tensor(
                                out=dst[:, :N], in0=acc[:, :N], scalar=sc, in1=rs[:, :N], op0=ALU.mult, op1=ALU.mult),
                                [acck, rsk], [dk])
                wt, wk = load_w(l, 7)
                for h in range(4):
                    silu_gate(wt, wk, h, SZ4[h], f"sz{h}")
                wt, wk = load_w(l, 8)
                for blk in range(nb):
                    bank, bk = proj_tm(wt, wk, blk, 8)
                    act(lambda e, bank=bank, blk=blk: e.activation(out=BD[:, blk, 0:4], in_=bank[:, 0:4], func=AF.Sigmoid),
                        [bk], [("bd", blk)])
                    if gb0 + blk == 0:
                        dve(lambda e, blk=blk: e.tensor_tensor(out=BD[:, blk, 0:4], in0=BD[:, blk, 0:4],
                                                              in1=c32("validcol"), op=ALU.mult),
                            [("bd", blk), "C32"], [("bd", blk)])
                    dve(lambda e, bank=bank, blk=blk: e.tensor_tensor(out=BD[:, blk, 4:8], in0=bank[:, 4:8],
                                                                     in1=par("dtb", l * 4, 4), op=ALU.add),
                        [bk, "PAR"], [("bd", blk)])
                    act(lambda e, blk=blk: e.activation(out=BD[:, blk, 4:8], in_=BD[:, blk, 4:8], func=AF.Exp),
                        [("bd", blk)], [("bd", blk)])
                    act(lambda e, blk=blk: e.activation(out=BD[:, blk, 4:8], in_=BD[:, blk, 4:8], func=AF.Ln,
                                                        bias=EPSB[:, 1:2], scale=1.0), [("bd", blk), "epsb"], [("bd", blk)])
                    dve(lambda e, blk=blk: e.tensor_tensor(out=BD[:, blk, 4:8], in0=BD[:, blk, 4:8],
                                                          in1=NEGA[:, l * 4:l * 4 + 4], op=ALU.mult),
                        [("bd", blk), "nega"], [("bd", blk)])

                if stop == "gdnproj" or stop == f"gdnproj@{gi}":
                    raise _Stop()
                for blk in range(nb):
                    c0, c1 = blk * 128, (blk + 1) * 128
                    bdk = ("bd", blk)
                    gc_b, gck = ps()
                    mm(gc_b[:, 0:4], c32("uincl"), BD[:, blk, 4:8], True, True, ["C32", bdk], [gck])
                    st, stk = GST, "gst"
                    dve(lambda e, st=st, gc_b=gc_b: e.tensor_copy(out=st[:, 0:4], in_=gc_b[:, 0:4]), [gck], [stk])
                    dve(lambda e, st=st, blk=blk: e.tensor_scalar(out=st[:, 4:8], in0=BD[:, blk, 0:4], scalar1=-1.0,
                                                                 scalar2=None, op0=ALU.mult), [bdk], [stk])
                    act(lambda e, st=st: e.activation(out=st[:, 8:12], in_=st[:, 0:4], func=AF.Exp), [stk], [stk])
                    dve(lambda e, st=st, blk=blk: e.tensor_tensor(out=st[:, 8:12], in0=st[:, 8:12], in1=BD[:, blk, 0:4],
                                                                 op=ALU.mult), [stk, bdk], [stk])
                    for h in range(4):
                        ng, ngk = ring("b32")
                        dve(lambda e, ng=ng, blk=blk, h=h: e.tensor_scalar(out=ng[:, :], in0=c32("ones"),
                                                                          scalar1=BD[:, blk, 4 + h:5 + h], scalar2=-1.0,
                                                                          op0=ALU.mult, op1=ALU.mult),
                            ["C32", bdk], [ngk])
                        gb_b, gbk = ps()
                        mm(gb_b[:, 0:128], ng[:, :], c32("uincl"), True, True, [ngk, "C32"], [gbk])
                        dve(lambda e, st=st, gb_b=gb_b, h=h: e.tensor_scalar(
                            out=st[:, 12 + h:13 + h], in0=st[:, h:h + 1], scalar1=gb_b[:, 127:128], scalar2=-1.0,
                            op0=ALU.add, op1=ALU.mult), [stk, gbk], [stk])
                        act(lambda e, st=st, h=h: e.activation(out=st[:, 12 + h:13 + h], in_=st[:, 12 + h:13 + h],
                                                              func=AF.Exp), [stk], [stk])
                        act(lambda e, st=st, gb_b=gb_b, h=h: e.activation(out=st[:, 16 + h:17 + h], in_=gb_b[:, 127:128],
                                                                         func=AF.Exp, scale=-1.0), [gbk], [stk])
                        Dm, dmk = ring("b32")
                        dve(lambda e, Dm=Dm, gb_b=gb_b, st=st, h=h: e.tensor_scalar(
                            out=Dm[:, :], in0=gb_b[:, 0:128], scalar1=st[:, h:h + 1], scalar2=0.0, op0=ALU.add,
                            op1=ALU.min), [gbk, stk], [dmk])
                        act(lambda e, Dm=Dm: e.activation(out=Dm[:, :], in_=Dm[:, :], func=AF.Exp), [dmk], [dmk])
                        Es, esk = ring("b32")
                        Ec, eck = ring("b32")
                        dve(lambda e, Es=Es, Dm=Dm: e.tensor_tensor(out=Es[:, :], in0=Dm[:, :], in1=c32("strict"),
                                                                  op=ALU.mult), [dmk, "C32"], [esk])
                        dve(lambda e, Ec=Ec, Dm=Dm: e.tensor_tensor(out=Ec[:, :], in0=Dm[:, :], in1=c32("causal"),
                                                                  op=ALU.mult), [dmk, "C32"], [eck])
                        EG, egk = ring("b32")
                        act(lambda e, EG=EG, gb_b=gb_b: e.activation(out=EG[:, :], in_=gb_b[:, 0:128], func=AF.Exp,
                                                                     scale=-1.0), [gbk], [egk])
                        qd, qdk = ring("c32r")
                        dve(lambda e, qd=qd, EG=EG, h=h: e.tensor_tensor(out=qd[:, :], in0=GA[h][:, c0:c1], in1=EG[:, :],
                                                                       op=ALU.mult), [f"ga{h}", egk], [qdk])
                        kk_b, kkk = ps()
                        mm(kk_b[:, 0:128], GBt[h][:, c0:c1], GBt[h][:, c0:c1], True, True, [f"gb{h}"], [kkk])
                        PT, ptk = ring("b32")
                        dve(lambda e, PT=PT, kk_b=kk_b, st=st, Es=Es, h=h: e.scalar_tensor_tensor(
                            out=PT[:, :], in0=kk_b[:, 0:128], scalar=st[:, 4 + h:5 + h], in1=Es[:, :], op0=ALU.mult,
                            op1=ALU.mult), [kkk, stk, esk], [ptk])
                        qk_b, qkk = ps()
                        mm(qk_b[:, 0:128], GA[h][:, c0:c1], GBt[h][:, c0:c1], True, True, [f"ga{h}", f"gb{h}"], [qkk])
                        aqk, aqkk = ring("b32")
                        dve(lambda e, aqk=aqk, qk_b=qk_b, Ec=Ec: e.tensor_tensor(out=aqk[:, :], in0=qk_b[:, 0:128],
                                                                               in1=Ec[:, :], op=ALU.mult),
                            [qkk, eck], [aqkk])

                        def tr32(src, srck, kind):
                            tb_, tbk_ = ps()
                            mm(tb_[:, 0:128], src, c32("ident"), True, True, [srck, "C32"], [tbk_])
                            d_, dk_ = ring(kind)
                            act(lambda e: e.activation(out=d_[:, :], in_=tb_[:, 0:128], func=AF.Copy), [tbk_], [dk_])
                            return d_, dk_
                        Pm, pmk = tr32(PT[:, :], ptk, "b32")
                        aqkT, aqkTk = tr32(aqk[:, :], aqkk, "c32r")
                        slot, sk_ = pst()
                        pe(lambda e, slot=slot, h=h: e.transpose(slot, GBt[h][:, c0:c1], cbf("ident")),
                           [f"gb{h}", "CBF"], [sk_])
                        kbg, kbgk = ring("c32r")
                        dve(lambda e, kbg=kbg, slot=slot, st=st, h=h: e.tensor_scalar(
                            out=kbg[:, :], in0=slot, scalar1=st[:, 8 + h:9 + h], scalar2=None, op0=ALU.mult),
                            [sk_, stk], [kbgk])
                        kd, kdk = ring("c32r")
                        dve(lambda e, kd=kd, slot=slot, st=st, h=h: e.tensor_scalar(
                            out=kd[:, :], in0=slot, scalar1=st[:, 12 + h:13 + h], scalar2=None, op0=ALU.mult),
                            [sk_, stk], [kdk])
                        slot2, sk2 = pst()
                        pe(lambda e, slot2=slot2, h=h: e.transpose(slot2, GC[h][:, c0:c1], cbf("ident")),
                           [f"gc{h}", "CBF"], [sk2])
                        vb, vbk = ring("c32r")
                        dve(lambda e, vb=vb, slot2=slot2, blk=blk, h=h: e.tensor_scalar(
                            out=vb[:, :], in0=slot2, scalar1=BD[:, blk, h:h + 1], scalar2=None, op0=ALU.mult),
                            [sk2, bdk], [vbk])
                        R32, r32k = ring("c32r")
                        dve(lambda e, R32=R32, Pm=Pm: e.tensor_tensor(out=R32[:, :], in0=Pm[:, :], in1=c32("ident"),
                                                                    op=ALU.add), [pmk, "C32"], [r32k])
                        Pc, pck, PTc, ptck = Pm, pmk, PT, ptk
                        for lev in range(1, 7):
                            b1, b1k = ps()
                            mm(b1[:, 0:128], Pc[:, :], PTc[:, :], True, True, [pck, ptck], [b1k])
                            PTn, ptnk = ring("b32")
                            act(lambda e, PTn=PTn, b1=b1: e.activation(out=PTn[:, :], in_=b1[:, 0:128], func=AF.Copy),
                                [b1k], [ptnk])
                            if lev < 6:
                                b2, b2k = ps()
                                mm(b2[:, 0:128], PTc[:, :], Pc[:, :], True, True, [pck, ptck], [b2k])
                                Pn, pnk = ring("b32")
                                dve(lambda e, Pn=Pn, b2=b2: e.tensor_copy(out=Pn[:, :], in_=b2[:, 0:128]), [b2k], [pnk])
                            b3, b3k = ps()
                            mm(b3[:, 0:128], PTn[:, :], R32[:, :], True, True, [ptnk, r32k], [b3k])
                            Rn, rnk = ring("c32r")
                            dve(lambda e, Rn=Rn, R32=R32, b3=b3: e.tensor_tensor(out=Rn[:, :], in0=R32[:, :],
                                                                               in1=b3[:, 0:128], op=ALU.add),
                                [r32k, b3k], [rnk])
                            R32, r32k = Rn, rnk
                            PTc, ptck = PTn, ptnk
                            if lev < 6:
                                Pc, pck = Pn, pnk
                        wb_, wbk = ps()
                        mm(wb_[:, 0:128], kbg[:, :], R32[:, :], True, True, [kbgk, r32k], [wbk])
                        nwT, nwTk = ring("b32")
                        dve(lambda e, nwT=nwT, wb_=wb_: e.tensor_scalar(out=nwT[:, :], in0=wb_[:, 0:128], scalar1=-1.0,
                                                                      scalar2=None, op0=ALU.mult), [wbk], [nwTk])
                        vn_b, vnk = ps()
                        mm(vn_b[:, 0:128], R32[:, :], vb[:, :], True, False, [r32k, vbk], [vnk])
                        mm(vn_b[:, 0:128], nwT[:, :], SG32[h][:, :], False, True, [nwTk, f"sg32_{h}"], [vnk])
                        vn, vn16k = ring("b32")
                        act(lambda e, vn=vn, vn_b=vn_b: e.activation(out=vn[:, :], in_=vn_b[:, 0:128], func=AF.Copy),
                            [vnk], [vn16k])
                        o_b, obk = ps()
                        mm(o_b[:, 0:128], qd[:, :], SG32[h][:, :], True, False, [qdk, f"sg32_{h}"], [obk])
                        mm(o_b[:, 0:128], aqkT[:, :], vn[:, :], False, True, [aqkTk, vn16k], [obk])
                        s_b, sbk = ps()
                        mm(s_b[:, 0:128], kd[:, :], vn[:, :], True, True, [kdk, vn16k], [sbk])
                        finish_o(P, o_b, obk, par("gdon", l), OB[4 + h], f"ob{4 + h}", SZ4[h], f"sz{h}", c0, c1,
                                 ring, pst, cbf, act, dve, pe, rstd_from)
                        dve(lambda e, s_b=s_b, st=st, h=h: e.scalar_tensor_tensor(
                            out=SG32[h][:, :], in0=SG32[h][:, :], scalar=st[:, 16 + h:17 + h], in1=s_b[:, 0:128],
                            op0=ALU.mult, op1=ALU.add), [f"sg32_{h}", stk, sbk], [f"sg32_{h}"])

                if stop == "gdn" or stop == f"gdn@{gi}":
                    raise _Stop()
                wt, wk = load_w(l, 9)
                for h in range(4):
                    silu_gate(wt, wk, h, GA[h], f"ga{h}")
                wt, wk = load_w(l, 10)
                for h in range(4):
                    bank, bk = proj_fm(wt, wk, h)
                    sg, sgk = ring("a32")
                    act(lambda e, sg=sg, bank=bank: e.activation(out=sg[:, :N], in_=bank[:, :N], func=AF.Sigmoid),
                        [bk], [sgk])
                    oml = LB[:, h, depth + l:depth + l + 1]
                    lbc = LB[:, h, l:l + 1]
                    nm, nmk = ring("b32")
                    dve(lambda e, nm=nm, oml=oml: e.tensor_scalar(out=nm[:, 0:1], in0=oml, scalar1=-1.0, scalar2=None,
                                                                 op0=ALU.mult), ["lb"], [nmk])
                    dve(lambda e, sg=sg, nm=nm, oml=oml, h=h: e.tensor_scalar(out=HK[h][:, :N], in0=sg[:, :N],
                                                                             scalar1=nm[:, 0:1], scalar2=oml,
                                                                             op0=ALU.mult, op1=ALU.add),
                        [sgk, nmk, "lb"], [f"hk{h}"])
                    dve(lambda e, sg=sg, oml=oml, lbc=lbc: e.tensor_scalar(out=sg[:, :N], in0=sg[:, :N], scalar1=oml,
                                                                          scalar2=lbc, op0=ALU.mult, op1=ALU.add),
                        [sgk, "lb"], [sgk])
                    act(lambda e, sg=sg, h=h: e.activation(out=HG[h][:, :N], in_=sg[:, :N], func=AF.Ln), [sgk],
                        [f"hgl{h}"])
                wt, wk = load_w(l, 11)
                for blk in range(nb):
                    bank, bk = proj_tm(wt, wk, blk, 512)
                    if gb0 + blk == 0:
                        dve(lambda e, bank=bank, blk=blk: e.tensor_scalar(out=TOK[:, blk, :], in0=bank[:, :],
                                                                         scalar1=c32("validcol", 0, 1), scalar2=None,
                                                                         op0=ALU.mult), [bk, "C32"], [("tok", blk)])
                    else:
                        act(lambda e, bank=bank, blk=blk: e.activation(out=TOK[:, blk, :], in_=bank[:, :], func=AF.Copy),
                            [bk], [("tok", blk)])
                wt, wk = load_w(l, 12)
                for h in range(4):
                    silu_gate(wt, wk, h, SZ4[h], f"sz{h}")

                if stop == "hgproj" or stop == f"hgproj@{gi}":
                    raise _Stop()
                nch = N // 64
                for h in range(4):
                    GTt, gtk = ring("a32")
                    dve(lambda e, GTt=GTt, h=h: e.tensor_tensor_scan(out=GTt[:, :N], data0=c32("chm", 0, N),
                                                                    data1=HG[h][:, :N], initial=0.0,
                                                                    op0=ALU.mult, op1=ALU.add),
                        [f"hgl{h}", "C32"], [gtk])
                    if stop == "hgA" or stop == f"hgA@{gi}":
                        raise _Stop()
                    NM, nmk2 = ring("b32")
                    for cc in range(nch):
                        dve(lambda e, NM=NM, GTt=GTt, cc=cc: e.tensor_scalar(
                            out=NM[:, cc:cc + 1], in0=GTt[:, cc * 64 + 31:cc * 64 + 32], scalar1=-1.0, scalar2=None,
                            op0=ALU.mult), [gtk], [nmk2])
                    if stop == "hgB" or stop == f"hgB@{gi}":
                        raise _Stop()
                    E1, e1k = ring("a32")
                    E2, e2k = ring("a32")
                    E3, e3k = ring("a32")
                    E4, e4k = ring("a32")
                    act(lambda e, E3=E3, GTt=GTt: e.activation(out=E3[:, :N], in_=GTt[:, :N], func=AF.Exp), [gtk], [e3k])
                    for cc in range(nch):
                        q0, q1 = cc * 64, (cc + 1) * 64
                        act(lambda e, E1=E1, GTt=GTt, NM=NM, cc=cc, q0=q0, q1=q1: e.activation(
                            out=E1[:, q0:q1], in_=GTt[:, q0:q1], func=AF.Exp, bias=NM[:, cc:cc + 1], scale=1.0),
                            [gtk, nmk2], [e1k])
                        act(lambda e, E2=E2, GTt=GTt, q0=q0, q1=q1: e.activation(
                            out=E2[:, q0:q1], in_=GTt[:, q0:q1], func=AF.Exp, bias=GTt[:, q0 + 31:q0 + 32], scale=-1.0),
                            [gtk], [e2k])
                        act(lambda e, E4=E4, GTt=GTt, q0=q0, q1=q1: e.activation(
                            out=E4[:, q0:q1], in_=GTt[:, q0:q1], func=AF.Exp, bias=GTt[:, q1 - 1:q1], scale=-1.0),
                            [gtk], [e4k])
                    if stop == "hgC" or stop == f"hgC@{gi}":
                        raise _Stop()
                    qtg, qtgk = ring("a16")
                    ktg, ktgk = ring("a16")
                    qdg, qdgk = ring("a16")
                    kdg, kdgk = ring("a16")
                    dve(lambda e, qtg=qtg, E1=E1, h=h: e.tensor_tensor(out=qtg[:, :N], in0=GA[h][:, :N], in1=E1[:, :N],
                                                                     op=ALU.mult), [f"ga{h}", e1k], [qtgk])
                    dve(lambda e, ktg=ktg, E2=E2, h=h: e.tensor_tensor(out=ktg[:, :N], in0=HK[h][:, :N], in1=E2[:, :N],
                                                                     op=ALU.mult), [f"hk{h}", e2k], [ktgk])
                    dve(lambda e, qdg=qdg, E3=E3, h=h: e.tensor_tensor(out=qdg[:, :N], in0=GA[h][:, :N], in1=E3[:, :N],
                                                                     op=ALU.mult), [f"ga{h}", e3k], [qdgk])
                    dve(lambda e, kdg=kdg, E4=E4, h=h: e.tensor_tensor(out=kdg[:, :N], in0=HK[h][:, :N], in1=E4[:, :N],
                                                                     op=ALU.mult), [f"hk{h}", e4k], [kdgk])
                    if stop == "hgD" or stop == f"hgD@{gi}":
                        raise _Stop()
                    for blk in range(nb):
                        c0, c1 = blk * 128, (blk + 1) * 128
                        hvk = ("tok", blk)
                        slot, sk_ = pst()
                        pe(lambda e, slot=slot, kdg=kdg, c0=c0, c1=c1: e.transpose(slot, kdg[:, c0:c1], cbf("ident")),
                           [kdgk, "CBF"], [sk_])
                        if stop == f"hgJ{h}_{blk}@{gi}":
                            raise _Stop()
                        kds = []
                        for c in range(2):
                            kd, kdk = ring("b16")
                            if True:
                                dve(lambda e, kd=kd, slot=slot, c=c: e.tensor_scalar(
                                    out=kd[:, :], in0=slot, scalar1=c32(f"ch{c}col", 0, 1), scalar2=None, op0=ALU.mult),
                                    [sk_, "C32"], [kdk])
                            else:
                                act(lambda e, kd=kd, slot=slot, c=c: e.activation(
                                    out=kd[:, :], in_=slot, func=AF.Copy, scale=c32(f"ch{c}col", 0, 1)),
                                    [sk_, "C32"], [kdk])
                            kds.append((kd, kdk))
                        if stop == f"hgG{h}_{blk}@{gi}":
                            raise _Stop()
                        at_b, atbk = ps()
                        mm(at_b[:, 0:128], ktg[:, c0:c1], qtg[:, c0:c1], True, True, [ktgk, qtgk], [atbk])
                        atc, atck = ring("b32")
                        dve(lambda e, atc=atc, at_b=at_b: e.tensor_scalar(out=atc[:, :], in0=at_b[:, 0:128],
                                                                        scalar1=-1e30, scalar2=1e30, op0=ALU.max,
                                                                        op1=ALU.min), [atbk], [atck])
                        atm, atmk = ring("b16")
                        dve(lambda e, atm=atm, atc=atc: e.tensor_tensor(out=atm[:, :], in0=atc[:, :], in1=c32("maskt"),
                                                                      op=ALU.mult), [atck, "C32"], [atmk])
                        if stop == f"hgH{h}_{blk}@{gi}":
                            raise _Stop()
                        QDp = QD[h % 2]
                        qdpk = [f"qd{h % 2}_0", f"qd{h % 2}_1"]
                        for c in range(2):
                            a0, a1 = c * 64, (c + 1) * 64
                            dve(lambda e, c=c, a0=a0, a1=a1, qdg=qdg, c0=c0, QDp=QDp: e.tensor_copy(
                                out=QDp[c][:, a0:a1], in_=qdg[:, c0 + a0:c0 + a1]), [qdgk], [qdpk[c]])
                        if stop == f"hgI{h}_{blk}@{gi}":
                            raise _Stop()
                        o_b, obk = PS[4 + h % 2], f"ps{4 + h % 2}"
                        for c in range(2):
                            a0, a1 = c * 64, (c + 1) * 64
                            mm(o_b[:, 0:128], QDp[c][:, :], SH16[h][:, :], c == 0, False, [qdpk[c], f"sh16_{h}"], [obk])
                            if c == 0:
                                mm(o_b[:, 0:128], atm[:, :], TOK[:, blk, h * 128:(h + 1) * 128], False, False,
                                   [atmk, hvk], [obk])
                            s_b, sbk = ps()
                            mm(s_b[:, 0:128], kds[c][0][:, :], TOK[:, blk, h * 128:(h + 1) * 128], True, True,
                               [kds[c][1], hvk], [sbk])
                            dve(lambda e, s_b=s_b, E3=E3, col=c0 + a1 - 1, h=h: e.scalar_tensor_tensor(
                                out=SH32[h][:, :], in0=SH32[h][:, :], scalar=E3[:, col:col + 1], in1=s_b[:, 0:128],
                                op0=ALU.mult, op1=ALU.add), [f"sh32_{h}", e3k, sbk], [f"sh32_{h}"])
                            act(lambda e, h=h: e.activation(out=SH16[h][:, :], in_=SH32[h][:, :], func=AF.Copy),
                                [f"sh32_{h}"], [f"sh16_{h}"])
                        mm(o_b[:, 0:128], ZERO16[:, :], SH16[h][:, :], False, True, ["zero16", f"sh16_{h}"], [obk])
                        if stop == f"hgF{h}_{blk}@{gi}":
                            raise _Stop()
                        finish_o(P, o_b, obk, par("hgon", l), OB[8 + h], f"ob{8 + h}", SZ4[h], f"sz{h}", c0, c1,
                                 ring, pst, cbf, act, dve, pe, rstd_from)
                        if stop == f"hgE{h}_{blk}@{gi}":
                            raise _Stop()

                if stop == "hg" or stop == f"hg@{gi}":
                    raise _Stop()
                YACC = HK + HG
                yacck = [f"hk{h}" for h in range(4)] + [f"hgl{h}" for h in range(4)]
                for b in range(3):
                    wbt_, wbk_ = load_w(l, T_WB + b)
                    for half in range(2):
                        mt_, mk_ = load_w(l, 13 + b * 2 + half)
                        for f in range(4):
                            cch = half * 4 + f
                            gbank, gk_ = proj_fm(mt_, mk_, f)
                            gate, gatek = ring("a32")
                            act(lambda e, gate=gate, gbank=gbank: e.activation(out=gate[:, :N], in_=gbank[:, :N],
                                                                              func=AF.Sigmoid), [gk_], [gatek])
                            pb_, pbk = ps()
                            for h in range(4):
                                mm(pb_[:, :N], wbt_[:, h * 1024 + cch * 128:h * 1024 + (cch + 1) * 128],
                                   OB[b * 4 + h][:, :N], h == 0, h == 3, [wbk_, f"ob{b * 4 + h}"], [pbk])
                            ya, yak = YACC[cch], yacck[cch]
                            if b == 0:
                                dve(lambda e, ya=ya, pb_=pb_, gate=gate: e.tensor_tensor(
                                    out=ya[:, :N], in0=pb_[:, :N], in1=gate[:, :N], op=ALU.mult), [pbk, gatek], [yak])
                            else:
                                dve(lambda e, gate=gate, pb_=pb_: e.tensor_tensor(
                                    out=gate[:, :N], in0=pb_[:, :N], in1=gate[:, :N], op=ALU.mult), [pbk, gatek], [gatek])
                                if b == 1:
                                    dve(lambda e, ya=ya, gate=gate: e.tensor_tensor(
                                        out=ya[:, :N], in0=ya[:, :N], in1=gate[:, :N], op=ALU.add), [yak, gatek], [yak])
                                else:
                                    dve(lambda e, ya=ya, gate=gate, cch=cch: e.tensor_tensor(
                                        out=YT[cch][:, :N], in0=ya[:, :N], in1=gate[:, :N], op=ALU.add),
                                        [yak, gatek], [f"yt{cch}"])
                if stop == "merge" or stop == f"merge@{gi}":
                    raise _Stop()
                wot = [load_w(l, T_WO + e_) for e_ in range(2)]
                for blk in range(nb):
                    hb = HB[blk % 2]
                    hbk = f"hb{blk % 2}"
                    load_hblk(blk, hb, hbk)
                    for e_ in range(2):
                        ob_, obk_ = ps()
                        for cch in range(8):
                            mm(ob_[:, :], YT[cch][:, blk * 128:(blk + 1) * 128],
                               wot[e_][0][:, cch * 512:(cch + 1) * 512], cch == 0, cch == 7,
                               [f"yt{cch}", wot[e_][1]], [obk_])
                        dve(lambda e, hb=hb, ob_=ob_, e_=e_: e.tensor_tensor(
                            out=hb[:, e_ * 512:(e_ + 1) * 512], in0=hb[:, e_ * 512:(e_ + 1) * 512], in1=ob_[:, :],
                            op=ALU.add), [hbk, obk_], [hbk])
                    tg = (gb0 + blk) * 128
                    if last:
                        if gb0 + blk > 0:
                            P.dma("sp", (lambda e, hb=hb, tg=tg: e.dma_start(out=dr["y"].ap()[tg - 128:tg, :], in_=hb[:, :])),
                                  reads=[hbk], writes=[("y", gb0 + blk)])
                    else:
                        P.dma("sp", (lambda e, hb=hb, tg=tg: e.dma_start(out=hbuf.ap()[tg:tg + 128, :], in_=hb[:, :])),
                              reads=[hbk], writes=[("h", gb0 + blk)])

    except _Stop:
        pass
    P.barrier()
    P.emit()
    es.close()
    return nc, (p32, pb)


def finish_o(P, o_b, obk, wcol, dst, dk, sz, szk, c0, c1, ring, pst, cbf, act, dve, pe, rstd_from):
    junk, jk = ring("b32")
    st, stk = ring("b32")
    act(lambda e: e.activation(out=junk[:, :], in_=o_b[:, 0:128], func=AF.Square, accum_out=st[:, 0:1]),
        [obk], [jk, stk])
    rstd_from(st[:, 1:2], st[:, 0:1], 1.0 / 128.0, 1, [stk], [stk], None)
    on, onk = ring("b16")
    dve(lambda e: e.tensor_scalar(out=on[:, :], in0=o_b[:, 0:128], scalar1=st[:, 1:2], scalar2=None, op0=ALU.mult),
        [obk, stk], [onk])
    slot, sk_ = pst()
    pe(lambda e: e.transpose(slot, on[:, :], cbf("ident")), [onk, "CBF"], [sk_])
    dve(lambda e: e.scalar_tensor_tensor(out=dst[:, c0:c1], in0=slot, scalar=wcol, in1=sz[:, c0:c1], op0=ALU.mult,
                                         op1=ALU.mult), [sk_, "PAR", szk], [dk])


_CACHE = {}


def kernel(x, meta_tokens, norm_w, w_in, sb_q_norm, sb_k_norm, gdn_conv_w, gdn_a_log, gdn_dt_bias,
           gdn_out_norm, hgrn_lb_logits, hgrn_out_norm, w_branch, w_out):
    x = np.asarray(x, dtype=np.float32)
    bsz, seq, _ = x.shape
    depth = int(np.asarray(norm_w).shape[0])
    nblk = seq // 128 + 1
    key = (nblk, depth)
    if key not in _CACHE:
        _CACHE[key] = build(nblk, depth)
    nc, (p32, pb) = _CACHE[key]
    f = lambda a: np.ascontiguousarray(np.asarray(a, dtype=np.float32))
    parp, _ = _param_pack(depth, f(norm_w), f(sb_q_norm), f(sb_k_norm), f(gdn_conv_w), f(gdn_a_log),
                          f(gdn_dt_bias), f(gdn_out_norm), f(hgrn_lb_logits), f(hgrn_out_norm))
    n_cores = 8
    nwrow = np.ascontiguousarray(np.broadcast_to(f(norm_w)[:, None, :], (depth, 128, D)))
    in_maps = []
    for c in range(n_cores):
        b = c % bsz
        in_maps.append({"x": np.ascontiguousarray(x[b]), "meta": f(meta_tokens), "w_in": f(w_in),
                        "w_branch": f(w_branch), "w_out": f(w_out), "c32": p32, "cbf": pb, "par": parp,
                        "nwrow": nwrow})
    res = run_bass_kernel_spmd(nc, in_maps, core_ids=list(range(n_cores)))
    out = np.stack([np.asarray(res.results[b]["y"], dtype=np.float32) for b in range(bsz)], axis=0)
    return out
```

```python
from contextlib import ExitStack
import numpy as np
import ml_dtypes
import concourse.bass as bass
import concourse.mybir as mybir
from concourse.bass_utils import run_bass_kernel_spmd

F32 = mybir.dt.float32
BF16 = mybir.dt.bfloat16
AF = mybir.ActivationFunctionType
ALU = mybir.AluOpType
AX = mybir.AxisListType

D = 1024
NIN = 9224
NEG = -30000.0
EPS = 1e-6

WT_SPECS = [
    (0, 512), (512, 512), (1024, 512), (1536, 512),
    (2048, 512), (2560, 512), (3072, 512), (3584, 512),
    (4096, 8),
    (4104, 512), (4616, 512), (5128, 512), (5640, 512),
    (6152, 512), (6664, 512), (7176, 512), (7688, 512), (8200, 512), (8712, 512),
]
T_WB = 19
T_WO = 22
NTILES = 24


class _RecCall:
    def __init__(self, name, a, k):
        self.name, self.a, self.k = name, a, k

    def __call__(self, e):
        return getattr(e, self.name)(*self.a, **self.k)


class _Rec:
    def __getattr__(self, name):
        return lambda *a, **k: _RecCall(name, a, k)


_REC = _Rec()


class Prog:
    ENGS = ("pe", "act", "dve", "pool", "sp")

    def __init__(self, nc, es, n_dma_sems=20):
        self.nc = nc
        self.streams = {e: [] for e in self.ENGS}
        self.cnt = {e: 0 for e in self.ENGS}
        self.sem = {e: es.enter_context(nc.semaphore("s_" + e)) for e in ("pe", "act", "dve", "pool")}
        self.dsem = {q: [es.enter_context(nc.semaphore(f"d_{q}{i}")) for i in range(n_dma_sems)]
                     for q in ("sp", "pool")}
        self.dval = {q: [0] * n_dma_sems for q in ("sp", "pool")}
        self.dnext = {q: 0 for q in ("sp", "pool")}
        self.semobj = {}
        for e, s in self.sem.items():
            self.semobj[("c", e)] = s
        for q in self.dsem:
            for i, s in enumerate(self.dsem[q]):
                self.semobj[("d", q, i)] = s
        self.seen = {e: {} for e in self.ENGS}
        self.lastw = {}
        self.readers = {}

    def _deps(self, eng, reads, writes, extra=()):
        need = {}

        def add(tok):
            if tok is None:
                return
            k, v = tok
            if need.get(k, 0) < v:
                need[k] = v
        for r in reads:
            add(self.lastw.get(r))
        for w in writes:
            add(self.lastw.get(w))
            for t in self.readers.get(w, ()):
                add(t)
        for t in extra:
            add(t)
        waits = []
        sn = self.seen[eng]
        for k, v in need.items():
            if eng == "pe" and k == ("c", "pe"):
                continue
            if sn.get(k, 0) >= v:
                continue
            sn[k] = v
            waits.append((k, v))
        return waits

    def _commit(self, tok, reads, writes):
        for w in writes:
            self.lastw[w] = tok
            self.readers[w] = []
        for r in reads:
            if r in writes:
                continue
            self.readers.setdefault(r, []).append(tok)

    def op(self, eng, fn, reads=(), writes=()):
        reads = [r for r in reads if r is not None]
        writes = [w for w in writes if w is not None]
        ex = [r for r in reads if isinstance(r, str) and r.startswith("ps")]
        if ex:
            writes = writes + [r for r in ex if r not in writes]
            reads = [r for r in reads if r not in ex]
        waits = self._deps(eng, reads, writes)
        self.cnt[eng] += 1
        tok = (("c", eng), self.cnt[eng])
        self.streams[eng].append((waits, fn(_REC), ("c", eng), 1))
        self._commit(tok, reads, writes)

    def dma(self, q, fn, reads=(), writes=()):
        i = self.dnext[q]
        self.dnext[q] = (i + 1) % len(self.dsem[q])
        k = ("d", q, i)
        prev = (k, self.dval[q][i]) if self.dval[q][i] > 0 else None
        waits = self._deps(q, list(reads), list(writes), extra=(prev,))
        self.dval[q][i] += 16
        tok = (k, self.dval[q][i])
        self.streams[q].append((waits, fn(_REC), k, 16))
        self._commit(tok, reads, writes)

    def barrier(self):
        toks = [(("c", e), self.cnt[e]) for e in ("pe", "act", "dve", "pool") if self.cnt[e] > 0]
        for q in self.dsem:
            for i, v in enumerate(self.dval[q]):
                if v > 0:
                    toks.append((("d", q, i), v))
        for e in self.ENGS:
            waits = []
            for k, v in toks:
                if self.seen[e].get(k, 0) < v:
                    self.seen[e][k] = v
                    waits.append((k, v))
            if waits:
                self.streams[e].append((waits, None, None, 0))
        self.lastw.clear()
        self.readers.clear()

    def emit(self):
        nc = self.nc
        P = self

        def replay(name, e):
            for waits, fn, k, inc in P.streams[name]:
                for (wk, wv) in waits:
                    e.wait_ge(P.semobj[wk], wv)
                if fn is not None:
                    fn(e).then_inc(P.semobj[k], inc)

        with nc.Block() as block:
            @block.tensor
            def _(e):
                replay("pe", e)

            @block.scalar
            def _(e):
                replay("act", e)

            @block.vector
            def _(e):
                replay("dve", e)

            @block.gpsimd
            def _(e):
                replay("pool", e)

            @block.sync
            def _(e):
                replay("sp", e)


def _const_packs():
    i = np.arange(128)
    c = {}
    c["ident"] = np.eye(128, dtype=np.float32)
    c["ones"] = np.ones((128, 128), np.float32)
    c["uincl"] = (i[:, None] <= i[None, :]).astype(np.float32)
    c["strict"] = (i[None, :] < i[:, None]).astype(np.float32)
    c["causal"] = (i[None, :] <= i[:, None]).astype(np.float32)
    ch = i // 64
    same = (ch[:, None] == ch[None, :])
    mid = ch * 64 + 31
    c["umid"] = (same * ((i[:, None] <= i[None, :]).astype(np.float32)
                         - (i[:, None] <= mid[None, :]).astype(np.float32))).astype(np.float32)
    c["ucs"] = (same & (i[:, None] <= i[None, :])).astype(np.float32)
    c["uend"] = (same & (i[:, None] > i[None, :])).astype(np.float32)
    c["maskt"] = (same & (i[:, None] <= i[None, :])).astype(np.float32)
    valid = (i >= 112).astype(np.float32)
    c["validcol"] = np.repeat(valid[:, None], 4, axis=1)
    c["biasvalid"] = np.repeat(((1.0 - valid) * NEG)[:, None], 4, axis=1).astype(np.float32)
    c["ch0col"] = np.repeat((i < 64).astype(np.float32)[:, None], 4, axis=1)
    c["ch1col"] = np.repeat((i >= 64).astype(np.float32)[:, None], 4, axis=1)
    t512 = np.arange(512)
    c["chm"] = np.broadcast_to((t512 % 64 != 0).astype(np.float32)[None, :], (128, 512)).copy()
    names32 = ["ident", "ones", "uincl", "strict", "causal", "maskt",
               "validcol", "biasvalid", "ch0col", "ch1col", "chm"]
    off32 = {}
    cols = 0
    for n in names32:
        off32[n] = (cols, c[n].shape[1])
        cols += c[n].shape[1]
    p32 = np.concatenate([c[n] for n in names32], axis=1).astype(np.float32)
    b = {}
    b["ident"] = c["ident"]
    b["ones"] = c["ones"]
    b["ones0"] = c["ones"] * valid[:, None]
    negl = -(i[:, None] >= i[None, :]).astype(np.float32)
    b["negl"] = negl
    b["negl0"] = negl * valid[:, None]
    t = np.arange(512)
    for k in range(4):
        m01 = ((128 * k + i[:, None]) < t[None, :]).astype(np.float32)
        b[f"m01_{k}"] = m01
        b[f"mneg_{k}"] = (1.0 - m01) * NEG
    namesb = ["ident", "ones", "ones0", "negl", "negl0"] + [f"m01_{k}" for k in range(4)] + \
             [f"mneg_{k}" for k in range(4)]
    offb = {}
    cols = 0
    for n in namesb:
        offb[n] = (cols, b[n].shape[1])
        cols += b[n].shape[1]
    pb = np.concatenate([b[n] for n in namesb], axis=1).astype(ml_dtypes.bfloat16)
    return p32, off32, pb, offb


def _param_pack(depth, norm_w, sb_q_norm, sb_k_norm, gdn_conv_w, gdn_a_log, gdn_dt_bias,
                gdn_out_norm, hgrn_lb_logits, hgrn_out_norm):
    segs = {}
    segs["normw"] = norm_w.reshape(depth, 8, 128).transpose(2, 0, 1).reshape(128, depth * 8)
    segs["sbqn"] = sb_q_norm.T
    segs["sbkn"] = sb_k_norm.T
    segs["gdon"] = gdn_out_norm.T
    segs["hgon"] = hgrn_out_norm.T
    segs["convw"] = gdn_conv_w.reshape(depth, 4, 12, 128).transpose(3, 0, 2, 1).reshape(128, depth * 48)
    segs["lbl"] = hgrn_lb_logits.reshape(depth, 4, 128).transpose(2, 1, 0).reshape(128, 4 * depth)
    segs["alog"] = np.broadcast_to(gdn_a_log.reshape(1, depth * 4), (128, depth * 4))
    segs["dtb"] = np.broadcast_to(gdn_dt_bias.reshape(1, depth * 4), (128, depth * 4))
    off = {}
    cols = 0
    parts = []
    for n, a in segs.items():
        a = np.ascontiguousarray(a, dtype=np.float32)
        off[n] = (cols, a.shape[1])
        cols += a.shape[1]
        parts.append(a)
    return np.concatenate(parts, axis=1), off


class _Stop(Exception):
    pass


def build(nblk, depth, debug=False, stop=None):
    T = nblk * 128
    TR = T - 128
    groups = [(0, 1)]
    b0 = 1
    while b0 < nblk:
        nb = min(4, nblk - b0)
        groups.append((b0, nb))
        b0 += nb
    p32, off32, pb, offb = _const_packs()
    npar = depth * 8 + 4 * depth + depth * 48 + 4 * depth + 8 * depth
    nc = bass.Bass("TRN2", target_bir_lowering=False)
    es = ExitStack()
    dr = {}
    dr["x"] = nc.dram_tensor("x", [TR, D], F32, kind="ExternalInput")
    dr["meta"] = nc.dram_tensor("meta", [16, D], F32, kind="ExternalInput")
    dr["w_in"] = nc.dram_tensor("w_in", [depth, D, NIN], F32, kind="ExternalInput")
    dr["w_branch"] = nc.dram_tensor("w_branch", [depth, 3, 512, D], F32, kind="ExternalInput")
    dr["w_out"] = nc.dram_tensor("w_out", [depth, D, D], F32, kind="ExternalInput")
    dr["c32"] = nc.dram_tensor("c32", list(p32.shape), F32, kind="ExternalInput")
    dr["cbf"] = nc.dram_tensor("cbf", list(pb.shape), BF16, kind="ExternalInput")
    dr["par"] = nc.dram_tensor("par", [128, npar], F32, kind="ExternalInput")
    dr["nwrow"] = nc.dram_tensor("nwrow", [depth, 128, D], F32, kind="ExternalInput")
    dr["y"] = nc.dram_tensor("y", [TR, D], F32, kind="ExternalOutput")
    hbuf = nc.dram_tensor("hbuf", [T, D], F32, kind="Internal")
    kts = nc.dram_tensor("kts", [4, 128, T], BF16, kind="Internal")
    vs = nc.dram_tensor("vs", [4, 128, nblk, 128], BF16, kind="Internal")
    wsc = nc.dram_tensor("wsc", [depth, NTILES, 128, 4096], BF16, kind="Internal")
    dbg = {}

    P = Prog(nc, es)
    sb_count = [0]

    def SB(shape, dt, name=None):
        sb_count[0] += 1
        return nc.alloc_sbuf_tensor(name or f"t{sb_count[0]}", shape, dt)

    C32 = SB(list(p32.shape), F32, "C32")
    CBF = SB(list(pb.shape), BF16, "CBF")
    PAR = SB([128, npar], F32, "PAR")
    P.dma("sp", lambda e: e.dma_start(out=C32[:, :], in_=dr["c32"].ap()), writes=["C32"])
    P.dma("sp", lambda e: e.dma_start(out=CBF[:, :], in_=dr["cbf"].ap()), writes=["CBF"])
    P.dma("sp", lambda e: e.dma_start(out=PAR[:, :], in_=dr["par"].ap()), writes=["PAR"])

    def c32(n, lo=0, hi=None):
        o, w = off32[n]
        hi = w if hi is None else hi
        return C32[:, o + lo:o + hi]

    def cbf(n, lo=0, hi=None):
        o, w = offb[n]
        hi = w if hi is None else hi
        return CBF[:, o + lo:o + hi]

    poff = {}
    cols = 0
    for n, w in (("normw", depth * 8), ("sbqn", depth), ("sbkn", depth), ("gdon", depth), ("hgon", depth),
                 ("convw", depth * 48), ("lbl", 4 * depth), ("alog", 4 * depth), ("dtb", 4 * depth)):
        poff[n] = cols
        cols += w
    assert cols == npar

    def par(n, idx, w=1):
        return PAR[:, poff[n] + idx:poff[n] + idx + w]

    PS = [nc.alloc_psum_tensor(f"ps{i}", [128, 512], F32) for i in range(6)]
    ps_rr = [0]

    def ps():
        i = ps_rr[0]
        ps_rr[0] = (i + 1) % 4
        return PS[i], f"ps{i}"

    with nc.sbuf_tensor("stg0", [128, 4096], F32) as stg0, nc.sbuf_tensor("stg1", [128, 4096], F32) as stg1, \
            nc.sbuf_tensor("cv0", [128, 4096], BF16) as cv0, nc.sbuf_tensor("cv1", [128, 4096], BF16) as cv1:
        stg = [stg0, stg1]
        cv = [cv0, cv1]
        n = 0
        for l in range(depth):
            for ti in range(NTILES):
                s = stg[n % 2]
                cvt = cv[n % 2]
                sk, ck = f"stg{n % 2}", f"cv{n % 2}"
                if ti < 19:
                    c0, w = WT_SPECS[ti]
                    src = dr["w_in"].ap()[l].rearrange("(kc p) n -> p kc n", p=128)[:, :, c0:c0 + w]
                    dst = s[:, :].rearrange("p (kc n) -> p kc n", kc=8)[:, :, 0:w]
                    wid = 4096 if w == 512 else None
                elif ti < 22:
                    src = dr["w_branch"].ap()[l, ti - 19].rearrange("(h p) c -> p h c", p=128)
                    dst = s[:, :].rearrange("p (h c) -> p h c", h=4)
                    wid = 4096
                else:
                    e0 = (ti - 22) * 512
                    src = dr["w_out"].ap()[l].rearrange("(c p) e -> p c e", p=128)[:, :, e0:e0 + 512]
                    dst = s[:, :].rearrange("p (c e) -> p c e", c=8)
                    wid = 4096
                P.dma("sp", (lambda e, dst=dst, src=src: e.dma_start(out=dst, in_=src)), writes=[sk])
                if ti == 8:
                    sv = s[:, :].rearrange("p (kc n) -> p kc n", kc=8)[:, :, 0:8]
                    dv = cvt[:, :].rearrange("p (kc n) -> p kc n", kc=8)[:, :, 0:8]
                    P.op("dve", (lambda e, dv=dv, sv=sv: e.tensor_copy(out=dv, in_=sv)), reads=[sk], writes=[ck])
                elif n % 2 == 0:
                    P.op("dve", (lambda e, cvt=cvt, s=s: e.tensor_copy(out=cvt[:, :], in_=s[:, :])),
                         reads=[sk], writes=[ck])
                else:
                    P.op("act", (lambda e, cvt=cvt, s=s: e.activation(out=cvt[:, :], in_=s[:, :], func=AF.Copy)),
                         reads=[sk], writes=[ck])
                P.dma("sp", (lambda e, cvt=cvt, l=l, ti=ti: e.dma_start(out=wsc.ap()[l, ti], in_=cvt[:, :])),
                      reads=[ck], writes=[("wsc", l, ti)])
                n += 1
        P.barrier()

    WR = [SB([128, 4096], BF16, f"wr{i}") for i in range(4)]
    wr_rr = [0]

    ngroups = len(groups)
    worder = []
    for l_ in range(depth):
        for g_ in range(ngroups):
            worder += [(l_, t_) for t_ in range(13)]
            for b_ in range(3):
                worder += [(l_, T_WB + b_), (l_, 13 + 2 * b_), (l_, 14 + 2 * b_)]
            worder += [(l_, T_WO), (l_, T_WO + 1)]
    wstate = {"next_use": 0, "next_load": 0, "slots": {}}

    def _issue_load():
        i = wstate["next_load"]
        if i >= len(worder):
            return
        l_, ti_ = worder[i]
        slot = i % len(WR)
        t = WR[slot]
        P.dma("sp", (lambda e: e.dma_start(out=t[:, :], in_=wsc.ap()[l_, ti_])), reads=[("wsc", l_, ti_)],
              writes=[f"wr{slot}"])
        wstate["next_load"] = i + 1

    def load_w(l, ti):
        i = wstate["next_use"]
        assert worder[i] == (l, ti), (worder[i], l, ti)
        while wstate["next_load"] <= min(i + 1, len(worder) - 1):
            _issue_load()
        wstate["next_use"] = i + 1
        slot = i % len(WR)
        return WR[slot], f"wr{slot}"

    HB = [SB([128, D], F32, f"hb{i}") for i in range(2)]
    XS = SB([128, D], BF16, "xs")
    NWROW = SB([128, D], F32, "nwrow_sb")
    JUNK = SB([128, D], BF16, "junk")
    SM = SB([128, 64], F32, "sm")
    XNT = SB([128, 8, 512], BF16, "xnt")
    A32 = [SB([128, 512], F32, f"a32_{i}") for i in range(8)]
    A16 = [SB([128, 512], BF16, f"a16_{i}") for i in range(8)]
    B32 = [SB([128, 128], F32, f"b32_{i}") for i in range(20)]
    C32R = [SB([128, 128], F32, f"c32r_{i}") for i in range(14)]
    B16 = [SB([128, 128], BF16, f"b16_{i}") for i in range(16)]
    C16 = [SB([128, 128], BF16, f"c16_{i}") for i in range(10)]
    rr = {"a32": 0, "a16": 0, "b32": 0, "b16": 0, "c16": 0, "c32r": 0}

    def ring(kind):
        lst = {"a32": A32, "a16": A16, "b32": B32, "b16": B16, "c16": C16, "c32r": C32R}[kind]
        i = rr[kind]
        rr[kind] = (i + 1) % len(lst)
        return lst[i], f"{kind}_{i}"

    GA = [SB([128, 512], BF16, f"ga{h}") for h in range(4)]
    GBt = [SB([128, 512], BF16, f"gb{h}") for h in range(4)]
    GC = [SB([128, 512], BF16, f"gc{h}") for h in range(4)]
    TOK = SB([128, 4, 512], BF16, "tok")
    SZ4 = [SB([128, 512], BF16, f"sz{h}") for h in range(4)]
    OB = [SB([128, 512], BF16, f"ob{i}") for i in range(12)]
    KSEG = [SB([128, 1024], BF16, f"kseg{i}") for i in range(2)]
    VSEG = [SB([128, 8, 128], BF16, f"vseg{i}") for i in range(2)]
    RSB2 = [SB([128, 512], F32, f"rsb{i}") for i in range(2)]
    CONVB = [SB([128, 515], F32, f"convb{i}") for i in range(2)]
    TAIL = SB([128, 12, 4], F32, "tail")
    BD = SB([128, 4, 8], F32, "bd")
    SG32 = [SB([128, 128], F32, f"sg32_{h}") for h in range(4)]
    SG16 = [SB([128, 128], BF16, f"sg16_{h}") for h in range(4)]
    SH32 = [SB([128, 128], F32, f"sh32_{h}") for h in range(4)]
    SH16 = [SB([128, 128], BF16, f"sh16_{h}") for h in range(4)]
    HK = [SB([128, 512], F32, f"hk{h}") for h in range(4)]
    HG = [SB([128, 512], F32, f"hgl{h}") for h in range(4)]
    GST = SB([128, 32], F32, "gst")
    QD = [[SB([128, 128], BF16, f"qd{i}_{c}") for c in range(2)] for i in range(2)]
    ZERO16 = SB([128, 128], BF16, "zero16")
    YT = [SB([128, 512], BF16, f"yt{c}") for c in range(8)]
    LB = SB([128, 4, 2 * depth], F32, "lb")
    NEGA = SB([128, 4 * depth], F32, "nega")
    PST = [nc.alloc_psum_tensor(f"pst{i}", [128, 1024], BF16) for i in range(2)]
    pst_rr = [0]

    def pst():
        i = pst_rr[0]
        pst_rr[0] = (i + 1) % 8
        return PST[i % 2][:, (i // 2) * 128:(i // 2 + 1) * 128], f"pst{i % 2}"

    def dve(fn, reads, writes):
        P.op("dve", fn, reads, writes)

    def act(fn, reads, writes):
        P.op("act", fn, reads, writes)

    def pool(fn, reads, writes):
        P.op("pool", fn, reads, writes)

    def pe(fn, reads, writes):
        P.op("pe", fn, reads, writes)

    def mm(out, lhsT, rhs, start, stop, reads, writes):
        pe(lambda e: e.matmul(out, lhsT, rhs, start=start, stop=stop), reads, writes)

    def rstd_from(out_ap, in_ap, scale, n_free, reads, writes, tmpk):
        act(lambda e: e.activation(out=out_ap, in_=in_ap, func=AF.Ln, scale=scale, bias=EPSB[:, 0:1]),
            reads + ["epsb"], writes)
        act(lambda e: e.activation(out=out_ap, in_=out_ap, func=AF.Exp, scale=-0.5), writes, writes)

    EPSB = SB([128, 2], F32, "epsb")
    dve(lambda e: e.memset(EPSB[:, 0:1], EPS), [], ["epsb"])
    dve(lambda e: e.memset(EPSB[:, 1:2], 1.0), [], ["epsb"])
    dve(lambda e: e.memset(ZERO16[:, :], 0.0), [], ["zero16"])
    for i_ in range(2):
        for c_ in range(2):
            dve(lambda e, i_=i_, c_=c_: e.memset(QD[i_][c_][:, :], 0.0), [], [f"qd{i_}_{c_}"])
    dve(lambda e: e.memset(TAIL[:, :, :], 0.0), [], [("tail", c_) for c_ in range(12)])

    for ch in range(4):
        lg = par("lbl", ch * depth, depth)
        mx = SM[:, 0:1]
        dve(lambda e, lg=lg, mx=mx: e.tensor_reduce(out=mx, in_=lg, axis=AX.X, op=ALU.max), ["PAR"], ["sm"])
        dve(lambda e, mx=mx: e.tensor_scalar(out=SM[:, 1:2], in0=mx, scalar1=-1.0, scalar2=None, op0=ALU.mult),
            ["sm"], ["sm"])
        ex = SM[:, 8:8 + depth]
        act(lambda e, lg=lg, ex=ex: e.activation(out=ex, in_=lg, func=AF.Exp, bias=SM[:, 1:2], scale=1.0),
            ["PAR", "sm"], ["sm"])
        dve(lambda e, ex=ex: e.tensor_reduce(out=SM[:, 2:3], in_=ex, axis=AX.X, op=ALU.add), ["sm"], ["sm"])
        dve(lambda e: e.reciprocal(out=SM[:, 3:4], in_=SM[:, 2:3]), ["sm"], ["sm"])
        pr = SM[:, 24:24 + depth]
        dve(lambda e, ex=ex, pr=pr: e.tensor_scalar(out=pr, in0=ex, scalar1=SM[:, 3:4], scalar2=None, op0=ALU.mult),
            ["sm"], ["sm"])
        dve(lambda e, ch=ch: e.memset(LB[:, ch, 0:1], 0.0), [], ["lb"])
        for l in range(1, depth):
            dve(lambda e, ch=ch, l=l: e.tensor_tensor(out=LB[:, ch, l:l + 1], in0=LB[:, ch, l - 1:l],
                                                      in1=SM[:, 24 + l:25 + l], op=ALU.add), ["lb", "sm"], ["lb"])
        dve(lambda e, ch=ch: e.tensor_scalar(out=LB[:, ch, depth:2 * depth], in0=LB[:, ch, 0:depth],
                                             scalar1=-1.0, scalar2=1.0, op0=ALU.mult, op1=ALU.add), ["lb"], ["lb"])
    act(lambda e: e.activation(out=NEGA[:, :], in_=par("alog", 0, 4 * depth), func=AF.Exp), ["PAR"], ["nega"])
    dve(lambda e: e.tensor_scalar(out=NEGA[:, :], in0=NEGA[:, :], scalar1=-1.0, scalar2=None, op0=ALU.mult),
        ["nega"], ["nega"])

    try:
        if stop == "pre":
            raise _Stop()
        for l in range(depth):
            first = (l == 0)
            last = (l == depth - 1)
            for h in range(4):
                dve(lambda e, h=h: e.memset(SG32[h][:, :], 0.0), [], [f"sg32_{h}"])
                dve(lambda e, h=h: e.memset(SG16[h][:, :], 0.0), [], [f"sg16_{h}"])
                dve(lambda e, h=h: e.memset(SH32[h][:, :], 0.0), [], [f"sh32_{h}"])
                dve(lambda e, h=h: e.memset(SH16[h][:, :], 0.0), [], [f"sh16_{h}"])
            dve(lambda e: e.memset(TAIL[:, :, :], 0.0), [], [("tail", c_) for c_ in range(12)])
            P.dma("sp", lambda e: e.dma_start(out=NWROW[:, :], in_=dr["nwrow"].ap()[l]), writes=["nwrow"])

            for gi, (gb0, nb) in enumerate(groups):
                N = nb * 128
                t0 = gb0 * 128

                def h_src(blk):
                    tg = (gb0 + blk) * 128
                    if first:
                        return None if gb0 + blk == 0 else dr["x"].ap()[tg - 128:tg, :]
                    return hbuf.ap()[tg:tg + 128, :]

                def load_hblk(blk, hb, hbk):
                    tg = (gb0 + blk) * 128
                    if first and gb0 + blk == 0:
                        dve(lambda e: e.memset(hb[:, :], 0.0), [], [hbk])
                        P.dma("sp", lambda e: e.dma_start(out=hb[112:128, :], in_=dr["meta"].ap()), writes=[hbk])
                    else:
                        src = h_src(blk)
                        rk = [("h", gb0 + blk)] if not first else []
                        P.dma("sp", lambda e: e.dma_start(out=hb[:, :], in_=src), reads=rk, writes=[hbk])

                for blk in range(nb):
                    hb = HB[blk % 2]
                    hbk = f"hb{blk % 2}"
                    load_hblk(blk, hb, hbk)
                    act(lambda e, hb=hb: e.activation(out=JUNK[:, :], in_=hb[:, :], func=AF.Square,
                                                       accum_out=SM[:, 40:41]), [hbk], ["junk", "sm40"])
                    rstd_from(SM[:, 41:42], SM[:, 40:41], 1.0 / D, 1, ["sm40"], ["sm41"], None)
                    dve(lambda e, hb=hb: e.scalar_tensor_tensor(out=XS[:, :], in0=hb[:, :], scalar=SM[:, 41:42],
                                                                 in1=NWROW[:, :], op0=ALU.mult, op1=ALU.mult),
                        [hbk, "sm41", "nwrow"], ["xs"])
                    for half in range(2):
                        pbank = PST[half]
                        pk = f"pst{half}"
                        for q in range(4):
                            kc = half * 4 + q
                            pe(lambda e, pbank=pbank, q=q, kc=kc: e.transpose(
                                pbank[:, q * 128:(q + 1) * 128], XS[:, kc * 128:(kc + 1) * 128], cbf("ident")),
                               ["xs", "CBF"], [pk])
                        o = XNT[:, half * 4:half * 4 + 4, blk * 128:(blk + 1) * 128]
                        src = pbank[:, 0:512].rearrange("p (q t) -> p q t", q=4)
                        if half == 0:
                            act(lambda e, o=o, src=src: e.activation(out=o, in_=src, func=AF.Copy), [pk], [("xnt", blk)])
                        else:
                            dve(lambda e, o=o, src=src: e.tensor_copy(out=o, in_=src), [pk], [("xnt", blk)])
                xk = [("xnt", b) for b in range(nb)]

                def proj_fm(wt, wk, f):
                    bank, bk = ps()
                    for kc in range(8):
                        mm(bank[:, :N], wt[:, kc * 512 + f * 128:kc * 512 + (f + 1) * 128], XNT[:, kc, :N],
                           kc == 0, kc == 7, [wk] + xk, [bk])
                    return bank, bk

                def proj_tm(wt, wk, blk, w):
                    bank, bk = ps()
                    for kc in range(8):
                        mm(bank[:, :w], XNT[:, kc, blk * 128:(blk + 1) * 128], wt[:, kc * 512:kc * 512 + w],
                           kc == 0, kc == 7, [wk, ("xnt", blk)], [bk])
                    return bank, bk

                def silu_gate(wt, wk, f, dst, dk):
                    bank, bk = proj_fm(wt, wk, f)
                    tmp, tk = ring("a32")
                    act(lambda e: e.activation(out=tmp[:, :N], in_=bank[:, :N], func=AF.Sigmoid), [bk], [tk])
                    dve(lambda e: e.tensor_tensor(out=dst[:, :N], in0=bank[:, :N], in1=tmp[:, :N], op=ALU.mult),
                        [bk, tk], [dk])

                def headnorm_fm(bank, bk, wcol, dst, dk, extra_scale):
                    sq, sqk = ring("a32")
                    act(lambda e: e.activation(out=sq[:, :N], in_=bank[:, :N], func=AF.Square), [bk], [sqk])
                    b2, b2k = ps()
                    mm(b2[:, :N], c32("ones"), sq[:, :N], True, True, [sqk, "C32"], [b2k])
                    rs, rsk = ring("a32")
                    rstd_from(rs[:, :N], b2[:, :N], 1.0 / 128.0, N, [b2k], [rsk], None)
                    if extra_scale != 1.0:
                        dve(lambda e: e.tensor_scalar(out=rs[:, :N], in0=rs[:, :N], scalar1=wcol, scalar2=extra_scale,
                                                      op0=ALU.mult, op1=ALU.mult), [rsk, "PAR"], [rsk])
                    else:
                        dve(lambda e: e.tensor_scalar(out=rs[:, :N], in0=rs[:, :N], scalar1=wcol, scalar2=None,
                                                      op0=ALU.mult), [rsk, "PAR"], [rsk])
                    dve(lambda e: e.tensor_tensor(out=dst[:, :N], in0=bank[:, :N], in1=rs[:, :N], op=ALU.mult),
                        [bk, rsk], [dk])

                if stop == "rms" or stop == f"rms@{gi}":
                    raise _Stop()
                wt, wk = load_w(l, 0)
                for h in range(4):
                    bank, bk = proj_fm(wt, wk, h)
                    headnorm_fm(bank, bk, par("sbqn", l), GA[h], f"ga{h}", 128.0 ** -0.5)
                wt, wk = load_w(l, 1)
                for h in range(4):
                    bank, bk = proj_fm(wt, wk, h)
                    headnorm_fm(bank, bk, par("sbkn", l), GBt[h], f"gb{h}", 1.0)
                    P.dma("sp", (lambda e, h=h: e.dma_start(out=kts.ap()[h, :, t0:t0 + N], in_=GBt[h][:, :N])),
                          reads=[f"gb{h}"], writes=[("kts", h, gi)])
                wt, wk = load_w(l, 2)
                for blk in range(nb):
                    bank, bk = proj_tm(wt, wk, blk, 512)
                    act(lambda e, bank=bank, blk=blk: e.activation(out=TOK[:, blk, :], in_=bank[:, :], func=AF.Copy),
                        [bk], [("tok", blk)])
                for h in range(4):
                    P.dma("sp", (lambda e, h=h: e.dma_start(out=vs.ap()[h, :, gb0:gb0 + nb, :],
                                                           in_=TOK[:, 0:nb, h * 128:(h + 1) * 128])),
                          reads=[("tok", b) for b in range(nb)], writes=[("vs", h, gi)])
                wt, wk = load_w(l, 3)
                for h in range(4):
                    silu_gate(wt, wk, h, SZ4[h], f"sz{h}")

                if stop == "sbproj" or stop == f"sbproj@{gi}":
                    raise _Stop()
                nkb = gb0 + nb
                nseg = (nkb + 7) // 8
                for hp in (0, 2):
                    heads = (hp, hp + 1)
                    firststep = True
                    for sgi in range(nseg - 1, -1, -1):
                        kb_lo = sgi * 8
                        kb_hi = min(nkb, kb_lo + 8)
                        gread = [gg for gg, (g0, gn) in enumerate(groups) if g0 < kb_hi and g0 + gn > kb_lo and gg <= gi]
                        for h in heads:
                            si = h % 2
                            ks, vsg = KSEG[si], VSEG[si]
                            P.dma("sp", (lambda e, ks=ks, kb_lo=kb_lo, kb_hi=kb_hi, h=h: e.dma_start(
                                out=ks[:, 0:(kb_hi - kb_lo) * 128], in_=kts.ap()[h, :, kb_lo * 128:kb_hi * 128])),
                                reads=[("kts", h, gg) for gg in gread], writes=[f"kseg{si}"])
                            P.dma("sp", (lambda e, vsg=vsg, kb_lo=kb_lo, kb_hi=kb_hi, h=h: e.dma_start(
                                out=vsg[:, 0:kb_hi - kb_lo, :], in_=vs.ap()[h, :, kb_lo:kb_hi, :])),
                                reads=[("vs", h, gg) for gg in gread], writes=[f"vseg{si}"])
                        for kb in range(kb_hi - 1, kb_lo - 1, -1):
                            j = kb - kb_lo
                            di = kb - gb0
                            isdiag = di >= 0
                            isb0 = (kb == 0)
                            laststep = (kb == 0)
                            cx = {}
                            for h in heads:
                                si = h % 2
                                ks = KSEG[si]
                                Z1, z1k = PS[2 * si], f"ps{2 * si}"
                                Zb, zk = PS[2 * si + 1], f"ps{2 * si + 1}"
                                mm(Z1[:, :N], ks[:, j * 128:(j + 1) * 128], GA[h][:, :N], True, True,
                                   [f"kseg{si}", f"ga{h}"], [z1k])
                                mm(Zb[:, :N], ks[:, j * 128:(j + 1) * 128], GA[h][:, :N], True, False,
                                   [f"kseg{si}", f"ga{h}"], [zk])
                                cx[h] = dict(Z1=Z1, z1k=z1k, Zb=Zb, zk=zk)
                            for h in heads:
                                c_ = cx[h]
                                E, ek = ring("a32")
                                act(lambda e, E=E, Z1=c_["Z1"]: e.activation(out=E[:, :N], in_=Z1[:, :N], func=AF.Exp),
                                    [c_["z1k"]], [ek])
                                SPb, spk = ring("a16")
                                act(lambda e, E=E, SPb=SPb: e.activation(out=SPb[:, :N], in_=E[:, :N], func=AF.Ln,
                                                                         bias=EPSB[:, 1:2], scale=1.0),
                                    [ek, "epsb"], [spk])
                                if isdiag:
                                    dve(lambda e, SPb=SPb, di=di: e.tensor_tensor(
                                        out=SPb[:, :N], in0=SPb[:, :N], in1=cbf(f"m01_{di}", 0, N), op=ALU.mult),
                                        [spk, "CBF"], [spk])
                                c_.update(SPb=SPb, spk=spk)
                            for h in heads:
                                c_ = cx[h]
                                Zb, zk, SPb, spk = c_["Zb"], c_["zk"], c_["SPb"], c_["spk"]
                                mm(Zb[:, :N], cbf("negl0") if isb0 else cbf("negl"), SPb[:, :N], False, not isdiag,
                                   [spk, "CBF"], [zk])
                                if isdiag:
                                    mm(Zb[:, :N], cbf("ident"), cbf(f"mneg_{di}", 0, N), False, True, ["CBF"], [zk])
                                if not laststep:
                                    Cb, ck = c_["Z1"], c_["z1k"]
                                    mm(Cb[:, :N], cbf("ones0") if isb0 else cbf("ones"), SPb[:, :N], True, True,
                                       [spk, "CBF"], [ck])
                                    c_.update(Cb=Cb, ck=ck)
                            for h in heads:
                                c_ = cx[h]
                                si = h % 2
                                RS, rsk_ = RSB2[si], f"rsb{si}"
                                Zb, zk = c_["Zb"], c_["zk"]
                                AT, atk = ring("a16")
                                bias_ap = c32("biasvalid", 0, 1) if isb0 else None
                                if firststep:
                                    src, srck = Zb, zk
                                else:
                                    T1, t1k = ring("a32")
                                    dve(lambda e, T1=T1, Zb=Zb, RS=RS: e.tensor_tensor(
                                        out=T1[:, :N], in0=Zb[:, :N], in1=RS[:, :N], op=ALU.subtract),
                                        [zk, rsk_], [t1k])
                                    src, srck = T1, t1k
                                if bias_ap is not None:
                                    act(lambda e, AT=AT, src=src, bias_ap=bias_ap: e.activation(
                                        out=AT[:, :N], in_=src[:, :N], func=AF.Exp, bias=bias_ap, scale=1.0),
                                        [srck, "C32"], [atk])
                                else:
                                    act(lambda e, AT=AT, src=src: e.activation(out=AT[:, :N], in_=src[:, :N],
                                                                               func=AF.Exp), [srck], [atk])
                                if not laststep:
                                    Cb, ck = c_["Cb"], c_["ck"]
                                    if firststep:
                                        dve(lambda e, Cb=Cb, RS=RS: e.tensor_copy(out=RS[:, :N], in_=Cb[:, :N]),
                                            [ck], [rsk_])
                                    else:
                                        dve(lambda e, Cb=Cb, RS=RS: e.tensor_tensor(
                                            out=RS[:, :N], in0=RS[:, :N], in1=Cb[:, :N], op=ALU.add),
                                            [ck, rsk_], [rsk_])
                                c_.update(AT=AT, atk=atk)
                            for h in heads:
                                c_ = cx[h]
                                si = h % 2
                                OT, otk = PS[4 + si], f"ps{4 + si}"
                                mm(OT[:, :N], VSEG[si][:, j, :], c_["AT"][:, :N], firststep, laststep,
                                   [f"vseg{si}", c_["atk"]], [otk])
                            firststep = False
                    for h in heads:
                        si = h % 2
                        OT, otk = PS[4 + si], f"ps{4 + si}"
                        dve(lambda e, OT=OT, h=h: e.tensor_tensor(out=OB[h][:, :N], in0=OT[:, :N], in1=SZ4[h][:, :N],
                                                                op=ALU.mult), [otk, f"sz{h}"], [f"ob{h}"])

                if stop == "sb" or stop == f"sb@{gi}":
                    raise _Stop()
                for ti in (4, 5, 6):
                    wt, wk = load_w(l, ti)
                    for f in range(4):
                        cidx = (ti - 4) * 4 + f
                        bank, bk = proj_fm(wt, wk, f)
                        cb = CONVB[cidx % 2]
                        cbk = f"convb{cidx % 2}"
                        act(lambda e, cb=cb, bank=bank: e.activation(out=cb[:, 3:3 + N], in_=bank[:, :N], func=AF.Copy),
                            [bk], [cbk])
                        dve(lambda e, cb=cb, cidx=cidx: e.tensor_copy(out=cb[:, 0:3], in_=TAIL[:, cidx, 0:3]),
                            [("tail", cidx)], [cbk])
                        dve(lambda e, cb=cb, cidx=cidx: e.tensor_copy(out=TAIL[:, cidx, 0:3], in_=cb[:, N:N + 3]),
                             [cbk], [("tail", cidx)])
                        acc, acck = ring("a32")
                        cw = poff["convw"] + l * 48 + cidx * 4
                        dve(lambda e, acc=acc, cb=cb, cw=cw: e.tensor_scalar(out=acc[:, :N], in0=cb[:, 0:N],
                                                                            scalar1=PAR[:, cw:cw + 1], scalar2=None,
                                                                            op0=ALU.mult), [cbk, "PAR"], [acck])
                        for i in (1, 2, 3):
                            dve(lambda e, acc=acc, cb=cb, cw=cw, i=i: e.scalar_tensor_tensor(
                                out=acc[:, :N], in0=cb[:, i:i + N], scalar=PAR[:, cw + i:cw + i + 1], in1=acc[:, :N],
                                op0=ALU.mult, op1=ALU.add), [cbk, "PAR", acck], [acck])
                        sg, sgk = ring("a32")
                        act(lambda e, sg=sg, acc=acc: e.activation(out=sg[:, :N], in_=acc[:, :N], func=AF.Sigmoid),
                            [acck], [sgk])
                        hh = cidx % 4
                        if cidx >= 8:
                            dve(lambda e, acc=acc, sg=sg, hh=hh: e.tensor_tensor(out=GC[hh][:, :N], in0=acc[:, :N],
                                                                               in1=sg[:, :N], op=ALU.mult),
                                [acck, sgk], [f"gc{hh}"])
                        else:
                            dve(lambda e, acc=acc, sg=sg: e.tensor_tensor(out=acc[:, :N], in0=acc[:, :N], in1=sg[:, :N],
                                                                        op=ALU.mult), [acck, sgk], [acck])
                            sq, sqk = ring("a32")
                            dve(lambda e, sq=sq, acc=acc: e.tensor_tensor(out=sq[:, :N], in0=acc[:, :N], in1=acc[:, :N],
                                                                         op=ALU.mult), [acck], [sqk])
                            b2, b2k = ps()
                            mm(b2[:, :N], c32("ones"), sq[:, :N], True, True, [sqk, "C32"], [b2k])
                            rs, rsk = ring("a32")
                            rstd_from(rs[:, :N], b2[:, :N], 1.0, N, [b2k], [rsk], None)
                            dst, dk = (GA[hh], f"ga{hh}") if cidx < 4 else (GBt[hh], f"gb{hh}")
                            sc = 128.0 ** -0.5 if cidx < 4 else 1.0
                            dve(lambda e, dst=dst, acc=acc, rs=rs, sc=sc: e.scalar_tensor_tensor(
                                out=dst[:, :N], in0=acc[:, :N], scalar=sc, in1=rs[:, :N], op0=ALU.mult, op1=ALU.mult),
                                [acck, rsk], [dk])
                wt, wk = load_w(l, 7)
                for h in range(4):
                    silu_gate(wt, wk, h, SZ4[h], f"sz{h}")
                wt, wk = load_w(l, 8)
                for blk in range(nb):
                    bank, bk = proj_tm(wt, wk, blk, 8)
                    act(lambda e, bank=bank, blk=blk: e.activation(out=BD[:, blk, 0:4], in_=bank[:, 0:4], func=AF.Sigmoid),
                        [bk], [("bd", blk)])
                    if gb0 + blk == 0:
                        dve(lambda e, blk=blk: e.tensor_tensor(out=BD[:, blk, 0:4], in0=BD[:, blk, 0:4],
                                                              in1=c32("validcol"), op=ALU.mult),
                            [("bd", blk), "C32"], [("bd", blk)])
                    dve(lambda e, bank=bank, blk=blk: e.tensor_tensor(out=BD[:, blk, 4:8], in0=bank[:, 4:8],
                                                                     in1=par("dtb", l * 4, 4), op=ALU.add),
                        [bk, "PAR"], [("bd", blk)])
                    act(lambda e, blk=blk: e.activation(out=BD[:, blk, 4:8], in_=BD[:, blk, 4:8], func=AF.Exp),
                        [("bd", blk)], [("bd", blk)])
                    act(lambda e, blk=blk: e.activation(out=BD[:, blk, 4:8], in_=BD[:, blk, 4:8], func=AF.Ln,
                                                        bias=EPSB[:, 1:2], scale=1.0), [("bd", blk), "epsb"], [("bd", blk)])
                    dve(lambda e, blk=blk: e.tensor_tensor(out=BD[:, blk, 4:8], in0=BD[:, blk, 4:8],
                                                          in1=NEGA[:, l * 4:l * 4 + 4], op=ALU.mult),
                        [("bd", blk), "nega"], [("bd", blk)])

                if stop == "gdnproj" or stop == f"gdnproj@{gi}":
                    raise _Stop()
                for blk in range(nb):
                    c0, c1 = blk * 128, (blk + 1) * 128
                    bdk = ("bd", blk)
                    gc_b, gck = ps()
                    mm(gc_b[:, 0:4], c32("uincl"), BD[:, blk, 4:8], True, True, ["C32", bdk], [gck])
                    st, stk = GST, "gst"
                    dve(lambda e, st=st, gc_b=gc_b: e.tensor_copy(out=st[:, 0:4], in_=gc_b[:, 0:4]), [gck], [stk])
                    dve(lambda e, st=st, blk=blk: e.tensor_scalar(out=st[:, 4:8], in0=BD[:, blk, 0:4], scalar1=-1.0,
                                                                 scalar2=None, op0=ALU.mult), [bdk], [stk])
                    act(lambda e, st=st: e.activation(out=st[:, 8:12], in_=st[:, 0:4], func=AF.Exp), [stk], [stk])
                    dve(lambda e, st=st, blk=blk: e.tensor_tensor(out=st[:, 8:12], in0=st[:, 8:12], in1=BD[:, blk, 0:4],
                                                                 op=ALU.mult), [stk, bdk], [stk])
                    for h in range(4):
                        ng, ngk = ring("b32")
                        dve(lambda e, ng=ng, blk=blk, h=h: e.tensor_scalar(out=ng[:, :], in0=c32("ones"),
                                                                          scalar1=BD[:, blk, 4 + h:5 + h], scalar2=-1.0,
                                                                          op0=ALU.mult, op1=ALU.mult),
                            ["C32", bdk], [ngk])
                        gb_b, gbk = ps()
                        mm(gb_b[:, 0:128], ng[:, :], c32("uincl"), True, True, [ngk, "C32"], [gbk])
                        dve(lambda e, st=st, gb_b=gb_b, h=h: e.tensor_scalar(
                            out=st[:, 12 + h:13 + h], in0=st[:, h:h + 1], scalar1=gb_b[:, 127:128], scalar2=-1.0,
                            op0=ALU.add, op1=ALU.mult), [stk, gbk], [stk])
                        act(lambda e, st=st, h=h: e.activation(out=st[:, 12 + h:13 + h], in_=st[:, 12 + h:13 + h],
                                                              func=AF.Exp), [stk], [stk])
                        act(lambda e, st=st, gb_b=gb_b, h=h: e.activation(out=st[:, 16 + h:17 + h], in_=gb_b[:, 127:128],
                                                                         func=AF.Exp, scale=-1.0), [gbk], [stk])
                        Dm, dmk = ring("b32")
                        dve(lambda e, Dm=Dm, gb_b=gb_b, st=st, h=h: e.tensor_scalar(
                            out=Dm[:, :], in0=gb_b[:, 0:128], scalar1=st[:, h:h + 1], scalar2=0.0, op0=ALU.add,
                            op1=ALU.min), [gbk, stk], [dmk])
                        act(lambda e, Dm=Dm: e.activation(out=Dm[:, :], in_=Dm[:, :], func=AF.Exp), [dmk], [dmk])
                        Es, esk = ring("b32")
                        Ec, eck = ring("b32")
                        dve(lambda e, Es=Es, Dm=Dm: e.tensor_tensor(out=Es[:, :], in0=Dm[:, :], in1=c32("strict"),
                                                                  op=ALU.mult), [dmk, "C32"], [esk])
                        dve(lambda e, Ec=Ec, Dm=Dm: e.tensor_tensor(out=Ec[:, :], in0=Dm[:, :], in1=c32("causal"),
                                                                  op=ALU.mult), [dmk, "C32"], [eck])
                        EG, egk = ring("b32")
                        act(lambda e, EG=EG, gb_b=gb_b: e.activation(out=EG[:, :], in_=gb_b[:, 0:128], func=AF.Exp,
                                                                     scale=-1.0), [gbk], [egk])
                        qd, qdk = ring("c32r")
                        dve(lambda e, qd=qd, EG=EG, h=h: e.tensor_tensor(out=qd[:, :], in0=GA[h][:, c0:c1], in1=EG[:, :],
                                                                       op=ALU.mult), [f"ga{h}", egk], [qdk])
                        kk_b, kkk = ps()
                        mm(kk_b[:, 0:128], GBt[h][:, c0:c1], GBt[h][:, c0:c1], True, True, [f"gb{h}"], [kkk])
                        PT, ptk = ring("b32")
                        dve(lambda e, PT=PT, kk_b=kk_b, st=st, Es=Es, h=h: e.scalar_tensor_tensor(
                            out=PT[:, :], in0=kk_b[:, 0:128], scalar=st[:, 4 + h:5 + h], in1=Es[:, :], op0=ALU.mult,
                            op1=ALU.mult), [kkk, stk, esk], [ptk])
                        qk_b, qkk = ps()
                        mm(qk_b[:, 0:128], GA[h][:, c0:c1], GBt[h][:, c0:c1], True, True, [f"ga{h}", f"gb{h}"], [qkk])
                        aqk, aqkk = ring("b32")
                        dve(lambda e, aqk=aqk, qk_b=qk_b, Ec=Ec: e.tensor_tensor(out=aqk[:, :], in0=qk_b[:, 0:128],
                                                                               in1=Ec[:, :], op=ALU.mult),
                            [qkk, eck], [aqkk])

                        def tr32(src, srck, kind):
                            tb_, tbk_ = ps()
                            mm(tb_[:, 0:128], src, c32("ident"), True, True, [srck, "C32"], [tbk_])
                            d_, dk_ = ring(kind)
                            act(lambda e: e.activation(out=d_[:, :], in_=tb_[:, 0:128], func=AF.Copy), [tbk_], [dk_])
                            return d_, dk_
                        Pm, pmk = tr32(PT[:, :], ptk, "b32")
                        aqkT, aqkTk = tr32(aqk[:, :], aqkk, "c32r")
                        slot, sk_ = pst()
                        pe(lambda e, slot=slot, h=h: e.transpose(slot, GBt[h][:, c0:c1], cbf("ident")),
                           [f"gb{h}", "CBF"], [sk_])
                        kbg, kbgk = ring("c32r")
                        dve(lambda e, kbg=kbg, slot=slot, st=st, h=h: e.tensor_scalar(
                            out=kbg[:, :], in0=slot, scalar1=st[:, 8 + h:9 + h], scalar2=None, op0=ALU.mult),
                            [sk_, stk], [kbgk])
                        kd, kdk = ring("c32r")
                        dve(lambda e, kd=kd, slot=slot, st=st, h=h: e.tensor_scalar(
                            out=kd[:, :], in0=slot, scalar1=st[:, 12 + h:13 + h], scalar2=None, op0=ALU.mult),
                            [sk_, stk], [kdk])
                        slot2, sk2 = pst()
                        pe(lambda e, slot2=slot2, h=h: e.transpose(slot2, GC[h][:, c0:c1], cbf("ident")),
                           [f"gc{h}", "CBF"], [sk2])
                        vb, vbk = ring("c32r")
                        dve(lambda e, vb=vb, slot2=slot2, blk=blk, h=h: e.tensor_scalar(
                            out=vb[:, :], in0=slot2, scalar1=BD[:, blk, h:h + 1], scalar2=None, op0=ALU.mult),
                            [sk2, bdk], [vbk])
                        R32, r32k = ring("c32r")
                        dve(lambda e, R32=R32, Pm=Pm: e.tensor_tensor(out=R32[:, :], in0=Pm[:, :], in1=c32("ident"),
                                                                    op=ALU.add), [pmk, "C32"], [r32k])
                        Pc, pck, PTc, ptck = Pm, pmk, PT, ptk
                        for lev in range(1, 7):
                            b1, b1k = ps()
                            mm(b1[:, 0:128], Pc[:, :], PTc[:, :], True, True, [pck, ptck], [b1k])
                            PTn, ptnk = ring("b32")
                            act(lambda e, PTn=PTn, b1=b1: e.activation(out=PTn[:, :], in_=b1[:, 0:128], func=AF.Copy),
                                [b1k], [ptnk])
                            if lev < 6:
                                b2, b2k = ps()
                                mm(b2[:, 0:128], PTc[:, :], Pc[:, :], True, True, [pck, ptck], [b2k])
                                Pn, pnk = ring("b32")
                                dve(lambda e, Pn=Pn, b2=b2: e.tensor_copy(out=Pn[:, :], in_=b2[:, 0:128]), [b2k], [pnk])
                            b3, b3k = ps()
                            mm(b3[:, 0:128], PTn[:, :], R32[:, :], True, True, [ptnk, r32k], [b3k])
                            Rn, rnk = ring("c32r")
                            dve(lambda e, Rn=Rn, R32=R32, b3=b3: e.tensor_tensor(out=Rn[:, :], in0=R32[:, :],
                                                                               in1=b3[:, 0:128], op=ALU.add),
                                [r32k, b3k], [rnk])
                            R32, r32k = Rn, rnk
                            PTc, ptck = PTn, ptnk
                            if lev < 6:
                                Pc, pck = Pn, pnk
                        wb_, wbk = ps()
                        mm(wb_[:, 0:128], kbg[:, :], R32[:, :], True, True, [kbgk, r32k], [wbk])
                        nwT, nwTk = ring("b32")
                        dve(lambda e, nwT=nwT, wb_=wb_: e.tensor_scalar(out=nwT[:, :], in0=wb_[:, 0:128], scalar1=-1.0,
                                                                      scalar2=None, op0=ALU.mult), [wbk], [nwTk])
                        vn_b, vnk = ps()
                        mm(vn_b[:, 0:128], R32[:, :], vb[:, :], True, False, [r32k, vbk], [vnk])
                        mm(vn_b[:, 0:128], nwT[:, :], SG32[h][:, :], False, True, [nwTk, f"sg32_{h}"], [vnk])
                        vn, vn16k = ring("b32")
                        act(lambda e, vn=vn, vn_b=vn_b: e.activation(out=vn[:, :], in_=vn_b[:, 0:128], func=AF.Copy),
                            [vnk], [vn16k])
                        o_b, obk = ps()
                        mm(o_b[:, 0:128], qd[:, :], SG32[h][:, :], True, False, [qdk, f"sg32_{h}"], [obk])
                        mm(o_b[:, 0:128], aqkT[:, :], vn[:, :], False, True, [aqkTk, vn16k], [obk])
                        s_b, sbk = ps()
                        mm(s_b[:, 0:128], kd[:, :], vn[:, :], True, True, [kdk, vn16k], [sbk])
                        finish_o(P, o_b, obk, par("gdon", l), OB[4 + h], f"ob{4 + h}", SZ4[h], f"sz{h}", c0, c1,
                                 ring, pst, cbf, act, dve, pe, rstd_from)
                        dve(lambda e, s_b=s_b, st=st, h=h: e.scalar_tensor_tensor(
                            out=SG32[h][:, :], in0=SG32[h][:, :], scalar=st[:, 16 + h:17 + h], in1=s_b[:, 0:128],
                            op0=ALU.mult, op1=ALU.add), [f"sg32_{h}", stk, sbk], [f"sg32_{h}"])

                if stop == "gdn" or stop == f"gdn@{gi}":
                    raise _Stop()
                wt, wk = load_w(l, 9)
                for h in range(4):
                    silu_gate(wt, wk, h, GA[h], f"ga{h}")
                wt, wk = load_w(l, 10)
                for h in range(4):
                    bank, bk = proj_fm(wt, wk, h)
                    sg, sgk = ring("a32")
                    act(lambda e, sg=sg, bank=bank: e.activation(out=sg[:, :N], in_=bank[:, :N], func=AF.Sigmoid),
                        [bk], [sgk])
                    oml = LB[:, h, depth + l:depth + l + 1]
                    lbc = LB[:, h, l:l + 1]
                    nm, nmk = ring("b32")
                    dve(lambda e, nm=nm, oml=oml: e.tensor_scalar(out=nm[:, 0:1], in0=oml, scalar1=-1.0, scalar2=None,
                                                                 op0=ALU.mult), ["lb"], [nmk])
                    dve(lambda e, sg=sg, nm=nm, oml=oml, h=h: e.tensor_scalar(out=HK[h][:, :N], in0=sg[:, :N],
                                                                             scalar1=nm[:, 0:1], scalar2=oml,
                                                                             op0=ALU.mult, op1=ALU.add),
                        [sgk, nmk, "lb"], [f"hk{h}"])
                    dve(lambda e, sg=sg, oml=oml, lbc=lbc: e.tensor_scalar(out=sg[:, :N], in0=sg[:, :N], scalar1=oml,
                                                                          scalar2=lbc, op0=ALU.mult, op1=ALU.add),
                        [sgk, "lb"], [sgk])
                    act(lambda e, sg=sg, h=h: e.activation(out=HG[h][:, :N], in_=sg[:, :N], func=AF.Ln), [sgk],
                        [f"hgl{h}"])
                wt, wk = load_w(l, 11)
                for blk in range(nb):
                    bank, bk = proj_tm(wt, wk, blk, 512)
                    if gb0 + blk == 0:
                        dve(lambda e, bank=bank, blk=blk: e.tensor_scalar(out=TOK[:, blk, :], in0=bank[:, :],
                                                                         scalar1=c32("validcol", 0, 1), scalar2=None,
                                                                         op0=ALU.mult), [bk, "C32"], [("tok", blk)])
                    else:
                        act(lambda e, bank=bank, blk=blk: e.activation(out=TOK[:, blk, :], in_=bank[:, :], func=AF.Copy),
                            [bk], [("tok", blk)])
                wt, wk = load_w(l, 12)
                for h in range(4):
                    silu_gate(wt, wk, h, SZ4[h], f"sz{h}")

                if stop == "hgproj" or stop == f"hgproj@{gi}":
                    raise _Stop()
                nch = N // 64
                for h in range(4):
                    GTt, gtk = ring("a32")
                    dve(lambda e, GTt=GTt, h=h: e.tensor_tensor_scan(out=GTt[:, :N], data0=c32("chm", 0, N),
                                                                    data1=HG[h][:, :N], initial=0.0,
                                                                    op0=ALU.mult, op1=ALU.add),
                        [f"hgl{h}", "C32"], [gtk])
                    if stop == "hgA" or stop == f"hgA@{gi}":
                        raise _Stop()
                    NM, nmk2 = ring("b32")
                    for cc in range(nch):
                        dve(lambda e, NM=NM, GTt=GTt, cc=cc: e.tensor_scalar(
                            out=NM[:, cc:cc + 1], in0=GTt[:, cc * 64 + 31:cc * 64 + 32], scalar1=-1.0, scalar2=None,
                            op0=ALU.mult), [gtk], [nmk2])
                    if stop == "hgB" or stop == f"hgB@{gi}":
                        raise _Stop()
                    E1, e1k = ring("a32")
                    E2, e2k = ring("a32")
                    E3, e3k = ring("a32")
                    E4, e4k = ring("a32")
                    act(lambda e, E3=E3, GTt=GTt: e.activation(out=E3[:, :N], in_=GTt[:, :N], func=AF.Exp), [gtk], [e3k])
                    for cc in range(nch):
                        q0, q1 = cc * 64, (cc + 1) * 64
                        act(lambda e, E1=E1, GTt=GTt, NM=NM, cc=cc, q0=q0, q1=q1: e.activation(
                            out=E1[:, q0:q1], in_=GTt[:, q0:q1], func=AF.Exp, bias=NM[:, cc:cc + 1], scale=1.0),
                            [gtk, nmk2], [e1k])
                        act(lambda e, E2=E2, GTt=GTt, q0=q0, q1=q1: e.activation(
                            out=E2[:, q0:q1], in_=GTt[:, q0:q1], func=AF.Exp, bias=GTt[:, q0 + 31:q0 + 32], scale=-1.0),
                            [gtk], [e2k])
                        act(lambda e, E4=E4, GTt=GTt, q0=q0, q1=q1: e.activation(
                            out=E4[:, q0:q1], in_=GTt[:, q0:q1], func=AF.Exp, bias=GTt[:, q1 - 1:q1], scale=-1.0),
                            [gtk], [e4k])
                    if stop == "hgC" or stop == f"hgC@{gi}":
                        raise _Stop()
                    qtg, qtgk = ring("a16")
                    ktg, ktgk = ring("a16")
                    qdg, qdgk = ring("a16")
                    kdg, kdgk = ring("a16")
                    dve(lambda e, qtg=qtg, E1=E1, h=h: e.tensor_tensor(out=qtg[:, :N], in0=GA[h][:, :N], in1=E1[:, :N],
                                                                     op=ALU.mult), [f"ga{h}", e1k], [qtgk])
                    dve(lambda e, ktg=ktg, E2=E2, h=h: e.tensor_tensor(out=ktg[:, :N], in0=HK[h][:, :N], in1=E2[:, :N],
                                                                     op=ALU.mult), [f"hk{h}", e2k], [ktgk])
                    dve(lambda e, qdg=qdg, E3=E3, h=h: e.tensor_tensor(out=qdg[:, :N], in0=GA[h][:, :N], in1=E3[:, :N],
                                                                     op=ALU.mult), [f"ga{h}", e3k], [qdgk])
                    dve(lambda e, kdg=kdg, E4=E4, h=h: e.tensor_tensor(out=kdg[:, :N], in0=HK[h][:, :N], in1=E4[:, :N],
                                                                     op=ALU.mult), [f"hk{h}", e4k], [kdgk])
                    if stop == "hgD" or stop == f"hgD@{gi}":
                        raise _Stop()
                    for blk in range(nb):
                        c0, c1 = blk * 128, (blk + 1) * 128
                        hvk = ("tok", blk)
                        slot, sk_ = pst()
                        pe(lambda e, slot=slot, kdg=kdg, c0=c0, c1=c1: e.transpose(slot, kdg[:, c0:c1], cbf("ident")),
                           [kdgk, "CBF"], [sk_])
                        if stop == f"hgJ{h}_{blk}@{gi}":
                            raise _Stop()
                        kds = []
                        for c in range(2):
                            kd, kdk = ring("b16")
                            if True:
                                dve(lambda e, kd=kd, slot=slot, c=c: e.tensor_scalar(
                                    out=kd[:, :], in0=slot, scalar1=c32(f"ch{c}col", 0, 1), scalar2=None, op0=ALU.mult),
                                    [sk_, "C32"], [kdk])
                            else:
                                act(lambda e, kd=kd, slot=slot, c=c: e.activation(
                                    out=kd[:, :], in_=slot, func=AF.Copy, scale=c32(f"ch{c}col", 0, 1)),
                                    [sk_, "C32"], [kdk])
                            kds.append((kd, kdk))
                        if stop == f"hgG{h}_{blk}@{gi}":
                            raise _Stop()
                        at_b, atbk = ps()
                        mm(at_b[:, 0:128], ktg[:, c0:c1], qtg[:, c0:c1], True, True, [ktgk, qtgk], [atbk])
                        atc, atck = ring("b32")
                        dve(lambda e, atc=atc, at_b=at_b: e.tensor_scalar(out=atc[:, :], in0=at_b[:, 0:128],
                                                                        scalar1=-1e30, scalar2=1e30, op0=ALU.max,
                                                                        op1=ALU.min), [atbk], [atck])
                        atm, atmk = ring("b16")
                        dve(lambda e, atm=atm, atc=atc: e.tensor_tensor(out=atm[:, :], in0=atc[:, :], in1=c32("maskt"),
                                                                      op=ALU.mult), [atck, "C32"], [atmk])
                        if stop == f"hgH{h}_{blk}@{gi}":
                            raise _Stop()
                        QDp = QD[h % 2]
                        qdpk = [f"qd{h % 2}_0", f"qd{h % 2}_1"]
                        for c in range(2):
                            a0, a1 = c * 64, (c + 1) * 64
                            dve(lambda e, c=c, a0=a0, a1=a1, qdg=qdg, c0=c0, QDp=QDp: e.tensor_copy(
                                out=QDp[c][:, a0:a1], in_=qdg[:, c0 + a0:c0 + a1]), [qdgk], [qdpk[c]])
                        if stop == f"hgI{h}_{blk}@{gi}":
                            raise _Stop()
                        o_b, obk = PS[4 + h % 2], f"ps{4 + h % 2}"
                        for c in range(2):
                            a0, a1 = c * 64, (c + 1) * 64
                            mm(o_b[:, 0:128], QDp[c][:, :], SH16[h][:, :], c == 0, False, [qdpk[c], f"sh16_{h}"], [obk])
                            if c == 0:
                                mm(o_b[:, 0:128], atm[:, :], TOK[:, blk, h * 128:(h + 1) * 128], False, False,
                                   [atmk, hvk], [obk])
                            s_b, sbk = ps()
                            mm(s_b[:, 0:128], kds[c][0][:, :], TOK[:, blk, h * 128:(h + 1) * 128], True, True,
                               [kds[c][1], hvk], [sbk])
                            dve(lambda e, s_b=s_b, E3=E3, col=c0 + a1 - 1, h=h: e.scalar_tensor_tensor(
                                out=SH32[h][:, :], in0=SH32[h][:, :], scalar=E3[:, col:col + 1], in1=s_b[:, 0:128],
                                op0=ALU.mult, op1=ALU.add), [f"sh32_{h}", e3k, sbk], [f"sh32_{h}"])
                            act(lambda e, h=h: e.activation(out=SH16[h][:, :], in_=SH32[h][:, :], func=AF.Copy),
                                [f"sh32_{h}"], [f"sh16_{h}"])
                        mm(o_b[:, 0:128], ZERO16[:, :], SH16[h][:, :], False, True, ["zero16", f"sh16_{h}"], [obk])
                        if stop == f"hgF{h}_{blk}@{gi}":
                            raise _Stop()
                        finish_o(P, o_b, obk, par("hgon", l), OB[8 + h], f"ob{8 + h}", SZ4[h], f"sz{h}", c0, c1,
                                 ring, pst, cbf, act, dve, pe, rstd_from)
                        if stop == f"hgE{h}_{blk}@{gi}":
                            raise _Stop()

                if stop == "hg" or stop == f"hg@{gi}":
                    raise _Stop()
                YACC = HK + HG
                yacck = [f"hk{h}" for h in range(4)] + [f"hgl{h}" for h in range(4)]
                for b in range(3):
                    wbt_, wbk_ = load_w(l, T_WB + b)
                    for half in range(2):
                        mt_, mk_ = load_w(l, 13 + b * 2 + half)
                        for f in range(4):
                            cch = half * 4 + f
                            gbank, gk_ = proj_fm(mt_, mk_, f)
                            gate, gatek = ring("a32")
                            act(lambda e, gate=gate, gbank=gbank: e.activation(out=gate[:, :N], in_=gbank[:, :N],
                                                                              func=AF.Sigmoid), [gk_], [gatek])
                            pb_, pbk = ps()
                            for h in range(4):
                                mm(pb_[:, :N], wbt_[:, h * 1024 + cch * 128:h * 1024 + (cch + 1) * 128],
                                   OB[b * 4 + h][:, :N], h == 0, h == 3, [wbk_, f"ob{b * 4 + h}"], [pbk])
                            ya, yak = YACC[cch], yacck[cch]
                            if b == 0:
                                dve(lambda e, ya=ya, pb_=pb_, gate=gate: e.tensor_tensor(
                                    out=ya[:, :N], in0=pb_[:, :N], in1=gate[:, :N], op=ALU.mult), [pbk, gatek], [yak])
                            else:
                                dve(lambda e, gate=gate, pb_=pb_: e.tensor_tensor(
                                    out=gate[:, :N], in0=pb_[:, :N], in1=gate[:, :N], op=ALU.mult), [pbk, gatek], [gatek])
                                if b == 1:
                                    dve(lambda e, ya=ya, gate=gate: e.tensor_tensor(
                                        out=ya[:, :N], in0=ya[:, :N], in1=gate[:, :N], op=ALU.add), [yak, gatek], [yak])
                                else:
                                    dve(lambda e, ya=ya, gate=gate, cch=cch: e.tensor_tensor(
                                        out=YT[cch][:, :N], in0=ya[:, :N], in1=gate[:, :N], op=ALU.add),
                                        [yak, gatek], [f"yt{cch}"])
                if stop == "merge" or stop == f"merge@{gi}":
                    raise _Stop()
                wot = [load_w(l, T_WO + e_) for e_ in range(2)]
                for blk in range(nb):
                    hb = HB[blk % 2]
                    hbk = f"hb{blk % 2}"
                    load_hblk(blk, hb, hbk)
                    for e_ in range(2):
                        ob_, obk_ = ps()
                        for cch in range(8):
                            mm(ob_[:, :], YT[cch][:, blk * 128:(blk + 1) * 128],
                               wot[e_][0][:, cch * 512:(cch + 1) * 512], cch == 0, cch == 7,
                               [f"yt{cch}", wot[e_][1]], [obk_])
                        dve(lambda e, hb=hb, ob_=ob_, e_=e_: e.tensor_tensor(
                            out=hb[:, e_ * 512:(e_ + 1) * 512], in0=hb[:, e_ * 512:(e_ + 1) * 512], in1=ob_[:, :],
                            op=ALU.add), [hbk, obk_], [hbk])
                    tg = (gb0 + blk) * 128
                    if last:
                        if gb0 + blk > 0:
                            P.dma("sp", (lambda e, hb=hb, tg=tg: e.dma_start(out=dr["y"].ap()[tg - 128:tg, :], in_=hb[:, :])),
                                  reads=[hbk], writes=[("y", gb0 + blk)])
                    else:
                        P.dma("sp", (lambda e, hb=hb, tg=tg: e.dma_start(out=hbuf.ap()[tg:tg + 128, :], in_=hb[:, :])),
                              reads=[hbk], writes=[("h", gb0 + blk)])

    except _Stop:
        pass
    P.barrier()
    P.emit()
    es.close()
    return nc, (p32, pb)


def finish_o(P, o_b, obk, wcol, dst, dk, sz, szk, c0, c1, ring, pst, cbf, act, dve, pe, rstd_from):
    junk, jk = ring("b32")
    st, stk = ring("b32")
    act(lambda e: e.activation(out=junk[:, :], in_=o_b[:, 0:128], func=AF.Square, accum_out=st[:, 0:1]),
        [obk], [jk, stk])
    rstd_from(st[:, 1:2], st[:, 0:1], 1.0 / 128.0, 1, [stk], [stk], None)
    on, onk = ring("b16")
    dve(lambda e: e.tensor_scalar(out=on[:, :], in0=o_b[:, 0:128], scalar1=st[:, 1:2], scalar2=None, op0=ALU.mult),
        [obk, stk], [onk])
    slot, sk_ = pst()
    pe(lambda e: e.transpose(slot, on[:, :], cbf("ident")), [onk, "CBF"], [sk_])
    dve(lambda e: e.scalar_tensor_tensor(out=dst[:, c0:c1], in0=slot, scalar=wcol, in1=sz[:, c0:c1], op0=ALU.mult,
                                         op1=ALU.mult), [sk_, "PAR", szk], [dk])


_CACHE = {}


def kernel(x, meta_tokens, norm_w, w_in, sb_q_norm, sb_k_norm, gdn_conv_w, gdn_a_log, gdn_dt_bias,
           gdn_out_norm, hgrn_lb_logits, hgrn_out_norm, w_branch, w_out):
    x = np.asarray(x, dtype=np.float32)
    bsz, seq, _ = x.shape
    depth = int(np.asarray(norm_w).shape[0])
    nblk = seq // 128 + 1
    key = (nblk, depth)
    if key not in _CACHE:
        _CACHE[key] = build(nblk, depth)
    nc, (p32, pb) = _CACHE[key]
    f = lambda a: np.ascontiguousarray(np.asarray(a, dtype=np.float32))
    parp, _ = _param_pack(depth, f(norm_w), f(sb_q_norm), f(sb_k_norm), f(gdn_conv_w), f(gdn_a_log),
                          f(gdn_dt_bias), f(gdn_out_norm), f(hgrn_lb_logits), f(hgrn_out_norm))
    n_cores = 8
    nwrow = np.ascontiguousarray(np.broadcast_to(f(norm_w)[:, None, :], (depth, 128, D)))
    in_maps = []
    for c in range(n_cores):
        b = c % bsz
        in_maps.append({"x": np.ascontiguousarray(x[b]), "meta": f(meta_tokens), "w_in": f(w_in),
                        "w_branch": f(w_branch), "w_out": f(w_out), "c32": p32, "cbf": pb, "par": parp,
                        "nwrow": nwrow})
    res = run_bass_kernel_spmd(nc, in_maps, core_ids=list(range(n_cores)))
    out = np.stack([np.asarray(res.results[b]["y"], dtype=np.float32) for b in range(bsz)], axis=0)
    return out
```

```python
from contextlib import ExitStack
import numpy as np
import ml_dtypes
import concourse.bass as bass
import concourse.mybir as mybir
from concourse.bass_utils import run_bass_kernel_spmd

F32 = mybir.dt.float32
BF16 = mybir.dt.bfloat16
AF = mybir.ActivationFunctionType
ALU = mybir.AluOpType
AX = mybir.AxisListType

D = 1024
NIN = 9224
NEG = -30000.0
EPS = 1e-6

WT_SPECS = [
    (0, 512), (512, 512), (1024, 512), (1536, 512),
    (2048, 512), (2560, 512), (3072, 512), (3584, 512),
    (4096, 8),
    (4104, 512), (4616, 512), (5128, 512), (5640, 512),
    (6152, 512), (6664, 512), (7176, 512), (7688, 512), (8200, 512), (8712, 512),
]
T_WB = 19
T_WO = 22
NTILES = 24


class _RecCall:
    def __init__(self, name, a, k):
        self.name, self.a, self.k = name, a, k

    def __call__(self, e):
        return getattr(e, self.name)(*self.a, **self.k)


class _Rec:
    def __getattr__(self, name):
        return lambda *a, **k: _RecCall(name, a, k)


_REC = _Rec()


class Prog:
    ENGS = ("pe", "act", "dve", "pool", "sp")

    def __init__(self, nc, es, n_dma_sems=20):
        self.nc = nc
        self.streams = {e: [] for e in self.ENGS}
        self.cnt = {e: 0 for e in self.ENGS}
        self.sem = {e: es.enter_context(nc.semaphore("s_" + e)) for e in ("pe", "act", "dve", "pool")}
        self.dsem = {q: [es.enter_context(nc.semaphore(f"d_{q}{i}")) for i in range(n_dma_sems)]
                     for q in ("sp", "pool")}
        self.dval = {q: [0] * n_dma_sems for q in ("sp", "pool")}
        self.dnext = {q: 0 for q in ("sp", "pool")}
        self.semobj = {}
        for e, s in self.sem.items():
            self.semobj[("c", e)] = s
        for q in self.dsem:
            for i, s in enumerate(self.dsem[q]):
                self.semobj[("d", q, i)] = s
        self.seen = {e: {} for e in self.ENGS}
        self.lastw = {}
        self.readers = {}

    def _deps(self, eng, reads, writes, extra=()):
        need = {}

        def add(tok):
            if tok is None:
                return
            k, v = tok
            if need.get(k, 0) < v:
                need[k] = v
        for r in reads:
            add(self.lastw.get(r))
        for w in writes:
            add(self.lastw.get(w))
            for t in self.readers.get(w, ()):
                add(t)
        for t in extra:
            add(t)
        waits = []
        sn = self.seen[eng]
        for k, v in need.items():
            if eng == "pe" and k == ("c", "pe"):
                continue
            if sn.get(k, 0) >= v:
                continue
            sn[k] = v
            waits.append((k, v))
        return waits

    def _commit(self, tok, reads, writes):
        for w in writes:
            self.lastw[w] = tok
            self.readers[w] = []
        for r in reads:
            if r in writes:
                continue
            self.readers.setdefault(r, []).append(tok)

    def op(self, eng, fn, reads=(), writes=()):
        reads = [r for r in reads if r is not None]
        writes = [w for w in writes if w is not None]
        ex = [r for r in reads if isinstance(r, str) and r.startswith("ps")]
        if ex:
            writes = writes + [r for r in ex if r not in writes]
            reads = [r for r in reads if r not in ex]
        waits = self._deps(eng, reads, writes)
        self.cnt[eng] += 1
        tok = (("c", eng), self.cnt[eng])
        self.streams[eng].append((waits, fn(_REC), ("c", eng), 1))
        self._commit(tok, reads, writes)

    def dma(self, q, fn, reads=(), writes=()):
        i = self.dnext[q]
        self.dnext[q] = (i + 1) % len(self.dsem[q])
        k = ("d", q, i)
        prev = (k, self.dval[q][i]) if self.dval[q][i] > 0 else None
        waits = self._deps(q, list(reads), list(writes), extra=(prev,))
        self.dval[q][i] += 16
        tok = (k, self.dval[q][i])
        self.streams[q].append((waits, fn(_REC), k, 16))
        self._commit(tok, reads, writes)

    def barrier(self):
        toks = [(("c", e), self.cnt[e]) for e in ("pe", "act", "dve", "pool") if self.cnt[e] > 0]
        for q in self.dsem:
            for i, v in enumerate(self.dval[q]):
                if v > 0:
                    toks.append((("d", q, i), v))
        for e in self.ENGS:
            waits = []
            for k, v in toks:
                if self.seen[e].get(k, 0) < v:
                    self.seen[e][k] = v
                    waits.append((k, v))
            if waits:
                self.streams[e].append((waits, None, None, 0))
        self.lastw.clear()
        self.readers.clear()

    def emit(self):
        nc = self.nc
        P = self

        def replay(name, e):
            for waits, fn, k, inc in P.streams[name]:
                for (wk, wv) in waits:
                    e.wait_ge(P.semobj[wk], wv)
                if fn is not None:
                    fn(e).then_inc(P.semobj[k], inc)

        with nc.Block() as block:
            @block.tensor
            def _(e):
                replay("pe", e)

            @block.scalar
            def _(e):
                replay("act", e)

            @block.vector
            def _(e):
                replay("dve", e)

            @block.gpsimd
            def _(e):
                replay("pool", e)

            @block.sync
            def _(e):
                replay("sp", e)


def _const_packs():
    i = np.arange(128)
    c = {}
    c["ident"] = np.eye(128, dtype=np.float32)
    c["ones"] = np.ones((128, 128), np.float32)
    c["uincl"] = (i[:, None] <= i[None, :]).astype(np.float32)
    c["strict"] = (i[None, :] < i[:, None]).astype(np.float32)
    c["causal"] = (i[None, :] <= i[:, None]).astype(np.float32)
    ch = i // 64
    same = (ch[:, None] == ch[None, :])
    mid = ch * 64 + 31
    c["umid"] = (same * ((i[:, None] <= i[None, :]).astype(np.float32)
                         - (i[:, None] <= mid[None, :]).astype(np.float32))).astype(np.float32)
    c["ucs"] = (same & (i[:, None] <= i[None, :])).astype(np.float32)
    c["uend"] = (same & (i[:, None] > i[None, :])).astype(np.float32)
    c["maskt"] = (same & (i[:, None] <= i[None, :])).astype(np.float32)
    valid = (i >= 112).astype(np.float32)
    c["validcol"] = np.repeat(valid[:, None], 4, axis=1)
    c["biasvalid"] = np.repeat(((1.0 - valid) * NEG)[:, None], 4, axis=1).astype(np.float32)
    c["ch0col"] = np.repeat((i < 64).astype(np.float32)[:, None], 4, axis=1)
    c["ch1col"] = np.repeat((i >= 64).astype(np.float32)[:, None], 4, axis=1)
    t512 = np.arange(512)
    c["chm"] = np.broadcast_to((t512 % 64 != 0).astype(np.float32)[None, :], (128, 512)).copy()
    names32 = ["ident", "ones", "uincl", "strict", "causal", "maskt",
               "validcol", "biasvalid", "ch0col", "ch1col", "chm"]
    off32 = {}
    cols = 0
    for n in names32:
        off32[n] = (cols, c[n].shape[1])
        cols += c[n].shape[1]
    p32 = np.concatenate([c[n] for n in names32], axis=1).astype(np.float32)
    b = {}
    b["ident"] = c["ident"]
    b["ones"] = c["ones"]
    b["ones0"] = c["ones"] * valid[:, None]
    negl = -(i[:, None] >= i[None, :]).astype(np.float32)
    b["negl"] = negl
    b["negl0"] = negl * valid[:, None]
    t = np.arange(512)
    for k in range(4):
        m01 = ((128 * k + i[:, None]) < t[None, :]).astype(np.float32)
        b[f"m01_{k}"] = m01
        b[f"mneg_{k}"] = (1.0 - m01) * NEG
    namesb = ["ident", "ones", "ones0", "negl", "negl0"] + [f"m01_{k}" for k in range(4)] + \
             [f"mneg_{k}" for k in range(4)]
    offb = {}
    cols = 0
    for n in namesb:
        offb[n] = (cols, b[n].shape[1])
        cols += b[n].shape[1]
    pb = np.concatenate([b[n] for n in namesb], axis=1).astype(ml_dtypes.bfloat16)
    return p32, off32, pb, offb


def _param_pack(depth, norm_w, sb_q_norm, sb_k_norm, gdn_conv_w, gdn_a_log, gdn_dt_bias,
                gdn_out_norm, hgrn_lb_logits, hgrn_out_norm):
    segs = {}
    segs["normw"] = norm_w.reshape(depth, 8, 128).transpose(2, 0, 1).reshape(128, depth * 8)
    segs["sbqn"] = sb_q_norm.T
    segs["sbkn"] = sb_k_norm.T
    segs["gdon"] = gdn_out_norm.T
    segs["hgon"] = hgrn_out_norm.T
    segs["convw"] = gdn_conv_w.reshape(depth, 4, 12, 128).transpose(3, 0, 2, 1).reshape(128, depth * 48)
    segs["lbl"] = hgrn_lb_logits.reshape(depth, 4, 128).transpose(2, 1, 0).reshape(128, 4 * depth)
    segs["alog"] = np.broadcast_to(gdn_a_log.reshape(1, depth * 4), (128, depth * 4))
    segs["dtb"] = np.broadcast_to(gdn_dt_bias.reshape(1, depth * 4), (128, depth * 4))
    off = {}
    cols = 0
    parts = []
    for n, a in segs.items():
        a = np.ascontiguousarray(a, dtype=np.float32)
        off[n] = (cols, a.shape[1])
        cols += a.shape[1]
        parts.append(a)
    return np.concatenate(parts, axis=1), off


class _Stop(Exception):
    pass


def build(nblk, depth, debug=False, stop=None):
    T = nblk * 128
    TR = T - 128
    groups = [(0, 1)]
    b0 = 1
    while b0 < nblk:
        nb = min(4, nblk - b0)
        groups.append((b0, nb))
        b0 += nb
    p32, off32, pb, offb = _const_packs()
    npar = depth * 8 + 4 * depth + depth * 48 + 4 * depth + 8 * depth
    nc = bass.Bass("TRN2", target_bir_lowering=False)
    es = ExitStack()
    dr = {}
    dr["x"] = nc.dram_tensor("x", [TR, D], F32, kind="ExternalInput")
    dr["meta"] = nc.dram_tensor("meta", [16, D], F32, kind="ExternalInput")
    dr["w_in"] = nc.dram_tensor("w_in", [depth, D, NIN], F32, kind="ExternalInput")
    dr["w_branch"] = nc.dram_tensor("w_branch", [depth, 3, 512, D], F32, kind="ExternalInput")
    dr["w_out"] = nc.dram_tensor("w_out", [depth, D, D], F32, kind="ExternalInput")
    dr["c32"] = nc.dram_tensor("c32", list(p32.shape), F32, kind="ExternalInput")
    dr["cbf"] = nc.dram_tensor("cbf", list(pb.shape), BF16, kind="ExternalInput")
    dr["par"] = nc.dram_tensor("par", [128, npar], F32, kind="ExternalInput")
    dr["nwrow"] = nc.dram_tensor("nwrow", [depth, 128, D], F32, kind="ExternalInput")
    dr["y"] = nc.dram_tensor("y", [TR, D], F32, kind="ExternalOutput")
    hbuf = nc.dram_tensor("hbuf", [T, D], F32, kind="Internal")
    kts = nc.dram_tensor("kts", [4, 128, T], BF16, kind="Internal")
    vs = nc.dram_tensor("vs", [4, 128, nblk, 128], BF16, kind="Internal")
    wsc = nc.dram_tensor("wsc", [depth, NTILES, 128, 4096], BF16, kind="Internal")
    dbg = {}

    P = Prog(nc, es)
    sb_count = [0]

    def SB(shape, dt, name=None):
        sb_count[0] += 1
        return nc.alloc_sbuf_tensor(name or f"t{sb_count[0]}", shape, dt)

    C32 = SB(list(p32.shape), F32, "C32")
    CBF = SB(list(pb.shape), BF16, "CBF")
    PAR = SB([128, npar], F32, "PAR")
    P.dma("sp", lambda e: e.dma_start(out=C32[:, :], in_=dr["c32"].ap()), writes=["C32"])
    P.dma("sp", lambda e: e.dma_start(out=CBF[:, :], in_=dr["cbf"].ap()), writes=["CBF"])
    P.dma("sp", lambda e: e.dma_start(out=PAR[:, :], in_=dr["par"].ap()), writes=["PAR"])

    def c32(n, lo=0, hi=None):
        o, w = off32[n]
        hi = w if hi is None else hi
        return C32[:, o + lo:o + hi]

    def cbf(n, lo=0, hi=None):
        o, w = offb[n]
        hi = w if hi is None else hi
        return CBF[:, o + lo:o + hi]

    poff = {}
    cols = 0
    for n, w in (("normw", depth * 8), ("sbqn", depth), ("sbkn", depth), ("gdon", depth), ("hgon", depth),
                 ("convw", depth * 48), ("lbl", 4 * depth), ("alog", 4 * depth), ("dtb", 4 * depth)):
        poff[n] = cols
        cols += w
    assert cols == npar

    def par(n, idx, w=1):
        return PAR[:, poff[n] + idx:poff[n] + idx + w]

    PS = [nc.alloc_psum_tensor(f"ps{i}", [128, 512], F32) for i in range(6)]
    ps_rr = [0]

    def ps():
        i = ps_rr[0]
        ps_rr[0] = (i + 1) % 4
        return PS[i], f"ps{i}"

    with nc.sbuf_tensor("stg0", [128, 4096], F32) as stg0, nc.sbuf_tensor("stg1", [128, 4096], F32) as stg1, \
            nc.sbuf_tensor("cv0", [128, 4096], BF16) as cv0, nc.sbuf_tensor("cv1", [128, 4096], BF16) as cv1:
        stg = [stg0, stg1]
        cv = [cv0, cv1]
        n = 0
        for l in range(depth):
            for ti in range(NTILES):
                s = stg[n % 2]
                cvt = cv[n % 2]
                sk, ck = f"stg{n % 2}", f"cv{n % 2}"
                if ti < 19:
                    c0, w = WT_SPECS[ti]
                    src = dr["w_in"].ap()[l].rearrange("(kc p) n -> p kc n", p=128)[:, :, c0:c0 + w]
                    dst = s[:, :].rearrange("p (kc n) -> p kc n", kc=8)[:, :, 0:w]
                    wid = 4096 if w == 512 else None
                elif ti < 22:
                    src = dr["w_branch"].ap()[l, ti - 19].rearrange("(h p) c -> p h c", p=128)
                    dst = s[:, :].rearrange("p (h c) -> p h c", h=4)
                    wid = 4096
                else:
                    e0 = (ti - 22) * 512
                    src = dr["w_out"].ap()[l].rearrange("(c p) e -> p c e", p=128)[:, :, e0:e0 + 512]
                    dst = s[:, :].rearrange("p (c e) -> p c e", c=8)
                    wid = 4096
                P.dma("sp", (lambda e, dst=dst, src=src: e.dma_start(out=dst, in_=src)), writes=[sk])
                if ti == 8:
                    sv = s[:, :].rearrange("p (kc n) -> p kc n", kc=8)[:, :, 0:8]
                    dv = cvt[:, :].rearrange("p (kc n) -> p kc n", kc=8)[:, :, 0:8]
                    P.op("dve", (lambda e, dv=dv, sv=sv: e.tensor_copy(out=dv, in_=sv)), reads=[sk], writes=[ck])
                elif n % 2 == 0:
                    P.op("dve", (lambda e, cvt=cvt, s=s: e.tensor_copy(out=cvt[:, :], in_=s[:, :])),
                         reads=[sk], writes=[ck])
                else:
                    P.op("act", (lambda e, cvt=cvt, s=s: e.activation(out=cvt[:, :], in_=s[:, :], func=AF.Copy)),
                         reads=[sk], writes=[ck])
                P.dma("sp", (lambda e, cvt=cvt, l=l, ti=ti: e.dma_start(out=wsc.ap()[l, ti], in_=cvt[:, :])),
                      reads=[ck], writes=[("wsc", l, ti)])
                n += 1
        P.barrier()

    WR = [SB([128, 4096], BF16, f"wr{i}") for i in range(4)]
    wr_rr = [0]

    ngroups = len(groups)
    worder = []
    for l_ in range(depth):
        for g_ in range(ngroups):
            worder += [(l_, t_) for t_ in range(13)]
            for b_ in range(3):
                worder += [(l_, T_WB + b_), (l_, 13 + 2 * b_), (l_, 14 + 2 * b_)]
            worder += [(l_, T_WO), (l_, T_WO + 1)]
    wstate = {"next_use": 0, "next_load": 0, "slots": {}}

    def _issue_load():
        i = wstate["next_load"]
        if i >= len(worder):
            return
        l_, ti_ = worder[i]
        slot = i % len(WR)
        t = WR[slot]
        P.dma("sp", (lambda e: e.dma_start(out=t[:, :], in_=wsc.ap()[l_, ti_])), reads=[("wsc", l_, ti_)],
              writes=[f"wr{slot}"])
        wstate["next_load"] = i + 1

    def load_w(l, ti):
        i = wstate["next_use"]
        assert worder[i] == (l, ti), (worder[i], l, ti)
        while wstate["next_load"] <= min(i + 1, len(worder) - 1):
            _issue_load()
        wstate["next_use"] = i + 1
        slot = i % len(WR)
        return WR[slot], f"wr{slot}"

    HB = [SB([128, D], F32, f"hb{i}") for i in range(2)]
    XS = SB([128, D], BF16, "xs")
    NWROW = SB([128, D], F32, "nwrow_sb")
    JUNK = SB([128, D], BF16, "junk")
    SM = SB([128, 64], F32, "sm")
    XNT = SB([128, 8, 512], BF16, "xnt")
    A32 = [SB([128, 512], F32, f"a32_{i}") for i in range(8)]
    A16 = [SB([128, 512], BF16, f"a16_{i}") for i in range(8)]
    B32 = [SB([128, 128], F32, f"b32_{i}") for i in range(20)]
    C32R = [SB([128, 128], F32, f"c32r_{i}") for i in range(14)]
    B16 = [SB([128, 128], BF16, f"b16_{i}") for i in range(16)]
    C16 = [SB([128, 128], BF16, f"c16_{i}") for i in range(10)]
    rr = {"a32": 0, "a16": 0, "b32": 0, "b16": 0, "c16": 0, "c32r": 0}

    def ring(kind):
        lst = {"a32": A32, "a16": A16, "b32": B32, "b16": B16, "c16": C16, "c32r": C32R}[kind]
        i = rr[kind]
        rr[kind] = (i + 1) % len(lst)
        return lst[i], f"{kind}_{i}"

    GA = [SB([128, 512], BF16, f"ga{h}") for h in range(4)]
    GBt = [SB([128, 512], BF16, f"gb{h}") for h in range(4)]
    GC = [SB([128, 512], BF16, f"gc{h}") for h in range(4)]
    TOK = SB([128, 4, 512], BF16, "tok")
    SZ4 = [SB([128, 512], BF16, f"sz{h}") for h in range(4)]
    OB = [SB([128, 512], BF16, f"ob{i}") for i in range(12)]
    KSEG = [SB([128, 1024], BF16, f"kseg{i}") for i in range(2)]
    VSEG = [SB([128, 8, 128], BF16, f"vseg{i}") for i in range(2)]
    RSB2 = [SB([128, 512], F32, f"rsb{i}") for i in range(2)]
    CONVB = [SB([128, 515], F32, f"convb{i}") for i in range(2)]
    TAIL = SB([128, 12, 4], F32, "tail")
    BD = SB([128, 4, 8], F32, "bd")
    SG32 = [SB([128, 128], F32, f"sg32_{h}") for h in range(4)]
    SG16 = [SB([128, 128], BF16, f"sg16_{h}") for h in range(4)]
    SH32 = [SB([128, 128], F32, f"sh32_{h}") for h in range(4)]
    SH16 = [SB([128, 128], BF16, f"sh16_{h}") for h in range(4)]
    HK = [SB([128, 512], F32, f"hk{h}") for h in range(4)]
    HG = [SB([128, 512], F32, f"hgl{h}") for h in range(4)]
    GST = SB([128, 32], F32, "gst")
    QD = [[SB([128, 128], BF16, f"qd{i}_{c}") for c in range(2)] for i in range(2)]
    ZERO16 = SB([128, 128], BF16, "zero16")
    YT = [SB([128, 512], BF16, f"yt{c}") for c in range(8)]
    LB = SB([128, 4, 2 * depth], F32, "lb")
    NEGA = SB([128, 4 * depth], F32, "nega")
    PST = [nc.alloc_psum_tensor(f"pst{i}", [128, 1024], BF16) for i in range(2)]
    pst_rr = [0]

    def pst():
        i = pst_rr[0]
        pst_rr[0] = (i + 1) % 8
        return PST[i % 2][:, (i // 2) * 128:(i // 2 + 1) * 128], f"pst{i % 2}"

    def dve(fn, reads, writes):
        P.op("dve", fn, reads, writes)

    def act(fn, reads, writes):
        P.op("act", fn, reads, writes)

    def pool(fn, reads, writes):
        P.op("pool", fn, reads, writes)

    def pe(fn, reads, writes):
        P.op("pe", fn, reads, writes)

    def mm(out, lhsT, rhs, start, stop, reads, writes):
        pe(lambda e: e.matmul(out, lhsT, rhs, start=start, stop=stop), reads, writes)

    def rstd_from(out_ap, in_ap, scale, n_free, reads, writes, tmpk):
        act(lambda e: e.activation(out=out_ap, in_=in_ap, func=AF.Ln, scale=scale, bias=EPSB[:, 0:1]),
            reads + ["epsb"], writes)
        act(lambda e: e.activation(out=out_ap, in_=out_ap, func=AF.Exp, scale=-0.5), writes, writes)

    EPSB = SB([128, 2], F32, "epsb")
    dve(lambda e: e.memset(EPSB[:, 0:1], EPS), [], ["epsb"])
    dve(lambda e: e.memset(EPSB[:, 1:2], 1.0), [], ["epsb"])
    dve(lambda e: e.memset(ZERO16[:, :], 0.0), [], ["zero16"])
    for i_ in range(2):
        for c_ in range(2):
            dve(lambda e, i_=i_, c_=c_: e.memset(QD[i_][c_][:, :], 0.0), [], [f"qd{i_}_{c_}"])
    dve(lambda e: e.memset(TAIL[:, :, :], 0.0), [], [("tail", c_) for c_ in range(12)])

    for ch in range(4):
        lg = par("lbl", ch * depth, depth)
        mx = SM[:, 0:1]
        dve(lambda e, lg=lg, mx=mx: e.tensor_reduce(out=mx, in_=lg, axis=AX.X, op=ALU.max), ["PAR"], ["sm"])
        dve(lambda e, mx=mx: e.tensor_scalar(out=SM[:, 1:2], in0=mx, scalar1=-1.0, scalar2=None, op0=ALU.mult),
            ["sm"], ["sm"])
        ex = SM[:, 8:8 + depth]
        act(lambda e, lg=lg, ex=ex: e.activation(out=ex, in_=lg, func=AF.Exp, bias=SM[:, 1:2], scale=1.0),
            ["PAR", "sm"], ["sm"])
        dve(lambda e, ex=ex: e.tensor_reduce(out=SM[:, 2:3], in_=ex, axis=AX.X, op=ALU.add), ["sm"], ["sm"])
        dve(lambda e: e.reciprocal(out=SM[:, 3:4], in_=SM[:, 2:3]), ["sm"], ["sm"])
        pr = SM[:, 24:24 + depth]
        dve(lambda e, ex=ex, pr=pr: e.tensor_scalar(out=pr, in0=ex, scalar1=SM[:, 3:4], scalar2=None, op0=ALU.mult),
            ["sm"], ["sm"])
        dve(lambda e, ch=ch: e.memset(LB[:, ch, 0:1], 0.0), [], ["lb"])
        for l in range(1, depth):
            dve(lambda e, ch=ch, l=l: e.tensor_tensor(out=LB[:, ch, l:l + 1], in0=LB[:, ch, l - 1:l],
                                                      in1=SM[:, 24 + l:25 + l], op=ALU.add), ["lb", "sm"], ["lb"])
        dve(lambda e, ch=ch: e.tensor_scalar(out=LB[:, ch, depth:2 * depth], in0=LB[:, ch, 0:depth],
                                             scalar1=-1.0, scalar2=1.0, op0=ALU.mult, op1=ALU.add), ["lb"], ["lb"])
    act(lambda e: e.activation(out=NEGA[:, :], in_=par("alog", 0, 4 * depth), func=AF.Exp), ["PAR"], ["nega"])
    dve(lambda e: e.tensor_scalar(out=NEGA[:, :], in0=NEGA[:, :], scalar1=-1.0, scalar2=None, op0=ALU.mult),
        ["nega"], ["nega"])

    try:
        if stop == "pre":
            raise _Stop()
        for l in range(depth):
            first = (l == 0)
            last = (l == depth - 1)
            for h in range(4):
                dve(lambda e, h=h: e.memset(SG32[h][:, :], 0.0), [], [f"sg32_{h}"])
                dve(lambda e, h=h: e.memset(SG16[h][:, :], 0.0), [], [f"sg16_{h}"])
                dve(lambda e, h=h: e.memset(SH32[h][:, :], 0.0), [], [f"sh32_{h}"])
                dve(lambda e, h=h: e.memset(SH16[h][:, :], 0.0), [], [f"sh16_{h}"])
            dve(lambda e: e.memset(TAIL[:, :, :], 0.0), [], [("tail", c_) for c_ in range(12)])
            P.dma("sp", lambda e: e.dma_start(out=NWROW[:, :], in_=dr["nwrow"].ap()[l]), writes=["nwrow"])

            for gi, (gb0, nb) in enumerate(groups):
                N = nb * 128
                t0 = gb0 * 128

                def h_src(blk):
                    tg = (gb0 + blk) * 128
                    if first:
                        return None if gb0 + blk == 0 else dr["x"].ap()[tg - 128:tg, :]
                    return hbuf.ap()[tg:tg + 128, :]

                def load_hblk(blk, hb, hbk):
                    tg = (gb0 + blk) * 128
                    if first and gb0 + blk == 0:
                        dve(lambda e: e.memset(hb[:, :], 0.0), [], [hbk])
                        P.dma("sp", lambda e: e.dma_start(out=hb[112:128, :], in_=dr["meta"].ap()), writes=[hbk])
                    else:
                        src = h_src(blk)
                        rk = [("h", gb0 + blk)] if not first else []
                        P.dma("sp", lambda e: e.dma_start(out=hb[:, :], in_=src), reads=rk, writes=[hbk])

                for blk in range(nb):
                    hb = HB[blk % 2]
                    hbk = f"hb{blk % 2}"
                    load_hblk(blk, hb, hbk)
                    act(lambda e, hb=hb: e.activation(out=JUNK[:, :], in_=hb[:, :], func=AF.Square,
                                                       accum_out=SM[:, 40:41]), [hbk], ["junk", "sm40"])
                    rstd_from(SM[:, 41:42], SM[:, 40:41], 1.0 / D, 1, ["sm40"], ["sm41"], None)
                    dve(lambda e, hb=hb: e.scalar_tensor_tensor(out=XS[:, :], in0=hb[:, :], scalar=SM[:, 41:42],
                                                                 in1=NWROW[:, :], op0=ALU.mult, op1=ALU.mult),
                        [hbk, "sm41", "nwrow"], ["xs"])
                    for half in range(2):
                        pbank = PST[half]
                        pk = f"pst{half}"
                        for q in range(4):
                            kc = half * 4 + q
                            pe(lambda e, pbank=pbank, q=q, kc=kc: e.transpose(
                                pbank[:, q * 128:(q + 1) * 128], XS[:, kc * 128:(kc + 1) * 128], cbf("ident")),
                               ["xs", "CBF"], [pk])
                        o = XNT[:, half * 4:half * 4 + 4, blk * 128:(blk + 1) * 128]
                        src = pbank[:, 0:512].rearrange("p (q t) -> p q t", q=4)
                        if half == 0:
                            act(lambda e, o=o, src=src: e.activation(out=o, in_=src, func=AF.Copy), [pk], [("xnt", blk)])
                        else:
                            dve(lambda e, o=o, src=src: e.tensor_copy(out=o, in_=src), [pk], [("xnt", blk)])
                xk = [("xnt", b) for b in range(nb)]

                def proj_fm(wt, wk, f):
                    bank, bk = ps()
                    for kc in range(8):
                        mm(bank[:, :N], wt[:, kc * 512 + f * 128:kc * 512 + (f + 1) * 128], XNT[:, kc, :N],
                           kc == 0, kc == 7, [wk] + xk, [bk])
                    return bank, bk

                def proj_tm(wt, wk, blk, w):
                    bank, bk = ps()
                    for kc in range(8):
                        mm(bank[:, :w], XNT[:, kc, blk * 128:(blk + 1) * 128], wt[:, kc * 512:kc * 512 + w],
                           kc == 0, kc == 7, [wk, ("xnt", blk)], [bk])
                    return bank, bk

                def silu_gate(wt, wk, f, dst, dk):
                    bank, bk = proj_fm(wt, wk, f)
                    tmp, tk = ring("a32")
                    act(lambda e: e.activation(out=tmp[:, :N], in_=bank[:, :N], func=AF.Sigmoid), [bk], [tk])
                    dve(lambda e: e.tensor_tensor(out=dst[:, :N], in0=bank[:, :N], in1=tmp[:, :N], op=ALU.mult),
                        [bk, tk], [dk])

                def headnorm_fm(bank, bk, wcol, dst, dk, extra_scale):
                    sq, sqk = ring("a32")
                    act(lambda e: e.activation(out=sq[:, :N], in_=bank[:, :N], func=AF.Square), [bk], [sqk])
                    b2, b2k = ps()
                    mm(b2[:, :N], c32("ones"), sq[:, :N], True, True, [sqk, "C32"], [b2k])
                    rs, rsk = ring("a32")
                    rstd_from(rs[:, :N], b2[:, :N], 1.0 / 128.0, N, [b2k], [rsk], None)
                    if extra_scale != 1.0:
                        dve(lambda e: e.tensor_scalar(out=rs[:, :N], in0=rs[:, :N], scalar1=wcol, scalar2=extra_scale,
                                                      op0=ALU.mult, op1=ALU.mult), [rsk, "PAR"], [rsk])
                    else:
                        dve(lambda e: e.tensor_scalar(out=rs[:, :N], in0=rs[:, :N], scalar1=wcol, scalar2=None,
                                                      op0=ALU.mult), [rsk, "PAR"], [rsk])
                    dve(lambda e: e.tensor_tensor(out=dst[:, :N], in0=bank[:, :N], in1=rs[:, :N], op=ALU.mult),
                        [bk, rsk], [dk])

                if stop == "rms" or stop == f"rms@{gi}":
                    raise _Stop()
                wt, wk = load_w(l, 0)
                for h in range(4):
                    bank, bk = proj_fm(wt, wk, h)
                    headnorm_fm(bank, bk, par("sbqn", l), GA[h], f"ga{h}", 128.0 ** -0.5)
                wt, wk = load_w(l, 1)
                for h in range(4):
                    bank, bk = proj_fm(wt, wk, h)
                    headnorm_fm(bank, bk, par("sbkn", l), GBt[h], f"gb{h}", 1.0)
                    P.dma("sp", (lambda e, h=h: e.dma_start(out=kts.ap()[h, :, t0:t0 + N], in_=GBt[h][:, :N])),
                          reads=[f"gb{h}"], writes=[("kts", h, gi)])
                wt, wk = load_w(l, 2)
                for blk in range(nb):
                    bank, bk = proj_tm(wt, wk, blk, 512)
                    act(lambda e, bank=bank, blk=blk: e.activation(out=TOK[:, blk, :], in_=bank[:, :], func=AF.Copy),
                        [bk], [("tok", blk)])
                for h in range(4):
                    P.dma("sp", (lambda e, h=h: e.dma_start(out=vs.ap()[h, :, gb0:gb0 + nb, :],
                                                           in_=TOK[:, 0:nb, h * 128:(h + 1) * 128])),
                          reads=[("tok", b) for b in range(nb)], writes=[("vs", h, gi)])
                wt, wk = load_w(l, 3)
                for h in range(4):
                    silu_gate(wt, wk, h, SZ4[h], f"sz{h}")

                if stop == "sbproj" or stop == f"sbproj@{gi}":
                    raise _Stop()
                nkb = gb0 + nb
                nseg = (nkb + 7) // 8
                for hp in (0, 2):
                    heads = (hp, hp + 1)
                    firststep = True
                    for sgi in range(nseg - 1, -1, -1):
                        kb_lo = sgi * 8
                        kb_hi = min(nkb, kb_lo + 8)
                        gread = [gg for gg, (g0, gn) in enumerate(groups) if g0 < kb_hi and g0 + gn > kb_lo and gg <= gi]
                        for h in heads:
                            si = h % 2
                            ks, vsg = KSEG[si], VSEG[si]
                            P.dma("sp", (lambda e, ks=ks, kb_lo=kb_lo, kb_hi=kb_hi, h=h: e.dma_start(
                                out=ks[:, 0:(kb_hi - kb_lo) * 128], in_=kts.ap()[h, :, kb_lo * 128:kb_hi * 128])),
                                reads=[("kts", h, gg) for gg in gread], writes=[f"kseg{si}"])
                            P.dma("sp", (lambda e, vsg=vsg, kb_lo=kb_lo, kb_hi=kb_hi, h=h: e.dma_start(
                                out=vsg[:, 0:kb_hi - kb_lo, :], in_=vs.ap()[h, :, kb_lo:kb_hi, :])),
                                reads=[("vs", h, gg) for gg in gread], writes=[f"vseg{si}"])
                        for kb in range(kb_hi - 1, kb_lo - 1, -1):
                            j = kb - kb_lo
                            di = kb - gb0
                            isdiag = di >= 0
                            isb0 = (kb == 0)
                            laststep = (kb == 0)
                            cx = {}
                            for h in heads:
                                si = h % 2
                                ks = KSEG[si]
                                Z1, z1k = PS[2 * si], f"ps{2 * si}"
                                Zb, zk = PS[2 * si + 1], f"ps{2 * si + 1}"
                                mm(Z1[:, :N], ks[:, j * 128:(j + 1) * 128], GA[h][:, :N], True, True,
                                   [f"kseg{si}", f"ga{h}"], [z1k])
                                mm(Zb[:, :N], ks[:, j * 128:(j + 1) * 128], GA[h][:, :N], True, False,
                                   [f"kseg{si}", f"ga{h}"], [zk])
                                cx[h] = dict(Z1=Z1, z1k=z1k, Zb=Zb, zk=zk)
                            for h in heads:
                                c_ = cx[h]
                                E, ek = ring("a32")
                                act(lambda e, E=E, Z1=c_["Z1"]: e.activation(out=E[:, :N], in_=Z1[:, :N], func=AF.Exp),
                                    [c_["z1k"]], [ek])
                                SPb, spk = ring("a16")
                                act(lambda e, E=E, SPb=SPb: e.activation(out=SPb[:, :N], in_=E[:, :N], func=AF.Ln,
                                                                         bias=EPSB[:, 1:2], scale=1.0),
                                    [ek, "epsb"], [spk])
                                if isdiag:
                                    dve(lambda e, SPb=SPb, di=di: e.tensor_tensor(
                                        out=SPb[:, :N], in0=SPb[:, :N], in1=cbf(f"m01_{di}", 0, N), op=ALU.mult),
                                        [spk, "CBF"], [spk])
                                c_.update(SPb=SPb, spk=spk)
                            for h in heads:
                                c_ = cx[h]
                                Zb, zk, SPb, spk = c_["Zb"], c_["zk"], c_["SPb"], c_["spk"]
                                mm(Zb[:, :N], cbf("negl0") if isb0 else cbf("negl"), SPb[:, :N], False, not isdiag,
                                   [spk, "CBF"], [zk])
                                if isdiag:
                                    mm(Zb[:, :N], cbf("ident"), cbf(f"mneg_{di}", 0, N), False, True, ["CBF"], [zk])
                                if not laststep:
                                    Cb, ck = c_["Z1"], c_["z1k"]
                                    mm(Cb[:, :N], cbf("ones0") if isb0 else cbf("ones"), SPb[:, :N], True, True,
                                       [spk, "CBF"], [ck])
                                    c_.update(Cb=Cb, ck=ck)
                            for h in heads:
                                c_ = cx[h]
                                si = h % 2
                                RS, rsk_ = RSB2[si], f"rsb{si}"
                                Zb, zk = c_["Zb"], c_["zk"]
                                AT, atk = ring("a16")
                                bias_ap = c32("biasvalid", 0, 1) if isb0 else None
                                if firststep:
                                    src, srck = Zb, zk
                                else:
                                    T1, t1k = ring("a32")
                                    dve(lambda e, T1=T1, Zb=Zb, RS=RS: e.tensor_tensor(
                                        out=T1[:, :N], in0=Zb[:, :N], in1=RS[:, :N], op=ALU.subtract),
                                        [zk, rsk_], [t1k])
                                    src, srck = T1, t1k
                                if bias_ap is not None:
                                    act(lambda e, AT=AT, src=src, bias_ap=bias_ap: e.activation(
                                        out=AT[:, :N], in_=src[:, :N], func=AF.Exp, bias=bias_ap, scale=1.0),
                                        [srck, "C32"], [atk])
                                else:
                                    act(lambda e, AT=AT, src=src: e.activation(out=AT[:, :N], in_=src[:, :N],
                                                                               func=AF.Exp), [srck], [atk])
                                if not laststep:
                                    Cb, ck = c_["Cb"], c_["ck"]
                                    if firststep:
                                        dve(lambda e, Cb=Cb, RS=RS: e.tensor_copy(out=RS[:, :N], in_=Cb[:, :N]),
                                            [ck], [rsk_])
                                    else:
                                        dve(lambda e, Cb=Cb, RS=RS: e.tensor_tensor(
                                            out=RS[:, :N], in0=RS[:, :N], in1=Cb[:, :N], op=ALU.add),
                                            [ck, rsk_], [rsk_])
                                c_.update(AT=AT, atk=atk)
                            for h in heads:
                                c_ = cx[h]
                                si = h % 2
                                OT, otk = PS[4 + si], f"ps{4 + si}"
                                mm(OT[:, :N], VSEG[si][:, j, :], c_["AT"][:, :N], firststep, laststep,
                                   [f"vseg{si}", c_["atk"]], [otk])
                            firststep = False
                    for h in heads:
                        si = h % 2
                        OT, otk = PS[4 + si], f"ps{4 + si}"
                        dve(lambda e, OT=OT, h=h: e.tensor_tensor(out=OB[h][:, :N], in0=OT[:, :N], in1=SZ4[h][:, :N],
                                                                op=ALU.mult), [otk, f"sz{h}"], [f"ob{h}"])

                if stop == "sb" or stop == f"sb@{gi}":
                    raise _Stop()
                for ti in (4, 5, 6):
                    wt, wk = load_w(l, ti)
                    for f in range(4):
                        cidx = (ti - 4) * 4 + f
                        bank, bk = proj_fm(wt, wk, f)
                        cb = CONVB[cidx % 2]
                        cbk = f"convb{cidx % 2}"
                        act(lambda e, cb=cb, bank=bank: e.activation(out=cb[:, 3:3 + N], in_=bank[:, :N], func=AF.Copy),
                            [bk], [cbk])
                        dve(lambda e, cb=cb, cidx=cidx: e.tensor_copy(out=cb[:, 0:3], in_=TAIL[:, cidx, 0:3]),
                            [("tail", cidx)], [cbk])
                        dve(lambda e, cb=cb, cidx=cidx: e.tensor_copy(out=TAIL[:, cidx, 0:3], in_=cb[:, N:N + 3]),
                             [cbk], [("tail", cidx)])
                        acc, acck = ring("a32")
                        cw = poff["convw"] + l * 48 + cidx * 4
                        dve(lambda e, acc=acc, cb=cb, cw=cw: e.tensor_scalar(out=acc[:, :N], in0=cb[:, 0:N],
                                                                            scalar1=PAR[:, cw:cw + 1], scalar2=None,
                                                                            op0=ALU.mult), [cbk, "PAR"], [acck])
                        for i in (1, 2, 3):
                            dve(lambda e, acc=acc, cb=cb, cw=cw, i=i: e.scalar_tensor_tensor(
                                out=acc[:, :N], in0=cb[:, i:i + N], scalar=PAR[:, cw + i:cw + i + 1], in1=acc[:, :N],
                                op0=ALU.mult, op1=ALU.add), [cbk, "PAR", acck], [acck])
                        sg, sgk = ring("a32")
                        act(lambda e, sg=sg, acc=acc: e.activation(out=sg[:, :N], in_=acc[:, :N], func=AF.Sigmoid),
                            [acck], [sgk])
                        hh = cidx % 4
                        if cidx >= 8:
                            dve(lambda e, acc=acc, sg=sg, hh=hh: e.tensor_tensor(out=GC[hh][:, :N], in0=acc[:, :N],
                                                                               in1=sg[:, :N], op=ALU.mult),
                                [acck, sgk], [f"gc{hh}"])
                        else:
                            dve(lambda e, acc=acc, sg=sg: e.tensor_tensor(out=acc[:, :N], in0=acc[:, :N], in1=sg[:, :N],
                                                                        op=ALU.mult), [acck, sgk], [acck])
                            sq, sqk = ring("a32")
                            dve(lambda e, sq=sq, acc=acc: e.tensor_tensor(out=sq[:, :N], in0=acc[:, :N], in1=acc[:, :N],
                                                                         op=ALU.mult), [acck], [sqk])
                            b2, b2k = ps()
                            mm(b2[:, :N], c32("ones"), sq[:, :N], True, True, [sqk, "C32"], [b2k])
                            rs, rsk = ring("a32")
                            rstd_from(rs[:, :N], b2[:, :N], 1.0, N, [b2k], [rsk], None)
                            dst, dk = (GA[hh], f"ga{hh}") if cidx < 4 else (GBt[hh], f"gb{hh}")
                            sc = 128.0 ** -0.5 if cidx < 4 else 1.0
                            dve(lambda e, dst=dst, acc=acc, rs=rs, sc=sc: e.scalar_tensor_tensor(
                                out=dst[:, :N], in0=acc[:, :N], scalar=sc, in1=rs[:, :N], op0=ALU.mult, op1=ALU.mult),
                                [acck, rsk], [dk])
                wt, wk = load_w(l, 7)
                for h in range(4):
                    silu_gate(wt, wk, h, SZ4[h], f"sz{h}")
                wt, wk = load_w(l, 8)
                for blk in range(nb):
                    bank, bk = proj_tm(wt, wk, blk, 8)
                    act(lambda e, bank=bank, blk=blk: e.activation(out=BD[:, blk, 0:4], in_=bank[:, 0:4], func=AF.Sigmoid),
                        [bk], [("bd", blk)])
                    if gb0 + blk == 0:
                        dve(lambda e, blk=blk: e.tensor_tensor(out=BD[:, blk, 0:4], in0=BD[:, blk, 0:4],
                                                              in1=c32("validcol"), op=ALU.mult),
                            [("bd", blk), "C32"], [("bd", blk)])
                    dve(lambda e, bank=bank, blk=blk: e.tensor_tensor(out=BD[:, blk, 4:8], in0=bank[:, 4:8],
                                                                     in1=par("dtb", l * 4, 4), op=ALU.add),
                        [bk, "PAR"], [("bd", blk)])
                    act(lambda e, blk=blk: e.activation(out=BD[:, blk, 4:8], in_=BD[:, blk, 4:8], func=AF.Exp),
                        [("bd", blk)], [("bd", blk)])
                    act(lambda e, blk=blk: e.activation(out=BD[:, blk, 4:8], in_=BD[:, blk, 4:8], func=AF.Ln,
                                                        bias=EPSB[:, 1:2], scale=1.0), [("bd", blk), "epsb"], [("bd", blk)])
                    dve(lambda e, blk=blk: e.tensor_tensor(out=BD[:, blk, 4:8], in0=BD[:, blk, 4:8],
                                                          in1=NEGA[:, l * 4:l * 4 + 4], op=ALU.mult),
                        [("bd", blk), "nega"], [("bd", blk)])

                if stop == "gdnproj" or stop == f"gdnproj@{gi}":
                    raise _Stop()
                for blk in range(nb):
                    c0, c1 = blk * 128, (blk + 1) * 128
                    bdk = ("bd", blk)
                    gc_b, gck = ps()
                    mm(gc_b[:, 0:4], c32("uincl"), BD[:, blk, 4:8], True, True, ["C32", bdk], [gck])
                    st, stk = GST, "gst"
                    dve(lambda e, st=st, gc_b=gc_b: e.tensor_copy(out=st[:, 0:4], in_=gc_b[:, 0:4]), [gck], [stk])
                    dve(lambda e, st=st, blk=blk: e.tensor_scalar(out=st[:, 4:8], in0=BD[:, blk, 0:4], scalar1=-1.0,
                                                                 scalar2=None, op0=ALU.mult), [bdk], [stk])
                    act(lambda e, st=st: e.activation(out=st[:, 8:12], in_=st[:, 0:4], func=AF.Exp), [stk], [stk])
                    dve(lambda e, st=st, blk=blk: e.tensor_tensor(out=st[:, 8:12], in0=st[:, 8:12], in1=BD[:, blk, 0:4],
                                                                 op=ALU.mult), [stk, bdk], [stk])
                    for h in range(4):
                        ng, ngk = ring("b32")
                        dve(lambda e, ng=ng, blk=blk, h=h: e.tensor_scalar(out=ng[:, :], in0=c32("ones"),
                                                                          scalar1=BD[:, blk, 4 + h:5 + h], scalar2=-1.0,
                                                                          op0=ALU.mult, op1=ALU.mult),
                            ["C32", bdk], [ngk])
                        gb_b, gbk = ps()
                        mm(gb_b[:, 0:128], ng[:, :], c32("uincl"), True, True, [ngk, "C32"], [gbk])
                        dve(lambda e, st=st, gb_b=gb_b, h=h: e.tensor_scalar(
                            out=st[:, 12 + h:13 + h], in0=st[:, h:h + 1], scalar1=gb_b[:, 127:128], scalar2=-1.0,
                            op0=ALU.add, op1=ALU.mult), [stk, gbk], [stk])
                        act(lambda e, st=st, h=h: e.activation(out=st[:, 12 + h:13 + h], in_=st[:, 12 + h:13 + h],
                                                              func=AF.Exp), [stk], [stk])
                        act(lambda e, st=st, gb_b=gb_b, h=h: e.activation(out=st[:, 16 + h:17 + h], in_=gb_b[:, 127:128],
                                                                         func=AF.Exp, scale=-1.0), [gbk], [stk])
                        Dm, dmk = ring("b32")
                        dve(lambda e, Dm=Dm, gb_b=gb_b, st=st, h=h: e.tensor_scalar(
                            out=Dm[:, :], in0=gb_b[:, 0:128], scalar1=st[:, h:h + 1], scalar2=0.0, op0=ALU.add,
                            op1=ALU.min), [gbk, stk], [dmk])
                        act(lambda e, Dm=Dm: e.activation(out=Dm[:, :], in_=Dm[:, :], func=AF.Exp), [dmk], [dmk])
                        Es, esk = ring("b32")
                        Ec, eck = ring("b32")
                        dve(lambda e, Es=Es, Dm=Dm: e.tensor_tensor(out=Es[:, :], in0=Dm[:, :], in1=c32("strict"),
                                                                  op=ALU.mult), [dmk, "C32"], [esk])
                        dve(lambda e, Ec=Ec, Dm=Dm: e.tensor_tensor(out=Ec[:, :], in0=Dm[:, :], in1=c32("causal"),
                                                                  op=ALU.mult), [dmk, "C32"], [eck])
                        EG, egk = ring("b32")
                        act(lambda e, EG=EG, gb_b=gb_b: e.activation(out=EG[:, :], in_=gb_b[:, 0:128], func=AF.Exp,
                                                                     scale=-1.0), [gbk], [egk])
                        qd, qdk = ring("c32r")
                        dve(lambda e, qd=qd, EG=EG, h=h: e.tensor_tensor(out=qd[:, :], in0=GA[h][:, c0:c1], in1=EG[:, :],
                                                                       op=ALU.mult), [f"ga{h}", egk], [qdk])
                        kk_b, kkk = ps()
                        mm(kk_b[:, 0:128], GBt[h][:, c0:c1], GBt[h][:, c0:c1], True, True, [f"gb{h}"], [kkk])
                        PT, ptk = ring("b32")
                        dve(lambda e, PT=PT, kk_b=kk_b, st=st, Es=Es, h=h: e.scalar_tensor_tensor(
                            out=PT[:, :], in0=kk_b[:, 0:128], scalar=st[:, 4 + h:5 + h], in1=Es[:, :], op0=ALU.mult,
                            op1=ALU.mult), [kkk, stk, esk], [ptk])
                        qk_b, qkk = ps()
                        mm(qk_b[:, 0:128], GA[h][:, c0:c1], GBt[h][:, c0:c1], True, True, [f"ga{h}", f"gb{h}"], [qkk])
                        aqk, aqkk = ring("b32")
                        dve(lambda e, aqk=aqk, qk_b=qk_b, Ec=Ec: e.tensor_tensor(out=aqk[:, :], in0=qk_b[:, 0:128],
                                                                               in1=Ec[:, :], op=ALU.mult),
                            [qkk, eck], [aqkk])

                        def tr32(src, srck, kind):
                            tb_, tbk_ = ps()
                            mm(tb_[:, 0:128], src, c32("ident"), True, True, [srck, "C32"], [tbk_])
                            d_, dk_ = ring(kind)
                            act(lambda e: e.activation(out=d_[:, :], in_=tb_[:, 0:128], func=AF.Copy), [tbk_], [dk_])
                            return d_, dk_
                        Pm, pmk = tr32(PT[:, :], ptk, "b32")
                        aqkT, aqkTk = tr32(aqk[:, :], aqkk, "c32r")
                        slot, sk_ = pst()
                        pe(lambda e, slot=slot, h=h: e.transpose(slot, GBt[h][:, c0:c1], cbf("ident")),
                           [f"gb{h}", "CBF"], [sk_])
                        kbg, kbgk = ring("c32r")
                        dve(lambda e, kbg=kbg, slot=slot, st=st, h=h: e.tensor_scalar(
                            out=kbg[:, :], in0=slot, scalar1=st[:, 8 + h:9 + h], scalar2=None, op0=ALU.mult),
                            [sk_, stk], [kbgk])
                        kd, kdk = ring("c32r")
                        dve(lambda e, kd=kd, slot=slot, st=st, h=h: e.tensor_scalar(
                            out=kd[:, :], in0=slot, scalar1=st[:, 12 + h:13 + h], scalar2=None, op0=ALU.mult),
                            [sk_, stk], [kdk])
                        slot2, sk2 = pst()
                        pe(lambda e, slot2=slot2, h=h: e.transpose(slot2, GC[h][:, c0:c1], cbf("ident")),
                           [f"gc{h}", "CBF"], [sk2])
                        vb, vbk = ring("c32r")
                        dve(lambda e, vb=vb, slot2=slot2, blk=blk, h=h: e.tensor_scalar(
                            out=vb[:, :], in0=slot2, scalar1=BD[:, blk, h:h + 1], scalar2=None, op0=ALU.mult),
                            [sk2, bdk], [vbk])
                        R32, r32k = ring("c32r")
                        dve(lambda e, R32=R32, Pm=Pm: e.tensor_tensor(out=R32[:, :], in0=Pm[:, :], in1=c32("ident"),
                                                                    op=ALU.add), [pmk, "C32"], [r32k])
                        Pc, pck, PTc, ptck = Pm, pmk, PT, ptk
                        for lev in range(1, 7):
                            b1, b1k = ps()
                            mm(b1[:, 0:128], Pc[:, :], PTc[:, :], True, True, [pck, ptck], [b1k])
                            PTn, ptnk = ring("b32")
                            act(lambda e, PTn=PTn, b1=b1: e.activation(out=PTn[:, :], in_=b1[:, 0:128], func=AF.Copy),
                                [b1k], [ptnk])
                            if lev < 6:
                                b2, b2k = ps()
                                mm(b2[:, 0:128], PTc[:, :], Pc[:, :], True, True, [pck, ptck], [b2k])
                                Pn, pnk = ring("b32")
                                dve(lambda e, Pn=Pn, b2=b2: e.tensor_copy(out=Pn[:, :], in_=b2[:, 0:128]), [b2k], [pnk])
                            b3, b3k = ps()
                            mm(b3[:, 0:128], PTn[:, :], R32[:, :], True, True, [ptnk, r32k], [b3k])
                            Rn, rnk = ring("c32r")
                            dve(lambda e, Rn=Rn, R32=R32, b3=b3: e.tensor_tensor(out=Rn[:, :], in0=R32[:, :],
                                                                               in1=b3[:, 0:128], op=ALU.add),
                                [r32k, b3k], [rnk])
                            R32, r32k = Rn, rnk
                            PTc, ptck = PTn, ptnk
                            if lev < 6:
                                Pc, pck = Pn, pnk
                        wb_, wbk = ps()
                        mm(wb_[:, 0:128], kbg[:, :], R32[:, :], True, True, [kbgk, r32k], [wbk])
                        nwT, nwTk = ring("b32")
                        dve(lambda e, nwT=nwT, wb_=wb_: e.tensor_scalar(out=nwT[:, :], in0=wb_[:, 0:128], scalar1=-1.0,
                                                                      scalar2=None, op0=ALU.mult), [wbk], [nwTk])
                        vn_b, vnk = ps()
                        mm(vn_b[:, 0:128], R32[:, :], vb[:, :], True, False, [r32k, vbk], [vnk])
                        mm(vn_b[:, 0:128], nwT[:, :], SG32[h][:, :], False, True, [nwTk, f"sg32_{h}"], [vnk])
                        vn, vn16k = ring("b32")
                        act(lambda e, vn=vn, vn_b=vn_b: e.activation(out=vn[:, :], in_=vn_b[:, 0:128], func=AF.Copy),
                            [vnk], [vn16k])
                        o_b, obk = ps()
                        mm(o_b[:, 0:128], qd[:, :], SG32[h][:, :], True, False, [qdk, f"sg32_{h}"], [obk])
                        mm(o_b[:, 0:128], aqkT[:, :], vn[:, :], False, True, [aqkTk, vn16k], [obk])
                        s_b, sbk = ps()
                        mm(s_b[:, 0:128], kd[:, :], vn[:, :], True, True, [kdk, vn16k], [sbk])
                        finish_o(P, o_b, obk, par("gdon", l), OB[4 + h], f"ob{4 + h}", SZ4[h], f"sz{h}", c0, c1,
                                 ring, pst, cbf, act, dve, pe, rstd_from)
                        dve(lambda e, s_b=s_b, st=st, h=h: e.scalar_tensor_tensor(
                            out=SG32[h][:, :], in0=SG32[h][:, :], scalar=st[:, 16 + h:17 + h], in1=s_b[:, 0:128],
                            op0=ALU.mult, op1=ALU.add), [f"sg32_{h}", stk, sbk], [f"sg32_{h}"])

                if stop == "gdn" or stop == f"gdn@{gi}":
                    raise _Stop()
                wt, wk = load_w(l, 9)
                for h in range(4):
                    silu_gate(wt, wk, h, GA[h], f"ga{h}")
                wt, wk = load_w(l, 10)
                for h in range(4):
                    bank, bk = proj_fm(wt, wk, h)
                    sg, sgk = ring("a32")
                    act(lambda e, sg=sg, bank=bank: e.activation(out=sg[:, :N], in_=bank[:, :N], func=AF.Sigmoid),
                        [bk], [sgk])
                    oml = LB[:, h, depth + l:depth + l + 1]
                    lbc = LB[:, h, l:l + 1]
                    nm, nmk = ring("b32")
                    dve(lambda e, nm=nm, oml=oml: e.tensor_scalar(out=nm[:, 0:1], in0=oml, scalar1=-1.0, scalar2=None,
                                                                 op0=ALU.mult), ["lb"], [nmk])
                    dve(lambda e, sg=sg, nm=nm, oml=oml, h=h: e.tensor_scalar(out=HK[h][:, :N], in0=sg[:, :N],
                                                                             scalar1=nm[:, 0:1], scalar2=oml,
                                                                             op0=ALU.mult, op1=ALU.add),
                        [sgk, nmk, "lb"], [f"hk{h}"])
                    dve(lambda e, sg=sg, oml=oml, lbc=lbc: e.tensor_scalar(out=sg[:, :N], in0=sg[:, :N], scalar1=oml,
                                                                          scalar2=lbc, op0=ALU.mult, op1=ALU.add),
                        [sgk, "lb"], [sgk])
                    act(lambda e, sg=sg, h=h: e.activation(out=HG[h][:, :N], in_=sg[:, :N], func=AF.Ln), [sgk],
                        [f"hgl{h}"])
                wt, wk = load_w(l, 11)
                for blk in range(nb):
                    bank, bk = proj_tm(wt, wk, blk, 512)
                    if gb0 + blk == 0:
                        dve(lambda e, bank=bank, blk=blk: e.tensor_scalar(out=TOK[:, blk, :], in0=bank[:, :],
                                                                         scalar1=c32("validcol", 0, 1), scalar2=None,
                                                                         op0=ALU.mult), [bk, "C32"], [("tok", blk)])
                    else:
                        act(lambda e, bank=bank, blk=blk: e.activation(out=TOK[:, blk, :], in_=bank[:, :], func=AF.Copy),
                            [bk], [("tok", blk)])
                wt, wk = load_w(l, 12)
                for h in range(4):
                    silu_gate(wt, wk, h, SZ4[h], f"sz{h}")

                if stop == "hgproj" or stop == f"hgproj@{gi}":
                    raise _Stop()
                nch = N // 64
                for h in range(4):
                    GTt, gtk = ring("a32")
                    dve(lambda e, GTt=GTt, h=h: e.tensor_tensor_scan(out=GTt[:, :N], data0=c32("chm", 0, N),
                                                                    data1=HG[h][:, :N], initial=0.0,
                                                                    op0=ALU.mult, op1=ALU.add),
                        [f"hgl{h}", "C32"], [gtk])
                    if stop == "hgA" or stop == f"hgA@{gi}":
                        raise _Stop()
                    NM, nmk2 = ring("b32")
                    for cc in range(nch):
                        dve(lambda e, NM=NM, GTt=GTt, cc=cc: e.tensor_scalar(
                            out=NM[:, cc:cc + 1], in0=GTt[:, cc * 64 + 31:cc * 64 + 32], scalar1=-1.0, scalar2=None,
                            op0=ALU.mult), [gtk], [nmk2])
                    if stop == "hgB" or stop == f"hgB@{gi}":
                        raise _Stop()
                    E1, e1k = ring("a32")
                    E2, e2k = ring("a32")
                    E3, e3k = ring("a32")
                    E4, e4k = ring("a32")
                    act(lambda e, E3=E3, GTt=GTt: e.activation(out=E3[:, :N], in_=GTt[:, :N], func=AF.Exp), [gtk], [e3k])
                    for cc in range(nch):
                        q0, q1 = cc * 64, (cc + 1) * 64
                        act(lambda e, E1=E1, GTt=GTt, NM=NM, cc=cc, q0=q0, q1=q1: e.activation(
                            out=E1[:, q0:q1], in_=GTt[:, q0:q1], func=AF.Exp, bias=NM[:, cc:cc + 1], scale=1.0),
                            [gtk, nmk2], [e1k])
                        act(lambda e, E2=E2, GTt=GTt, q0=q0, q1=q1: e.activation(
                            out=E2[:, q0:q1], in_=GTt[:, q0:q1], func=AF.Exp, bias=GTt[:, q0 + 31:q0 + 32], scale=-1.0),
                            [gtk], [e2k])
                        act(lambda e, E4=E4, GTt=GTt, q0=q0, q1=q1: e.activation(
                            out=E4[:, q0:q1], in_=GTt[:, q0:q1], func=AF.Exp, bias=GTt[:, q1 - 1:q1], scale=-1.0),
                            [gtk], [e4k])
                    if stop == "hgC" or stop == f"hgC@{gi}":
                        raise _Stop()
                    qtg, qtgk = ring("a16")
                    ktg, ktgk = ring("a16")
                    qdg, qdgk = ring("a16")
                    kdg, kdgk = ring("a16")
                    dve(lambda e, qtg=qtg, E1=E1, h=h: e.tensor_tensor(out=qtg[:, :N], in0=GA[h][:, :N], in1=E1[:, :N],
                                                                     op=ALU.mult), [f"ga{h}", e1k], [qtgk])
                    dve(lambda e, ktg=ktg, E2=E2, h=h: e.tensor_tensor(out=ktg[:, :N], in0=HK[h][:, :N], in1=E2[:, :N],
                                                                     op=ALU.mult), [f"hk{h}", e2k], [ktgk])
                    dve(lambda e, qdg=qdg, E3=E3, h=h: e.tensor_tensor(out=qdg[:, :N], in0=GA[h][:, :N], in1=E3[:, :N],
                                                                     op=ALU.mult), [f"ga{h}", e3k], [qdgk])
                    dve(lambda e, kdg=kdg, E4=E4, h=h: e.tensor_tensor(out=kdg[:, :N], in0=HK[h][:, :N], in1=E4[:, :N],
                                                                     op=ALU.mult), [f"hk{h}", e4k], [kdgk])
                    if stop == "hgD" or stop == f"hgD@{gi}":
                        raise _Stop()
                    for blk in range(nb):
                        c0, c1 = blk * 128, (blk + 1) * 128
                        hvk = ("tok", blk)
                        slot, sk_ = pst()
                        pe(lambda e, slot=slot, kdg=kdg, c0=c0, c1=c1: e.transpose(slot, kdg[:, c0:c1], cbf("ident")),
                           [kdgk, "CBF"], [sk_])
                        if stop == f"hgJ{h}_{blk}@{gi}":
                            raise _Stop()
                        kds = []
                        for c in range(2):
                            kd, kdk = ring("b16")
                            if True:
                                dve(lambda e, kd=kd, slot=slot, c=c: e.tensor_scalar(
                                    out=kd[:, :], in0=slot, scalar1=c32(f"ch{c}col", 0, 1), scalar2=None, op0=ALU.mult),
                                    [sk_, "C32"], [kdk])
                            else:
                                act(lambda e, kd=kd, slot=slot, c=c: e.activation(
                                    out=kd[:, :], in_=slot, func=AF.Copy, scale=c32(f"ch{c}col", 0, 1)),
                                    [sk_, "C32"], [kdk])
                            kds.append((kd, kdk))
                        if stop == f"hgG{h}_{blk}@{gi}":
                            raise _Stop()
                        at_b, atbk = ps()
                        mm(at_b[:, 0:128], ktg[:, c0:c1], qtg[:, c0:c1], True, True, [ktgk, qtgk], [atbk])
                        atc, atck = ring("b32")
                        dve(lambda e, atc=atc, at_b=at_b: e.tensor_scalar(out=atc[:, :], in0=at_b[:, 0:128],
                                                                        scalar1=-1e30, scalar2=1e30, op0=ALU.max,
                                                                        op1=ALU.min), [atbk], [atck])
                        atm, atmk = ring("b16")
                        dve(lambda e, atm=atm, atc=atc: e.tensor_tensor(out=atm[:, :], in0=atc[:, :], in1=c32("maskt"),
                                                                      op=ALU.mult), [atck, "C32"], [atmk])
                        if stop == f"hgH{h}_{blk}@{gi}":
                            raise _Stop()
                        QDp = QD[h % 2]
                        qdpk = [f"qd{h % 2}_0", f"qd{h % 2}_1"]
                        for c in range(2):
                            a0, a1 = c * 64, (c + 1) * 64
                            dve(lambda e, c=c, a0=a0, a1=a1, qdg=qdg, c0=c0, QDp=QDp: e.tensor_copy(
                                out=QDp[c][:, a0:a1], in_=qdg[:, c0 + a0:c0 + a1]), [qdgk], [qdpk[c]])
                        if stop == f"hgI{h}_{blk}@{gi}":
                            raise _Stop()
                        o_b, obk = PS[4 + h % 2], f"ps{4 + h % 2}"
                        for c in range(2):
                            a0, a1 = c * 64, (c + 1) * 64
                            mm(o_b[:, 0:128], QDp[c][:, :], SH16[h][:, :], c == 0, False, [qdpk[c], f"sh16_{h}"], [obk])
                            if c == 0:
                                mm(o_b[:, 0:128], atm[:, :], TOK[:, blk, h * 128:(h + 1) * 128], False, False,
                                   [atmk, hvk], [obk])
                            s_b, sbk = ps()
                            mm(s_b[:, 0:128], kds[c][0][:, :], TOK[:, blk, h * 128:(h + 1) * 128], True, True,
                               [kds[c][1], hvk], [sbk])
                            dve(lambda e, s_b=s_b, E3=E3, col=c0 + a1 - 1, h=h: e.scalar_tensor_tensor(
                                out=SH32[h][:, :], in0=SH32[h][:, :], scalar=E3[:, col:col + 1], in1=s_b[:, 0:128],
                                op0=ALU.mult, op1=ALU.add), [f"sh32_{h}", e3k, sbk], [f"sh32_{h}"])
                            act(lambda e, h=h: e.activation(out=SH16[h][:, :], in_=SH32[h][:, :], func=AF.Copy),
                                [f"sh32_{h}"], [f"sh16_{h}"])
                        mm(o_b[:, 0:128], ZERO16[:, :], SH16[h][:, :], False, True, ["zero16", f"sh16_{h}"], [obk])
                        if stop == f"hgF{h}_{blk}@{gi}":
                            raise _Stop()
                        finish_o(P, o_b, obk, par("hgon", l), OB[8 + h], f"ob{8 + h}", SZ4[h], f"sz{h}", c0, c1,
                                 ring, pst, cbf, act, dve, pe, rstd_from)
                        if stop == f"hgE{h}_{blk}@{gi}":
                            raise _Stop()

                if stop == "hg" or stop == f"hg@{gi}":
                    raise _Stop()
                YACC = HK + HG
                yacck = [f"hk{h}" for h in range(4)] + [f"hgl{h}" for h in range(4)]
                for b in range(3):
                    wbt_, wbk_ = load_w(l, T_WB + b)
                    for half in range(2):
                        mt_, mk_ = load_w(l, 13 + b * 2 + half)
                        for f in range(4):
                            cch = half * 4 + f
                            gbank, gk_ = proj_fm(mt_, mk_, f)
                            gate, gatek = ring("a32")
                            act(lambda e, gate=gate, gbank=gbank: e.activation(out=gate[:, :N], in_=gbank[:, :N],
                                                                              func=AF.Sigmoid), [gk_], [gatek])
                            pb_, pbk = ps()
                            for h in range(4):
                                mm(pb_[:, :N], wbt_[:, h * 1024 + cch * 128:h * 1024 + (cch + 1) * 128],
                                   OB[b * 4 + h][:, :N], h == 0, h == 3, [wbk_, f"ob{b * 4 + h}"], [pbk])
                            ya, yak = YACC[cch], yacck[cch]
                            if b == 0:
                                dve(lambda e, ya=ya, pb_=pb_, gate=gate: e.tensor_tensor(
                                    out=ya[:, :N], in0=pb_[:, :N], in1=gate[:, :N], op=ALU.mult), [pbk, gatek], [yak])
                            else:
                                dve(lambda e, gate=gate, pb_=pb_: e.tensor_tensor(
                                    out=gate[:, :N], in0=pb_[:, :N], in1=gate[:, :N], op=ALU.mult), [pbk, gatek], [gatek])
                                if b == 1:
                                    dve(lambda e, ya=ya, gate=gate: e.tensor_tensor(
                                        out=ya[:, :N], in0=ya[:, :N], in1=gate[:, :N], op=ALU.add), [yak, gatek], [yak])
                                else:
                                    dve(lambda e, ya=ya, gate=gate, cch=cch: e.tensor_tensor(
                                        out=YT[cch][:, :N], in0=ya[:, :N], in1=gate[:, :N], op=ALU.add),
                                        [yak, gatek], [f"yt{cch}"])
                if stop == "merge" or stop == f"merge@{gi}":
                    raise _Stop()
                wot = [load_w(l, T_WO + e_) for e_ in range(2)]
                for blk in range(nb):
                    hb = HB[blk % 2]
                    hbk = f"hb{blk % 2}"
                    load_hblk(blk, hb, hbk)
                    for e_ in range(2):
                        ob_, obk_ = ps()
                        for cch in range(8):
                            mm(ob_[:, :], YT[cch][:, blk * 128:(blk + 1) * 128],
                               wot[e_][0][:, cch * 512:(cch + 1) * 512], cch == 0, cch == 7,
                               [f"yt{cch}", wot[e_][1]], [obk_])
                        dve(lambda e, hb=hb, ob_=ob_, e_=e_: e.tensor_tensor(
                            out=hb[:, e_ * 512:(e_ + 1) * 512], in0=hb[:, e_ * 512:(e_ + 1) * 512], in1=ob_[:, :],
                            op=ALU.add), [hbk, obk_], [hbk])
                    tg = (gb0 + blk) * 128
                    if last:
                        if gb0 + blk > 0:
                            P.dma("sp", (lambda e, hb=hb, tg=tg: e.dma_start(out=dr["y"].ap()[tg - 128:tg, :], in_=hb[:, :])),
                                  reads=[hbk], writes=[("y", gb0 + blk)])
                    else:
                        P.dma("sp", (lambda e, hb=hb, tg=tg: e.dma_start(out=hbuf.ap()[tg:tg + 128, :], in_=hb[:, :])),
                              reads=[hbk], writes=[("h", gb0 + blk)])

    except _Stop:
        pass
    P.barrier()
    P.emit()
    es.close()
    return nc, (p32, pb)


def finish_o(P, o_b, obk, wcol, dst, dk, sz, szk, c0, c1, ring, pst, cbf, act, dve, pe, rstd_from):
    junk, jk = ring("b32")
    st, stk = ring("b32")
    act(lambda e: e.activation(out=junk[:, :], in_=o_b[:, 0:128], func=AF.Square, accum_out=st[:, 0:1]),
        [obk], [jk, stk])
    rstd_from(st[:, 1:2], st[:, 0:1], 1.0 / 128.0, 1, [stk], [stk], None)
    on, onk = ring("b16")
    dve(lambda e: e.tensor_scalar(out=on[:, :], in0=o_b[:, 0:128], scalar1=st[:, 1:2], scalar2=None, op0=ALU.mult),
        [obk, stk], [onk])
    slot, sk_ = pst()
    pe(lambda e: e.transpose(slot, on[:, :], cbf("ident")), [onk, "CBF"], [sk_])
    dve(lambda e: e.scalar_tensor_tensor(out=dst[:, c0:c1], in0=slot, scalar=wcol, in1=sz[:, c0:c1], op0=ALU.mult,
                                         op1=ALU.mult), [sk_, "PAR", szk], [dk])


_CACHE = {}


def kernel(x, meta_tokens, norm_w, w_in, sb_q_norm, sb_k_norm, gdn_conv_w, gdn_a_log, gdn_dt_bias,
           gdn_out_norm, hgrn_lb_logits, hgrn_out_norm, w_branch, w_out):
    x = np.asarray(x, dtype=np.float32)
    bsz, seq, _ = x.shape
    depth = int(np.asarray(norm_w).shape[0])
    nblk = seq // 128 + 1
    key = (nblk, depth)
    if key not in _CACHE:
        _CACHE[key] = build(nblk, depth)
    nc, (p32, pb) = _CACHE[key]
    f = lambda a: np.ascontiguousarray(np.asarray(a, dtype=np.float32))
    parp, _ = _param_pack(depth, f(norm_w), f(sb_q_norm), f(sb_k_norm), f(gdn_conv_w), f(gdn_a_log),
                          f(gdn_dt_bias), f(gdn_out_norm), f(hgrn_lb_logits), f(hgrn_out_norm))
    n_cores = 8
    nwrow = np.ascontiguousarray(np.broadcast_to(f(norm_w)[:, None, :], (depth, 128, D)))
    owner = [0, 1, 4, 5] if bsz == 4 else list(range(bsz))
    real = {"meta": f(meta_tokens), "w_in": f(w_in), "w_branch": f(w_branch), "w_out": f(w_out),
            "c32": p32, "cbf": pb, "par": parp, "nwrow": nwrow}
    zero = {k: np.zeros_like(v) for k, v in real.items()}
    zero["c32"], zero["cbf"] = p32, pb
    zx = np.zeros((seq, D), np.float32)
    in_maps = []
    for c in range(n_cores):
        if c in owner:
            m = dict(real)
            m["x"] = np.ascontiguousarray(x[owner.index(c)])
        else:
            m = dict(zero)
            m["x"] = zx
        in_maps.append(m)
    res = run_bass_kernel_spmd(nc, in_maps, core_ids=list(range(n_cores)))
    out = np.stack([np.asarray(res.results[owner[b]]["y"], dtype=np.float32) for b in range(bsz)], axis=0)
    return out
```
